# Optimizing a Trainium2 kernel written in Bass

```python
import math
import jax, jax.numpy as jnp
from jax import lax
import numpy as np

D_MODEL = 2048
BATCH = 4
SEQ = 2048
DEPTH = 4
DEC_BATCH = 128
DEC_SEQ = 8
PAST_LEN = 16384
PAGE_SIZE = 128

N_EVEN = (DEPTH + 1) // 2
N_ODD = DEPTH // 2
D_SSD = D_MODEL
SSD_HEAD_DIM = 64
SSD_HEADS = D_SSD // SSD_HEAD_DIM
SSD_GROUPS = 8
SSD_STATE = 128
SSD_CONV = 4
SSD_CHUNK = 128
SSD_CONV_CH = D_SSD + 2 * SSD_GROUPS * SSD_STATE
D_SC = D_MODEL
SC_GROUPS = 16
SC_WIDTH = 3
POOL_WINDOWS = (2, 4, 8, 16)
POOL_GROUPS = 4
POOL_GROUP_DIM = D_MODEL // POOL_GROUPS
POOL_CTX = 15
MEM_LEN = 256
X_HEADS = 4
X_HEAD_DIM = D_MODEL // X_HEADS
D_FF = 4 * D_MODEL
D_IN_EVEN = D_SSD + SSD_CONV_CH + SSD_HEADS + 3 * D_SC
ALPHA = (2.0 * DEPTH) ** 0.25
BETA = (8.0 * DEPTH) ** -0.25
LN_EPS = 1e-5
RMS_EPS = 1e-5

kernel_name = "hybrid_ssd_shortconv_pool_memxattn_step"


def layer_norm(x, g, b):
    xf = x.astype(jnp.float32)
    mu = jnp.mean(xf, -1, keepdims=True)
    var = jnp.mean(jnp.square(xf - mu), -1, keepdims=True)
    return ((xf - mu) * lax.rsqrt(var + LN_EPS) * g + b).astype(x.dtype)


def causal_dwconv(u, ctx, w):
    K = w.shape[0]
    T = u.shape[1]
    full = jnp.concatenate([ctx.astype(u.dtype), u], axis=1)
    out = full[:, 0:T] * w[0]
    for k in range(1, K):
        out = out + full[:, k:k + T] * w[k]
    return out, full[:, T:]


def ssd_scan(xh, dt, a_head, bmat, cmat, h0):
    f32 = jnp.float32
    b, T, H, P = xh.shape
    G, N = bmat.shape[2], bmat.shape[3]
    R = H // G
    L = SSD_CHUNK if T % SSD_CHUNK == 0 else T
    nc = T // L
    xdt = (xh.astype(f32) * dt[..., None]).reshape(b, nc, L, G, R, P)
    a = (dt * a_head).reshape(b, nc, L, G, R)
    acs = jnp.moveaxis(jnp.cumsum(a, axis=2), 2, -1)
    Bc = bmat.astype(f32).reshape(b, nc, L, G, N)
    Cc = cmat.astype(f32).reshape(b, nc, L, G, N)
    causal = jnp.tril(jnp.ones((L, L), dtype=bool))
    decay = jnp.exp(jnp.where(causal, acs[..., :, None] - acs[..., None, :], -jnp.inf))
    cb = jnp.einsum('bclgn,bcsgn->bcgls', Cc, Bc)
    y_diag = jnp.einsum('bcgls,bcgrls,bcsgrp->bclgrp', cb, decay, xdt)
    decay_end = jnp.exp(acs[..., -1:] - acs)
    states = jnp.einsum('bclgn,bcgrl,bclgrp->bcgrpn', Bc, decay_end, xdt)
    chunk_decay = jnp.exp(acs[..., -1])

    def step(h, inp):
        s, d = inp
        return h * d[..., None, None] + s, h

    h_init = h0.astype(f32).reshape(b, G, R, P, N)
    h_last, h_prev = lax.scan(step, h_init, (jnp.moveaxis(states, 1, 0), jnp.moveaxis(chunk_decay, 1, 0)))
    h_prev = jnp.moveaxis(h_prev, 0, 1)
    y_off = jnp.einsum('bclgn,bcgrpn,bcgrl->bclgrp', Cc, h_prev, jnp.exp(acs))
    y = (y_diag + y_off).reshape(b, T, H, P)
    return y, h_last.reshape(b, H, P, N)


def even_mixer(x, st_ssd, st_ssd_conv, st_sc, w_in, conv_w, conv_b, dt_bias, a_log, d_skip, norm_g, sc_w, w_out):
    f32 = jnp.float32
    b, T, _ = x.shape
    proj = x @ w_in
    o1 = D_SSD
    o2 = o1 + SSD_CONV_CH
    o3 = o2 + SSD_HEADS
    o4 = o3 + D_SC
    o5 = o4 + D_SC
    z, xbc, dt_raw, g_b, g_c, h_sc = jnp.split(proj, [o1, o2, o3, o4, o5], axis=-1)
    xbc, new_ssd_conv = causal_dwconv(xbc, st_ssd_conv, conv_w)
    xbc = jax.nn.silu(xbc + conv_b)
    xs, bm, cm = jnp.split(xbc, [D_SSD, D_SSD + SSD_GROUPS * SSD_STATE], axis=-1)
    xs = xs.reshape(b, T, SSD_HEADS, SSD_HEAD_DIM)
    bm = bm.reshape(b, T, SSD_GROUPS, SSD_STATE)
    cm = cm.reshape(b, T, SSD_GROUPS, SSD_STATE)
    dt = jax.nn.softplus(dt_raw.astype(f32) + dt_bias.astype(f32))
    a_head = -jnp.exp(a_log.astype(f32))
    y, new_h = ssd_scan(xs, dt, a_head, bm, cm, st_ssd)
    y = y + d_skip.astype(f32)[:, None] * xs.astype(f32)
    y = y.reshape(b, T, D_SSD) * jax.nn.silu(z.astype(f32))
    yg = y.reshape(b, T, SSD_GROUPS, D_SSD // SSD_GROUPS)
    yg = yg * lax.rsqrt(jnp.mean(yg * yg, -1, keepdims=True) + RMS_EPS)
    y_ssd = (yg.reshape(b, T, D_SSD) * norm_g.astype(f32)).astype(x.dtype)
    v, new_sc = causal_dwconv(g_c * h_sc, st_sc, sc_w)
    y_sc = g_b * v
    out = jnp.concatenate([y_ssd, y_sc], axis=-1) @ w_out
    return out, new_h.astype(st_ssd.dtype), new_ssd_conv, new_sc


def pool_mixer(x, st_pool, start_pos, w_pool, pool_scale):
    f32 = jnp.float32
    b, T, _ = x.shape
    full = jnp.concatenate([st_pool.astype(x.dtype), x], axis=1)
    cs = jnp.cumsum(full.astype(f32), axis=1)
    cs = jnp.concatenate([jnp.zeros((b, 1, D_MODEL), f32), cs], axis=1)
    pos = start_pos + jnp.arange(T)
    xf = x.astype(f32)
    outs = []
    for gi, w in enumerate(POOL_WINDOWS):
        lo, hi = gi * POOL_GROUP_DIM, (gi + 1) * POOL_GROUP_DIM
        s = cs[:, POOL_CTX + 1:POOL_CTX + 1 + T, lo:hi] - cs[:, POOL_CTX + 1 - w:POOL_CTX + 1 - w + T, lo:hi]
        cnt = jnp.minimum(pos + 1, w).astype(f32)[None, :, None]
        outs.append(s / cnt - xf[..., lo:hi])
    p = jnp.stack(outs, axis=2).astype(x.dtype)
    y = jnp.einsum('btgi,gio->btgo', p, w_pool).reshape(b, T, D_MODEL) * pool_scale
    return y, full[:, T:]


def cross_attn(x, k, v, wq, wo):
    b, T, _ = x.shape
    q = (x @ wq).reshape(b, T, X_HEADS, X_HEAD_DIM)
    s = jnp.einsum('bthd,bmhd->bhtm', q, k).astype(jnp.float32) * (X_HEAD_DIM ** -0.5)
    p = jax.nn.softmax(s, axis=-1).astype(x.dtype)
    o = jnp.einsum('bhtm,bmhd->bthd', p, v).reshape(b, T, D_MODEL)
    return o @ wo


def mlp(x, w_up, w_down):
    return jnp.square(jax.nn.relu(x @ w_up)) @ w_down


def trunk(x, start_pos, ssd_h, ssd_cb, sc_cb, pool_cb, mem_k, mem_v,
          w_in_even, ssd_conv_w, ssd_conv_b, ssd_dt_bias, ssd_a_log, ssd_d, ssd_norm_g, sc_conv_w,
          w_out_even, w_pool, pool_scale, wq_x, wo_x, w_up, w_down, ln_g, ln_b):
    hs, cbs, sbs, pbs = [], [], [], []
    for layer in range(DEPTH):
        if layer % 2 == 0:
            e = layer // 2
            m, h, cb, sb = even_mixer(x, ssd_h[e], ssd_cb[e], sc_cb[e], w_in_even[e], ssd_conv_w[e],
                                      ssd_conv_b[e], ssd_dt_bias[e], ssd_a_log[e], ssd_d[e],
                                      ssd_norm_g[e], sc_conv_w[e], w_out_even[e])
            hs.append(h)
            cbs.append(cb)
            sbs.append(sb)
        else:
            o = layer // 2
            m, pb = pool_mixer(x, pool_cb[o], start_pos, w_pool[o], pool_scale[o])
            pbs.append(pb)
        x = layer_norm(ALPHA * x + m, ln_g[layer, 0], ln_b[layer, 0])
        x = layer_norm(ALPHA * x + cross_attn(x, mem_k[layer], mem_v[layer], wq_x[layer], wo_x[layer]),
                       ln_g[layer, 1], ln_b[layer, 1])
        x = layer_norm(ALPHA * x + mlp(x, w_up[layer], w_down[layer]), ln_g[layer, 2], ln_b[layer, 2])
    return x, jnp.stack(hs), jnp.stack(cbs), jnp.stack(sbs), jnp.stack(pbs)


def setup_inputs(seed: int = 0) -> dict:
    key = jax.random.key(seed)
    ks = jax.random.split(key, 32)
    nrm = jax.random.normal
    D = D_MODEL
    dt0 = jnp.exp(jax.random.uniform(ks[12], (N_EVEN, SSD_HEADS)) * (math.log(0.1) - math.log(0.001)) + math.log(0.001))
    return {
        "x_prompt": nrm(ks[0], (BATCH, SEQ, D), jnp.float32),
        "x_sample": nrm(ks[1], (DEC_BATCH, DEC_SEQ, D), jnp.float32),
        "state_ssd": 0.2 * nrm(ks[2], (N_EVEN, DEC_BATCH, SSD_HEADS, SSD_HEAD_DIM, SSD_STATE), jnp.float32),
        "state_ssd_conv": nrm(ks[3], (N_EVEN, DEC_BATCH, SSD_CONV - 1, SSD_CONV_CH), jnp.float32),
        "state_short_conv": nrm(ks[4], (N_EVEN, DEC_BATCH, SC_WIDTH - 1, D_SC), jnp.float32),
        "state_pool": nrm(ks[5], (N_ODD, DEC_BATCH, POOL_CTX, D), jnp.float32),
        "cache_mem_k": nrm(ks[6], (DEPTH, DEC_BATCH, MEM_LEN, X_HEADS, X_HEAD_DIM), jnp.float32),
        "cache_mem_v": BETA * nrm(ks[7], (DEPTH, DEC_BATCH, MEM_LEN, X_HEADS, X_HEAD_DIM), jnp.float32),
        "mem_prompt": nrm(ks[8], (BATCH, MEM_LEN, D), jnp.float32),
        "w_in_even": nrm(ks[9], (N_EVEN, D, D_IN_EVEN), jnp.float32) * D ** -0.5,
        "ssd_conv_w": nrm(ks[10], (N_EVEN, SSD_CONV, SSD_CONV_CH), jnp.float32) * SSD_CONV ** -0.5,
        "ssd_conv_b": 0.01 * nrm(ks[11], (N_EVEN, SSD_CONV_CH), jnp.float32),
        "ssd_dt_bias": dt0 + jnp.log(-jnp.expm1(-dt0)),
        "ssd_a_log": jnp.log(jax.random.uniform(ks[13], (N_EVEN, SSD_HEADS), minval=1.0, maxval=16.0)),
        "ssd_d": 1.0 + 0.02 * nrm(ks[14], (N_EVEN, SSD_HEADS), jnp.float32),
        "ssd_norm_g": 1.0 + 0.02 * nrm(ks[15], (N_EVEN, D_SSD), jnp.float32),
        "sc_conv_w": nrm(ks[16], (N_EVEN, SC_WIDTH, D_SC), jnp.float32) * SC_WIDTH ** -0.5,
        "w_out_even": nrm(ks[17], (N_EVEN, D_SSD + D_SC, D), jnp.float32) * (D_SSD + D_SC) ** -0.5 * BETA,
        "w_pool": nrm(ks[18], (N_ODD, POOL_GROUPS, POOL_GROUP_DIM, POOL_GROUP_DIM), jnp.float32) * POOL_GROUP_DIM ** -0.5 * BETA,
        "pool_scale": 1.0 + 0.02 * nrm(ks[19], (N_ODD, D), jnp.float32),
        "wq_x": nrm(ks[20], (DEPTH, D, D), jnp.float32) * D ** -0.5,
        "wk_x": nrm(ks[21], (DEPTH, D, D), jnp.float32) * D ** -0.5,
        "wv_x": nrm(ks[22], (DEPTH, D, D), jnp.float32) * D ** -0.5 * BETA,
        "wo_x": nrm(ks[23], (DEPTH, D, D), jnp.float32) * D ** -0.5 * BETA,
        "w_up": nrm(ks[24], (DEPTH, D, D_FF), jnp.float32) * D ** -0.5 * BETA,
        "w_down": nrm(ks[25], (DEPTH, D_FF, D), jnp.float32) * D_FF ** -0.5 * BETA,
        "ln_g": 1.0 + 0.02 * nrm(ks[26], (DEPTH, 3, D), jnp.float32),
        "ln_b": 0.02 * nrm(ks[27], (DEPTH, 3, D), jnp.float32),
    }


def reference(x_prompt, x_sample, state_ssd, state_ssd_conv, state_short_conv, state_pool,
              cache_mem_k, cache_mem_v, mem_prompt,
              w_in_even, ssd_conv_w, ssd_conv_b, ssd_dt_bias, ssd_a_log, ssd_d, ssd_norm_g, sc_conv_w,
              w_out_even, w_pool, pool_scale, wq_x, wk_x, wv_x, wo_x, w_up, w_down, ln_g, ln_b):
    weights = (w_in_even, ssd_conv_w, ssd_conv_b, ssd_dt_bias, ssd_a_log, ssd_d, ssd_norm_g, sc_conv_w,
               w_out_even, w_pool, pool_scale, wq_x, wo_x, w_up, w_down, ln_g, ln_b)
    bp = x_prompt.shape[0]
    dtp = x_prompt.dtype
    h0 = jnp.zeros((N_EVEN, bp, SSD_HEADS, SSD_HEAD_DIM, SSD_STATE), dtp)
    cb0 = jnp.zeros((N_EVEN, bp, SSD_CONV - 1, SSD_CONV_CH), dtp)
    sb0 = jnp.zeros((N_EVEN, bp, SC_WIDTH - 1, D_SC), dtp)
    pb0 = jnp.zeros((N_ODD, bp, POOL_CTX, D_MODEL), dtp)
    mk_p = jnp.einsum('bmd,lde->lbme', mem_prompt, wk_x).reshape(DEPTH, bp, MEM_LEN, X_HEADS, X_HEAD_DIM)
    mv_p = jnp.einsum('bmd,lde->lbme', mem_prompt, wv_x).reshape(DEPTH, bp, MEM_LEN, X_HEADS, X_HEAD_DIM)
    y_prompt, h_p, cb_p, sb_p, pb_p = trunk(x_prompt, 0, h0, cb0, sb0, pb0, mk_p, mv_p, *weights)
    y_sample, h_s, cb_s, sb_s, pb_s = trunk(x_sample, PAST_LEN, state_ssd, state_ssd_conv, state_short_conv,
                                            state_pool, cache_mem_k, cache_mem_v, *weights)
    return (y_prompt, y_sample, h_p, h_s, cb_p, cb_s, sb_p, sb_s, pb_p, pb_s, mk_p, mv_p)
```

```python
import numpy as np
import ml_dtypes
from contextlib import ExitStack
import concourse.bass as bass
import concourse.mybir as mybir
from concourse.bass_utils import run_bass_kernel_spmd

F32 = mybir.dt.float32
BF16 = mybir.dt.bfloat16
AF = mybir.ActivationFunctionType
ALU = mybir.AluOpType
AX = mybir.AxisListType

D = 2048
DEPTH = 4
ALPHA = (2.0 * DEPTH) ** 0.25
LN_EPS = 1e-5
RMS_EPS = 1e-5
NSEQ = 16
ST = 8
XSCALE = 512 ** -0.5

OFF_LNG = 0
OFF_LNB = OFF_LNG + 4 * 3 * 16
OFF_PSC = OFF_LNB + 4 * 3 * 16
OFF_SCW = OFF_PSC + 2 * 16
OFF_CVW = OFF_SCW + 2 * 16 * 3
OFF_CVB = OFF_CVW + 2 * 32 * 4
OFF_NRG = OFF_CVB + 2 * 32
NVEC = OFF_NRG + 2 * 16
C_ID = 0; C_TRI = 128; C_TRIS = 256; C_MNEG = 384; C_MNEGS = 512; C_ONES = 640; C_SAME = 768
C_SIND = 896; C_ICNT = 912; C_SMT = 976; NCON = C_SMT + 16 * 128
R_DTB = 0; R_ALOG = 64; R_DSK = 128; NROW = 192


class Dep:
    __slots__ = ("w", "r", "ps")

    def __init__(self):
        self.w = None
        self.r = {}
        self.ps = False


class Eng:
    def __init__(self, h, sem, is_pe=False):
        self.h = h; self.sem = sem; self.cnt = 0; self.seen = {}; self.is_pe = is_pe


class Stream:
    def __init__(self, sems):
        self.sems = sems; self.cnts = [0] * len(sems); self.idx = 0


class Tl:
    def __init__(self, t, n):
        self.t = t
        self.d = [Dep() for _ in range(n)]


class Builder:
    def __init__(self, blocks, n_layers, first_seq_block=True):
        self.blocks = blocks
        self.n_layers = n_layers
        self.TP_total = sum(b[0] for b in blocks) * 128
        self.has_sample = any(b[1] for b in blocks)
        self.TC = self.TP_total + (128 if self.has_sample else 0)
        self.TMAX = max(b[0] * 128 + (128 if b[1] else 0) for b in blocks)
        self.nc = bass.Bass("TRN2", target_bir_lowering=False)
        self.es = ExitStack()
        self.ins = {}
        self.outs = {}
        self.nsem = 0

    def sem(self):
        self.nsem += 1
        return self.es.enter_context(self.nc.semaphore(f"s{self.nsem}"))

    def din(self, name, shape):
        self.ins[name] = shape
        return self.nc.dram_tensor(name, list(shape), F32, kind="ExternalInput").ap()

    def dout(self, name, shape):
        self.outs[name] = shape
        return self.nc.dram_tensor(name, list(shape), F32, kind="ExternalOutput").ap()

    def tile(self, name, shape, dt, n=1, stack=None):
        self.ntile = getattr(self, "ntile", 0) + 1
        t = (stack or self.es).enter_context(self.nc.sbuf_tensor(f"{name}_{self.ntile}", list(shape), dt))
        return Tl(t, n)

    def wait_for(self, E, pairs):
        best = {}
        for sem, v in pairs:
            k = sem.num
            if k not in best or best[k][1] < v:
                best[k] = (sem, v)
        for k, (sem, v) in best.items():
            if E.is_pe and sem is E.sem:
                continue
            if E.seen.get(k, 0) >= v:
                continue
            E.h.wait_ge(sem, v)
            E.seen[k] = v

    @staticmethod
    def pairs(R, W):
        for d in R:
            if d.w:
                yield d.w
            if d.ps:
                yield from d.r.values()
        for d in W:
            if d.w:
                yield d.w
            yield from d.r.values()

    def op(self, E, fn, R=(), W=()):
        self.wait_for(E, self.pairs(R, W))
        ins = fn()
        E.cnt += 1
        ins.then_inc(E.sem, 1)
        m = (E.sem, E.cnt)
        for d in W:
            d.w = m; d.r = {}
        for d in R:
            d.r[E.sem.num] = m

    def dma(self, Q, S, out, in_, R=(), W=()):
        j = S.idx % len(S.sems)
        S.idx += 1
        pr = list(self.pairs(R, W))
        if S.cnts[j]:
            pr.append((S.sems[j], S.cnts[j]))
        self.wait_for(Q, pr)
        Q.h.dma_start(out=out, in_=in_).then_inc(S.sems[j], 16)
        S.cnts[j] += 16
        m = (S.sems[j], S.cnts[j])
        for d in W:
            d.w = m; d.r = {}
        for d in R:
            d.r[S.sems[j].num] = m

    def mmg(self, psd, mms, R):
        PE = self.PE
        self.wait_for(PE, self.pairs(R, [psd]))
        ins = None
        for (o, l, r, st, sp) in mms:
            ins = PE.h.matmul(o, l, r, start=st, stop=sp)
        PE.cnt += 1
        ins.then_inc(PE.sem, 1)
        m = (PE.sem, PE.cnt)
        psd.w = m; psd.r = {}
        for d in R:
            d.r[PE.sem.num] = m

    def tr(self, ps, n_out_part, in_ap, R, ncols=128):
        PE = self.PE
        self.wait_for(PE, self.pairs(list(R) + [self.CON.d[0]], [ps.d[0]]))
        ins = PE.h.transpose(ps.t[:, 0:128], in_ap, self.CON.t[:, C_ID:C_ID + 128])
        PE.cnt += 1
        ins.then_inc(PE.sem, 1)
        m = (PE.sem, PE.cnt)
        ps.d[0].w = m; ps.d[0].r = {}
        for d in R:
            d.r[PE.sem.num] = m

    def ps(self):
        p = self.PS[self.psi % 8]
        self.psi += 1
        return p

    def ws(self):
        w = self.WS[self.wsi % len(self.WS)]
        self.wsi += 1
        return w

    def barrier(self):
        engs = [self.PE, self.ACT, self.DVE, self.PQ]
        targets = [(e.sem, e.cnt) for e in (self.PE, self.ACT, self.DVE) if e.cnt > 0]
        for s in (self.SL, self.SO):
            targets += [(sm, c) for sm, c in zip(s.sems, s.cnts) if c > 0]
        for e in engs:
            self.wait_for(e, targets)

    def build(self):
        nc = self.nc
        TC, TMAX = self.TC, self.TMAX
        NL = self.n_layers
        self.xT = self.din("xT", (128, 16, TC))
        self.vecs_d = self.din("vecs", (128, NVEC))
        self.rows_d = self.din("rows", (128, NROW))
        self.con_d = self.din("consts", (128, NCON))
        self.memT_d = self.din("memT", (128, 16, 256))
        self.w_in_d = self.din("w_in_s", (2, 24, 128, 16, 512))
        self.w_dt_d = self.din("w_dt_s", (2, 128, 16, 32))
        self.w_out_d = self.din("w_out_s", (2, 8, 128, 16, 512))
        self.w_pool_d = self.din("w_pool_s", (2, 4, 128, 4, 512))
        self.wq_d = self.din("wq_s", (4, 4, 128, 16, 512))
        self.wk_d = self.din("wk_s", (4, 4, 128, 16, 512))
        self.wv_d = self.din("wv_s", (4, 4, 128, 16, 512))
        self.wo_d = self.din("wo_s", (4, 4, 128, 16, 512))
        self.wup_d = self.din("w_up_s", (4, 16, 128, 16, 512))
        self.wdn_d = self.din("w_down_s", (4, 16, 128, 16, 512))
        self.kT_d = self.din("kT_s", (4, NSEQ, 4, 128, 4, 256))
        self.v_d = self.din("v_s", (4, NSEQ, 4, 128, 2, 512))
        self.ssdT_d = self.din("ssdT_s", (2, 8, 128, NSEQ, 256))
        self.ssdN_d = self.din("ssdN_s", (2, 8, 2, 128, NSEQ, 128))
        self.convT_d = self.din("convT_s", (2, 128, 32, NSEQ, 3))
        self.scT_d = self.din("scT_s", (2, 128, 16, NSEQ, 2))
        self.poolT_d = self.din("poolT_s", (2, 128, 16, NSEQ, 15))
        self.yT = self.dout("yT", (128, 16, TC))
        self.hT_p = self.dout("hT_p", (2, 128, 8, 256))
        self.hN_s = self.dout("hN_s", (2, 8, 2, 128, NSEQ, 128))
        self.cbT_p = self.dout("cbT_p", (2, 128, 32, 3))
        self.cbT_s = self.dout("cbT_s", (2, 128, 32, NSEQ, 3))
        self.sbT_p = self.dout("sbT_p", (2, 128, 16, 2))
        self.sbT_s = self.dout("sbT_s", (2, 128, 16, NSEQ, 2))
        self.pbT_p = self.dout("pbT_p", (2, 128, 16, 15))
        self.pbT_s = self.dout("pbT_s", (2, 128, 16, NSEQ, 15))
        self.mk_p = self.dout("mk_p", (4, 256, 2048))
        self.mv_p = self.dout("mv_p", (4, 256, 2048))
        self.PE = Eng(nc.tensor, self.sem(), True)
        self.ACT = Eng(nc.scalar, self.sem())
        self.DVE = Eng(nc.vector, self.sem())
        self.SP = Eng(nc.sync, self.sem())
        self.PQ = Eng(nc.gpsimd, self.sem())
        self.SL = Stream([self.sem() for _ in range(8)])
        self.SO = Stream([self.sem() for _ in range(8)])
        self.SW = Stream([self.sem() for _ in range(6)])
        self.SWB = Stream([self.sem() for _ in range(4)])
        self.SCS = Stream([self.sem() for _ in range(4)])
        self.wd = {"w_in": self.w_in_d, "w_dt": self.w_dt_d, "w_out": self.w_out_d, "w_pool": self.w_pool_d, "wq": self.wq_d, "wk": self.wk_d,
                   "wv": self.wv_d, "wo": self.wo_d, "w_up": self.wup_d, "w_dn": self.wdn_d}
        self.wc = {k: nc.dram_tensor("c_" + k, list(v.shape), BF16, kind="Internal").ap() for k, v in self.wd.items()}
        self.cdeps = {}
        T = TMAX
        self.XR = self.tile("XR", (128, 16, T), F32, 16)
        self.XB = self.tile("XB", (128, 16, T), BF16, 16)
        self.WS = [self.tile(f"WS{i}", (128, 16, 512), BF16) for i in range(2)]
        self.wsi = 0
        self.CON = self.tile("CON", (128, C_SMT), F32)
        self.VEC = self.tile("VEC", (128, NVEC), F32)
        self.ROW = self.tile("ROW", (128, NROW), F32)
        self.ONESB = self.tile("ONESB", (128, 128), BF16)
        self.SMTB = self.tile("SMTB", (128, NSEQ, 128), BF16)
        self.MEMT = self.tile("MEMT", (128, 16, 256), BF16)
        self.KTB = [self.tile(f"KTB{i}", (128, 4, 256), BF16) for i in range(2)]
        self.VB = [self.tile(f"VB{i}", (128, 2, 512), BF16) for i in range(2)]
        self.HS = [self.tile(f"HS{e}", (128, 8, 256), F32, 8) for e in range(2)]
        self.H0T = [self.tile(f"H0T{i}", (128, NSEQ, 256), BF16) for i in range(1)]
        self.CVH = [self.tile(f"CVH{e}", (128, 32, 3), F32, 32) for e in range(2)]
        self.SCH = [self.tile(f"SCH{e}", (128, 16, 2), F32, 16) for e in range(2)]
        self.PH = [self.tile(f"PH{o}", (128, 16, 15), F32, 16) for o in range(2)]
        self.AH = self.tile("AH", (128, 64), F32)
        self.PS = []
        for i in range(8):
            t = self.es.enter_context(nc.psum_tensor(f"PS{i}", [128, 512], F32))
            self.PS.append(Tl(t, 1))
            self.PS[-1].d[0].ps = True
        self.psi = 0
        SP, ACT, DVE, PQ = self.SP, self.ACT, self.DVE, self.PQ
        self.dma(self.PQ, self.SL, self.CON.t[:], self.con_d[:, 0:C_SMT], W=self.CON.d)
        self.dma(self.PQ, self.SL, self.VEC.t[:], self.vecs_d, W=self.VEC.d)
        self.dma(self.PQ, self.SL, self.ROW.t[:], self.rows_d, W=self.ROW.d)
        self.dma(PQ, self.SW, self.MEMT.t[:], self.memT_d, W=self.MEMT.d)
        self.op(DVE, lambda: DVE.h.tensor_scalar(out=self.ONESB.t[:], in0=self.CON.t[:, C_ONES:C_ONES + 128], scalar1=1.0 / D,
                                                 scalar2=None, op0=ALU.mult), R=self.CON.d, W=self.ONESB.d)
        with ExitStack() as st0:
            smt = self.tile("SMTF", (128, 16 * 128), F32, stack=st0)
            self.dma(self.PQ, self.SL, smt.t[:], self.con_d[:, C_SMT:C_SMT + 16 * 128], W=smt.d)
            self.op(DVE, lambda: DVE.h.tensor_copy(self.SMTB.t[:], smt.t[:].rearrange("p (b l) -> p b l", l=128)), R=smt.d, W=self.SMTB.d)
            self.DVE.h.wait_ge(self.DVE.sem, self.DVE.cnt)
            self.barrier()
        self.op(ACT, lambda: ACT.h.activation(out=self.AH.t[:], in_=self.ROW.t[:, R_ALOG:R_ALOG + 64], func=AF.Exp), R=self.ROW.d, W=self.AH.d)
        self.op(ACT, lambda: ACT.h.mul(self.AH.t[:], self.AH.t[:], -1.0), R=self.AH.d, W=self.AH.d)
        for tl in self.HS + self.CVH + self.SCH + self.PH:
            self.op(DVE, lambda tl=tl: DVE.h.memset(tl.t[:], 0.0), W=tl.d)
        pcol = 0
        for bi, (npt, hs) in enumerate(self.blocks):
            self.run_block(bi, npt, hs, pcol, first=(bi == 0), last=(bi == len(self.blocks) - 1))
            pcol += npt * 128
        self.barrier()
        for sm, c in zip(self.SO.sems, self.SO.cnts):
            if c:
                self.PQ.h.wait_ge(sm, c)
        return nc

    def run_block(self, bi, npt, hs, pcol, first, last):
        SP, ACT, DVE = self.SP, self.ACT, self.DVE
        Tp = npt * 128
        Ts = 128 if hs else 0
        T = Tp + Ts
        self.Tp, self.Ts, self.T, self.npt, self.hs = Tp, Ts, T, npt, hs
        self.first, self.last = first, last
        segs = []
        o = 0
        while o < Tp:
            l = min(512, Tp - o); segs.append((o, l)); o += l
        if hs:
            segs.append((Tp, 128))
        self.segs = segs
        self.msegs = []
        o = 0
        while o < T:
            l = min(512, T - o); self.msegs.append((o, l)); o += l
        for c in range(16):
            self.dma(self.PQ, self.SL, self.XR.t[:, c, 0:Tp], self.xT[:, c, pcol:pcol + Tp], W=[self.XR.d[c]])
            if hs:
                self.dma(self.PQ, self.SL, self.XR.t[:, c, Tp:T], self.xT[:, c, self.TP_total:self.TP_total + 128], W=[self.XR.d[c]])
            self.op(ACT, lambda c=c: ACT.h.copy(self.XB.t[:, c, 0:T], self.XR.t[:, c, 0:T]), R=[self.XR.d[c]], W=[self.XB.d[c]])
        import os
        stop = int(os.environ.get("KSTOP", "1000"))
        ph = [0]

        def run(f, *a):
            if ph[0] < stop:
                f(*a)
            ph[0] += 1
        for l in range(self.n_layers):
            if l % 2 == 0:
                run(self.even_sc, l // 2)
                run(self.even_ssd, l // 2)
            else:
                run(self.pool, l // 2)
            run(self.ln, l, 0)
            run(self.xattn, l)
            run(self.ln, l, 1)
            run(self.mlp, l)
            run(self.ln, l, 2)
        for c in range(16):
            self.dma(self.PQ, self.SO, self.yT[:, c, pcol:pcol + Tp], self.XR.t[:, c, 0:Tp], R=[self.XR.d[c]])
            if hs:
                self.dma(self.PQ, self.SO, self.yT[:, c, self.TP_total:self.TP_total + 128], self.XR.t[:, c, Tp:T], R=[self.XR.d[c]])
        self.barrier()

    def load_slab(self, key, kc, ncols):
        w = self.ws()
        name, idx = key[0], tuple(key[1:])
        src = self.wd[name][idx]
        cch = self.wc[name][idx]
        dep = self.cdeps.setdefault(key, Dep())
        if self.first:
            self.dma(self.PQ, self.SW, w.t[:, 0:kc, 0:ncols], src, W=w.d)
            if len(self.blocks) > 1:
                self.dma(self.PQ, self.SCS, cch, w.t[:, 0:kc, 0:ncols], R=w.d, W=[dep])
        else:
            self.dma(self.SP, self.SWB, w.t[:, 0:kc, 0:ncols], cch, R=[dep], W=w.d)
        return w

    def linear(self, src, kc, X, epi, ncols=512, w=None, segs=None):
        if w is None:
            w = self.load_slab(src, kc, ncols)
        for nci in range(ncols // 128):
            for (s0, sl) in (segs or self.msegs):
                ps = self.ps()
                mms = [(ps.t[:, 0:sl], w.t[:, k, nci * 128:(nci + 1) * 128], X.t[:, k, s0:s0 + sl], k == 0, k == kc - 1)
                       for k in range(kc)]
                self.mmg(ps.d[0], mms, R=w.d + X.d[0:kc])
                epi(nci, s0, sl, ps)
        return w

    def acc_epi(self, cbase, firstacc):
        DVE = self.DVE

        def epi(nci, s0, sl, ps):
            c = cbase + nci
            xr = self.XR.t[:, c, s0:s0 + sl]
            if firstacc:
                self.op(DVE, lambda: DVE.h.scalar_tensor_tensor(out=xr, in0=xr, scalar=ALPHA, in1=ps.t[:, 0:sl], op0=ALU.mult, op1=ALU.add),
                        R=[ps.d[0], self.XR.d[c]], W=[self.XR.d[c]])
            else:
                self.op(DVE, lambda: DVE.h.tensor_tensor(out=xr, in0=xr, in1=ps.t[:, 0:sl], op=ALU.add),
                        R=[ps.d[0], self.XR.d[c]], W=[self.XR.d[c]])
        return epi

    def ln(self, l, i):
        ACT, DVE = self.ACT, self.DVE
        T = self.T
        with ExitStack() as st:
            tA = [self.tile(f"lnA{j}", (128, T), BF16, stack=st) for j in range(2)]
            tB = [self.tile(f"lnB{j}", (128, T), BF16, stack=st) for j in range(2)]
            mean = self.tile("lnmean", (128, T), F32, stack=st)
            rstd = self.tile("lnrstd", (128, T), F32, stack=st)
            nmr = self.tile("lnnmr", (128, T), F32, stack=st)
            tmp = [self.tile(f"lntmp{j}", (128, T), F32, stack=st) for j in range(2)]
            pm = [self.ps() for _ in self.msegs]
            pq = [self.ps() for _ in self.msegs]
            for c in range(16):
                a = tA[c % 2]; b = tB[c % 2]
                self.op(DVE, lambda: DVE.h.tensor_copy(a.t[:], self.XR.t[:, c, 0:T]), R=[self.XR.d[c]], W=a.d)
                self.op(ACT, lambda: ACT.h.activation(out=b.t[:], in_=self.XR.t[:, c, 0:T], func=AF.Square), R=[self.XR.d[c]], W=b.d)
                for si, (s0, sl) in enumerate(self.msegs):
                    self.mmg(pm[si].d[0], [(pm[si].t[:, 0:sl], self.ONESB.t[:], a.t[:, s0:s0 + sl], c == 0, c == 15)], R=a.d + self.ONESB.d)
                    self.mmg(pq[si].d[0], [(pq[si].t[:, 0:sl], self.ONESB.t[:], b.t[:, s0:s0 + sl], c == 0, c == 15)], R=b.d + self.ONESB.d)
            for si, (s0, sl) in enumerate(self.msegs):
                mn = mean.t[:, s0:s0 + sl]; rs = rstd.t[:, s0:s0 + sl]; nm = nmr.t[:, s0:s0 + sl]
                self.op(ACT, lambda: ACT.h.copy(mn, pm[si].t[:, 0:sl]), R=pm[si].d, W=mean.d)
                self.op(DVE, lambda: DVE.h.tensor_tensor(out=nm, in0=mn, in1=mn, op=ALU.mult), R=mean.d, W=nmr.d)
                self.op(DVE, lambda: DVE.h.tensor_tensor(out=rs, in0=pq[si].t[:, 0:sl], in1=nm, op=ALU.subtract), R=pq[si].d + nmr.d, W=rstd.d)
                self.op(ACT, lambda: ACT.h.activation(out=rs, in_=rs, func=AF.Ln, bias=LN_EPS), R=rstd.d, W=rstd.d)
                self.op(ACT, lambda: ACT.h.activation(out=rs, in_=rs, func=AF.Exp, scale=-0.5), R=rstd.d, W=rstd.d)
                self.op(DVE, lambda: DVE.h.scalar_tensor_tensor(out=nm, in0=mn, scalar=-1.0, in1=rs, op0=ALU.mult, op1=ALU.mult),
                        R=mean.d + rstd.d, W=nmr.d)
            for c in range(16):
                t = tmp[c % 2]
                g = self.VEC.t[:, OFF_LNG + (l * 3 + i) * 16 + c:OFF_LNG + (l * 3 + i) * 16 + c + 1]
                bb = self.VEC.t[:, OFF_LNB + (l * 3 + i) * 16 + c:OFF_LNB + (l * 3 + i) * 16 + c + 1]
                self.op(DVE, lambda: DVE.h.tensor_tensor(out=t.t[:], in0=self.XR.t[:, c, 0:T], in1=rstd.t[:], op=ALU.mult),
                        R=[self.XR.d[c]] + rstd.d, W=t.d)
                self.op(DVE, lambda: DVE.h.tensor_tensor(out=t.t[:], in0=t.t[:], in1=nmr.t[:], op=ALU.add), R=t.d + nmr.d, W=t.d)
                self.op(ACT, lambda: ACT.h.activation(out=self.XR.t[:, c, 0:T], in_=t.t[:], func=AF.Identity, bias=bb, scale=g),
                        R=t.d + self.VEC.d, W=[self.XR.d[c]])
                self.op(ACT, lambda: ACT.h.activation(out=self.XB.t[:, c, 0:T], in_=t.t[:], func=AF.Identity, bias=bb, scale=g),
                        R=t.d + self.VEC.d, W=[self.XB.d[c]])
            self.barrier()

    def mlp(self, l):
        ACT, DVE = self.ACT, self.DVE
        T = self.T
        with ExitStack() as st:
            YB = self.tile("mlpY", (128, 16, T), BF16, 16, stack=st)
            rt = [self.tile(f"mlpr{j}", (128, 512), F32, stack=st) for j in range(3)]
            cnt = [0]
            for q in range(4):
                for s in range(4):
                    def epi(nci, s0, sl, ps, s=s):
                        c = s * 4 + nci
                        r = rt[cnt[0] % 3]; cnt[0] += 1
                        self.op(ACT, lambda: ACT.h.activation(out=r.t[:, 0:sl], in_=ps.t[:, 0:sl], func=AF.Relu), R=ps.d, W=r.d)
                        self.op(DVE, lambda: DVE.h.tensor_tensor(out=YB.t[:, c, s0:s0 + sl], in0=r.t[:, 0:sl], in1=r.t[:, 0:sl], op=ALU.mult),
                                R=r.d, W=[YB.d[c]])
                    self.linear(("w_up", l, q * 4 + s), 16, self.XB, epi)
                for s in range(4):
                    self.linear(("w_dn", l, q * 4 + s), 16, YB, self.acc_epi(s * 4, q == 0))
            self.barrier()

    def softmax_pt(self, ps_s, PT, col0, st_tiles, idx):
        ACT, DVE = self.ACT, self.DVE
        mx, nb, p, rs = st_tiles
        k = idx % 2
        self.op(DVE, lambda: DVE.h.reduce_max(out=mx[k].t[:], in_=ps_s.t[:, 0:256], axis=AX.X), R=ps_s.d, W=mx[k].d)
        self.op(DVE, lambda: DVE.h.tensor_scalar(out=nb[k].t[:], in0=mx[k].t[:], scalar1=-XSCALE, scalar2=None, op0=ALU.mult), R=mx[k].d, W=nb[k].d)
        self.op(ACT, lambda: ACT.h.activation(out=p[k].t[:], in_=ps_s.t[:, 0:256], func=AF.Exp, bias=nb[k].t[:], scale=XSCALE, accum_out=rs[k].t[:]),
                R=ps_s.d + nb[k].d, W=p[k].d + rs[k].d)
        self.op(DVE, lambda: DVE.h.reciprocal(out=rs[k].t[:], in_=rs[k].t[:]), R=rs[k].d, W=rs[k].d)
        self.op(DVE, lambda: DVE.h.tensor_scalar(out=p[k].t[:], in0=p[k].t[:], scalar1=rs[k].t[:], scalar2=None, op0=ALU.mult),
                R=p[k].d + rs[k].d, W=p[k].d)
        for mt in range(2):
            pst = self.ps()
            self.tr(pst, 128, p[k].t[:, mt * 128:(mt + 1) * 128], R=p[k].d)
            self.op(ACT, lambda: ACT.h.copy(PT.t[:, mt, col0:col0 + 128], pst.t[:, 0:128]), R=pst.d, W=[PT.d[mt]])

    def xattn(self, l):
        ACT, DVE, SP, PQ = self.ACT, self.DVE, self.SP, self.PQ
        T, Tp, Ts = self.T, self.Tp, self.Ts
        with ExitStack() as st:
            YB = self.tile("xaY", (128, 16, T), BF16, 16, stack=st)
            QT = self.tile("xaQ", (128, 4, T), BF16, 4, stack=st)
            KT = self.tile("xaK", (128, 4, 256), BF16, 4, stack=st)
            VH = self.tile("xaV", (128, 2, 512), BF16, 2, stack=st)
            PT = self.tile("xaPT", (128, 2, T), BF16, 2, stack=st)
            QM = [self.tile(f"xaQM{j}", (128, 4, 128), BF16, stack=st) for j in range(2)]
            stg = [self.tile(f"xaS{j}", (128, 512), F32, stack=st) for j in range(2)]
            mx = [self.tile(f"xamx{j}", (128, 1), F32, stack=st) for j in range(2)]
            nb = [self.tile(f"xanb{j}", (128, 1), F32, stack=st) for j in range(2)]
            p = [self.tile(f"xap{j}", (128, 256), F32, stack=st) for j in range(2)]
            rs = [self.tile(f"xars{j}", (128, 1), F32, stack=st) for j in range(2)]
            sm = (mx, nb, p, rs)
            if Ts:
                for q in QM:
                    self.op(DVE, lambda q=q: DVE.h.memset(q.t[:], 0.0), W=q.d)
            sidx = 0
            stgi = 0
            import os
            XS = int(os.environ.get("XSTOP", "100"))
            for hd in range(4):
                def qepi(nci, s0, sl, ps):
                    self.op(ACT, lambda: ACT.h.copy(QT.t[:, nci, s0:s0 + sl], ps.t[:, 0:sl]), R=ps.d, W=[QT.d[nci]])
                self.linear(("wq", l, hd), 16, self.XB, qepi)
                if Tp and XS >= 2:
                    wk = self.load_slab(("wk", l, hd), 16, 512)
                    for nci in range(4):
                        ps = self.ps()
                        self.mmg(ps.d[0], [(ps.t[:, 0:256], wk.t[:, k, nci * 128:(nci + 1) * 128], self.MEMT.t[:, k, :], k == 0, k == 15) for k in range(16)],
                                 R=wk.d + self.MEMT.d)
                        self.op(ACT, lambda: ACT.h.copy(KT.t[:, nci, :], ps.t[:, 0:256]), R=ps.d, W=[KT.d[nci]])
                    if self.first and XS >= 3:
                        for mt in range(2):
                            ps = self.ps()
                            self.mmg(ps.d[0], [(ps.t[:, :], self.MEMT.t[:, k, mt * 128:(mt + 1) * 128], wk.t[:, k, :], k == 0, k == 15) for k in range(16)],
                                     R=wk.d + self.MEMT.d)
                            sg = stg[stgi % 2]; stgi += 1
                            self.op(ACT, lambda: ACT.h.copy(sg.t[:], ps.t[:, :]), R=ps.d, W=sg.d)
                            self.dma(self.PQ, self.SO, self.mk_p[l, mt * 128:(mt + 1) * 128, hd * 512:(hd + 1) * 512], sg.t[:], R=sg.d)
                    wv = self.load_slab(("wv", l, hd), 16, 512)
                    for mt in range(2 if XS >= 4 else 0):
                        ps = self.ps()
                        self.mmg(ps.d[0], [(ps.t[:, :], self.MEMT.t[:, k, mt * 128:(mt + 1) * 128], wv.t[:, k, :], k == 0, k == 15) for k in range(16)],
                                 R=wv.d + self.MEMT.d)
                        self.op(ACT, lambda: ACT.h.copy(VH.t[:, mt, :], ps.t[:, :]), R=ps.d, W=[VH.d[mt]])
                        if self.first:
                            sg = stg[stgi % 2]; stgi += 1
                            self.op(ACT, lambda: ACT.h.copy(sg.t[:], ps.t[:, :]), R=ps.d, W=sg.d)
                            self.dma(self.PQ, self.SO, self.mv_p[l, mt * 128:(mt + 1) * 128, hd * 512:(hd + 1) * 512], sg.t[:], R=sg.d)
                    for ti in range(self.npt if XS >= 5 else 0):
                        ps = self.ps()
                        self.mmg(ps.d[0], [(ps.t[:, 0:256], QT.t[:, dc, ti * 128:(ti + 1) * 128], KT.t[:, dc, :], dc == 0, dc == 3) for dc in range(4)],
                                 R=QT.d + KT.d)
                        self.softmax_pt(ps, PT, ti * 128, sm, sidx); sidx += 1
                    for dc in range(4 if XS >= 6 else 0):
                        for (s0, sl) in self.segs:
                            if s0 >= Tp:
                                continue
                            ps = self.ps()
                            self.mmg(ps.d[0], [(ps.t[:, 0:sl], VH.t[:, mt, dc * 128:(dc + 1) * 128], PT.t[:, mt, s0:s0 + sl], mt == 0, mt == 1) for mt in range(2)],
                                     R=VH.d + PT.d)
                            c = hd * 4 + dc
                            self.op(ACT, lambda: ACT.h.copy(YB.t[:, c, s0:s0 + sl], ps.t[:, 0:sl]), R=ps.d, W=[YB.d[c]])
                if Ts:
                    pss = self.ps()
                    for b in range(NSEQ):
                        kb = self.KTB[b % 2]
                        self.dma(PQ, self.SW, kb.t[:], self.kT_d[l, b, hd], W=kb.d)
                        qm = QM[b % 2]
                        cs = slice(b * ST, (b + 1) * ST)
                        self.op(DVE, lambda: DVE.h.tensor_copy(qm.t[:, :, cs], QT.t[:, :, Tp + b * ST:Tp + (b + 1) * ST]), R=QT.d, W=qm.d)
                        self.mmg(pss.d[0], [(pss.t[:, 0:256], qm.t[:, dc, :], kb.t[:, dc, :], (b == 0 and dc == 0), (b == NSEQ - 1 and dc == 3)) for dc in range(4)],
                                 R=qm.d + kb.d)
                        self.op(DVE, lambda: DVE.h.memset(qm.t[:, :, cs], 0.0), W=qm.d)
                    self.softmax_pt(pss, PT, Tp, sm, sidx); sidx += 1
                    pso = [self.ps() for _ in range(4)]
                    for b in range(NSEQ):
                        vb = self.VB[b % 2]
                        self.dma(PQ, self.SW, vb.t[:], self.v_d[l, b, hd], W=vb.d)
                        for dc in range(4):
                            self.mmg(pso[dc].d[0], [(pso[dc].t[:, b * ST:(b + 1) * ST], vb.t[:, mt, dc * 128:(dc + 1) * 128],
                                                     PT.t[:, mt, Tp + b * ST:Tp + (b + 1) * ST], mt == 0, mt == 1) for mt in range(2)],
                                     R=vb.d + PT.d)
                    for dc in range(4):
                        c = hd * 4 + dc
                        self.op(ACT, lambda: ACT.h.copy(YB.t[:, c, Tp:T], pso[dc].t[:, 0:128]), R=pso[dc].d, W=[YB.d[c]])
            for s in range(4 if XS >= 7 else 0):
                self.linear(("wo", l, s), 16, YB, self.acc_epi(s * 4, True))
            self.barrier()

    def pool(self, o):
        ACT, DVE, SP = self.ACT, self.DVE, self.SP
        T, Tp, Ts = self.T, self.Tp, self.Ts
        with ExitStack() as st:
            YB = self.tile("plY", (128, 16, T), BF16, 16, stack=st)
            FP = [self.tile(f"plF{j}", (128, 15 + max(Tp, 1)), F32, stack=st) for j in range(2)]
            SA = [self.tile(f"plA{j}", (128, 15 + max(Tp, 1)), F32, stack=st) for j in range(2)]
            SB_ = [self.tile(f"plB{j}", (128, 15 + max(Tp, 1)), F32, stack=st) for j in range(2)]
            FS = [self.tile(f"plFS{j}", (128, NSEQ, 23), F32, stack=st) for j in range(2)]
            SAs = [self.tile(f"plAs{j}", (128, NSEQ, 23), F32, stack=st) for j in range(2)]
            SBs = [self.tile(f"plBs{j}", (128, NSEQ, 23), F32, stack=st) for j in range(2)]
            tm = [self.tile(f"pltm{j}", (128, 512), F32, stack=st) for j in range(2)]
            for c in range(16):
                gi = c // 4
                w = 2 << gi
                nst = gi + 1
                k = c % 2
                if Tp:
                    f = FP[k]; a = SA[k]; b = SB_[k]
                    self.op(ACT, lambda: ACT.h.copy(f.t[:, 0:15], self.PH[o].t[:, c, :]), R=[self.PH[o].d[c]], W=f.d)
                    self.op(ACT, lambda: ACT.h.copy(f.t[:, 15:15 + Tp], self.XR.t[:, c, 0:Tp]), R=[self.XR.d[c]], W=f.d)
                    src = f; lo = 0
                    for s_ in range(nst):
                        sh = 1 << s_
                        dst = a if s_ % 2 == 0 else b
                        nlo = lo + sh
                        self.op(DVE, lambda: DVE.h.tensor_tensor(out=dst.t[:, nlo:15 + Tp], in0=src.t[:, nlo:15 + Tp], in1=src.t[:, nlo - sh:15 + Tp - sh], op=ALU.add),
                                R=src.d, W=dst.d)
                        src = dst; lo = nlo
                    self.op(DVE, lambda: DVE.h.scalar_tensor_tensor(out=YB.t[:, c, 0:Tp], in0=src.t[:, 15:15 + Tp], scalar=1.0 / w, in1=f.t[:, 15:15 + Tp],
                                                                    op0=ALU.mult, op1=ALU.subtract), R=src.d + f.d, W=[YB.d[c]])
                    if self.first:
                        t_ = tm[k]
                        self.op(DVE, lambda: DVE.h.tensor_tensor(out=t_.t[:, 0:16], in0=src.t[:, 15:31], in1=self.CON.t[:, C_ICNT + gi * 16:C_ICNT + gi * 16 + 16], op=ALU.mult),
                                R=src.d + self.CON.d, W=t_.d)
                        self.op(DVE, lambda: DVE.h.tensor_tensor(out=YB.t[:, c, 0:16], in0=t_.t[:, 0:16], in1=f.t[:, 15:31], op=ALU.subtract),
                                R=t_.d + f.d, W=[YB.d[c]])
                    self.op(ACT, lambda: ACT.h.copy(self.PH[o].t[:, c, :], f.t[:, Tp:Tp + 15]), R=f.d, W=[self.PH[o].d[c]])
                    if self.last:
                        self.dma(self.PQ, self.SO, self.pbT_p[o, :, c, :], self.PH[o].t[:, c, :], R=[self.PH[o].d[c]])
                if Ts:
                    f = FS[k]; a = SAs[k]; b = SBs[k]
                    self.dma(self.PQ, self.SL, f.t[:, :, 0:15], self.poolT_d[o, :, c, :, :], W=f.d)
                    self.op(ACT, lambda: ACT.h.copy(f.t[:, :, 15:23], self.XR.t[:, c, Tp:T].rearrange("p (b t) -> p b t", t=ST)), R=[self.XR.d[c]], W=f.d)
                    src = f; lo = 0
                    for s_ in range(nst):
                        sh = 1 << s_
                        dst = a if s_ % 2 == 0 else b
                        nlo = lo + sh
                        self.op(DVE, lambda: DVE.h.tensor_tensor(out=dst.t[:, :, nlo:23], in0=src.t[:, :, nlo:23], in1=src.t[:, :, nlo - sh:23 - sh], op=ALU.add),
                                R=src.d, W=dst.d)
                        src = dst; lo = nlo
                    self.op(DVE, lambda: DVE.h.scalar_tensor_tensor(out=YB.t[:, c, Tp:T].rearrange("p (b t) -> p b t", t=ST), in0=src.t[:, :, 15:23], scalar=1.0 / w,
                                                                    in1=f.t[:, :, 15:23], op0=ALU.mult, op1=ALU.subtract), R=src.d + f.d, W=[YB.d[c]])
                    self.dma(self.PQ, self.SO, self.pbT_s[o, :, c, :, :], f.t[:, :, 8:23], R=f.d)
            for gi in range(4):
                def epi(nci, s0, sl, ps, gi=gi):
                    c = gi * 4 + nci
                    t_ = tm[nci % 2]
                    sc = self.VEC.t[:, OFF_PSC + o * 16 + c:OFF_PSC + o * 16 + c + 1]
                    self.op(ACT, lambda: ACT.h.activation(out=t_.t[:, 0:sl], in_=ps.t[:, 0:sl], func=AF.Copy, scale=sc), R=ps.d + self.VEC.d, W=t_.d)
                    xr = self.XR.t[:, c, s0:s0 + sl]
                    self.op(DVE, lambda: DVE.h.scalar_tensor_tensor(out=xr, in0=xr, scalar=ALPHA, in1=t_.t[:, 0:sl], op0=ALU.mult, op1=ALU.add),
                            R=t_.d + [self.XR.d[c]], W=[self.XR.d[c]])
                Xv = Tl(YB.t[:, gi * 4:(gi + 1) * 4, :], 0)
                Xv.d = YB.d[gi * 4:(gi + 1) * 4]
                self.linear(("w_pool", o, gi), 4, Xv, epi)
            self.barrier()

    def even_sc(self, e):
        ACT, DVE, SP = self.ACT, self.DVE, self.SP
        T, Tp, Ts = self.T, self.Tp, self.Ts
        with ExitStack() as st:
            YB = self.tile("scY", (128, 16, T), BF16, 16, stack=st)
            U = self.tile("scU", (128, 4, 2 + max(Tp, 1)), F32, 4, stack=st)
            US = self.tile("scUS", (128, 4, NSEQ, 2 + ST), F32, 4, stack=st)
            V = self.tile("scV", (128, 512), F32, stack=st)
            VS = self.tile("scVS", (128, NSEQ, ST), F32, stack=st)
            for j in range(4):
                def uview(nci, s0, sl):
                    if s0 < Tp:
                        return U.t[:, nci, 2 + s0:2 + s0 + sl]
                    return US.t[:, nci, :, 2:2 + ST]

                def pview(ps, s0, sl):
                    if s0 < Tp:
                        return ps.t[:, 0:sl]
                    return ps.t[:, 0:128].rearrange("p (b t) -> p b t", t=ST)

                def epi_h(nci, s0, sl, ps):
                    self.op(ACT, lambda: ACT.h.copy(uview(nci, s0, sl), pview(ps, s0, sl)), R=ps.d, W=[U.d[nci]])

                def epi_g(nci, s0, sl, ps):
                    uv = uview(nci, s0, sl)
                    self.op(DVE, lambda: DVE.h.tensor_tensor(out=uv, in0=uv, in1=pview(ps, s0, sl), op=ALU.mult), R=ps.d + [U.d[nci]], W=[U.d[nci]])

                def epi_b(nci, s0, sl, ps, j=j):
                    c = j * 4 + nci
                    wv = [self.VEC.t[:, OFF_SCW + (e * 16 + c) * 3 + k:OFF_SCW + (e * 16 + c) * 3 + k + 1] for k in range(3)]
                    if s0 < Tp:
                        v = V.t[:, 0:sl]
                        ins = [U.t[:, nci, s0 + k:s0 + k + sl] for k in range(3)]
                        yv = YB.t[:, c, s0:s0 + sl]
                        vd = V.d
                    else:
                        v = VS.t[:]
                        ins = [US.t[:, nci, :, k:k + ST] for k in range(3)]
                        yv = YB.t[:, c, Tp:T].rearrange("p (b t) -> p b t", t=ST)
                        vd = VS.d
                    self.op(DVE, lambda: DVE.h.tensor_scalar(out=v, in0=ins[0], scalar1=wv[0], scalar2=None, op0=ALU.mult), R=[U.d[nci]] + self.VEC.d, W=vd)
                    self.op(DVE, lambda: DVE.h.scalar_tensor_tensor(out=v, in0=ins[1], scalar=wv[1], in1=v, op0=ALU.mult, op1=ALU.add), R=[U.d[nci]] + vd, W=vd)
                    self.op(DVE, lambda: DVE.h.scalar_tensor_tensor(out=v, in0=ins[2], scalar=wv[2], in1=v, op0=ALU.mult, op1=ALU.add), R=[U.d[nci]] + vd, W=vd)
                    self.op(DVE, lambda: DVE.h.tensor_tensor(out=yv, in0=v, in1=pview(ps, s0, sl), op=ALU.mult), R=ps.d + vd, W=[YB.d[c]])
                for nci in range(4):
                    c = j * 4 + nci
                    if Tp:
                        self.op(ACT, lambda: ACT.h.copy(U.t[:, nci, 0:2], self.SCH[e].t[:, c, :]), R=[self.SCH[e].d[c]], W=[U.d[nci]])
                    if Ts:
                        self.dma(self.PQ, self.SL, US.t[:, nci, :, 0:2], self.scT_d[e, :, c, :, :], W=[U.d[nci]])
                self.linear(("w_in", e, 12 + j * 3 + 0), 16, self.XB, epi_h, segs=self.segs)
                self.linear(("w_in", e, 12 + j * 3 + 1), 16, self.XB, epi_g, segs=self.segs)
                for nci in range(4):
                    c = j * 4 + nci
                    if Tp:
                        self.op(ACT, lambda: ACT.h.copy(self.SCH[e].t[:, c, :], U.t[:, nci, Tp:Tp + 2]), R=[U.d[nci]], W=[self.SCH[e].d[c]])
                        if self.last:
                            self.dma(self.PQ, self.SO, self.sbT_p[e, :, c, :], self.SCH[e].t[:, c, :], R=[self.SCH[e].d[c]])
                    if Ts:
                        self.dma(self.PQ, self.SO, self.sbT_s[e, :, c, :, :], US.t[:, nci, :, ST:ST + 2], R=[U.d[nci]])
                self.linear(("w_in", e, 12 + j * 3 + 2), 16, self.XB, epi_b, segs=self.segs)
            for s in range(4):
                self.linear(("w_out", e, 4 + s), 16, YB, self.acc_epi(s * 4, True))
            self.barrier()

    def even_ssd(self, e):
        ACT, DVE, SP, PQ = self.ACT, self.DVE, self.SP, self.PQ
        T, Tp, Ts, npt = self.T, self.Tp, self.Ts, self.npt
        NT = npt + (1 if Ts else 0)
        CON = self.CON
        with ExitStack() as st:
            tl = lambda name, shape, dt, n=1: self.tile(name, shape, dt, n, stack=st)
            YB = tl("sdY", (128, 16, T), BF16, 16)
            DT = tl("sdDT", (128, NT, 32), F32)
            AA = tl("sdA", (128, NT, 32), F32)
            ACS = tl("sdACS", (128, NT, 32), F32)
            NACS = tl("sdNACS", (128, NT, 32), F32)
            EACS = tl("sdEACS", (128, NT, 32), F32)
            DEND = tl("sdDEND", (128, NT, 32), F32)
            CDEC = tl("sdCDEC", (128, NT, 32), F32)
            ZG = tl("sdZG", (128, NT, 256), F32)
            XP = tl("sdXP", (128, 4, 3 + max(Tp, 1)), F32, 4)
            NQ = NSEQ if Ts else 1
            XPS = tl("sdXPS", (128, 4, NQ, 3 + ST), F32, 4)
            XC = tl("sdXC", (128, 3, T), F32, 3)
            BCb = tl("sdBCb", (128, 2, T), BF16, 2)
            XTK = tl("sdXTK", (128, NT, 256), F32)
            BTK = tl("sdBTK", (128, NT, 128), BF16)
            AT = [tl(f"sdAT{j}", (128, 4, 128), F32) for j in range(2)]
            LT = [tl(f"sdLT{j}", (128, 4, 128), F32) for j in range(2)]
            MT = [tl(f"sdMT{j}", (128, 4, 128), BF16) for j in range(2)]
            XDT = [tl(f"sdXDT{j}", (128, 256), BF16) for j in range(2)]
            XDD = [tl(f"sdXDD{j}", (128, 256), BF16) for j in range(2)]
            Y1 = [tl(f"sdY1{j}", (128, 256), F32) for j in range(2)]
            Y2 = [tl(f"sdY2{j}", (128, 256), F32) for j in range(2)]
            SS = [tl(f"sdSS{j}", (128, 1), F32) for j in range(2)]
            HB = tl("sdHB", (128, 256), BF16)
            HTMP = tl("sdHTMP", (128, 256), F32)
            CM = tl("sdCM", (128, NQ, 128), BF16)
            BM = tl("sdBM", (128, NQ, 128), BF16)
            AEX = tl("sdAEX", (128, 256), F32)
            CDC = tl("sdCDC", (128, 2, NSEQ), F32)
            H0N = [tl("sdH0N", (128, NQ, 128), F32)] * 2
            HNO = [tl("sdHNO", (128, NQ, 128), F32)] * 2
            A_e = self.AH.t[:, e * 32:(e + 1) * 32]
            self.cvtmp = tl("sdCVT", (128, max(Tp, 128)), F32)
            wdt = self.load_slab(("w_dt", e), 16, 32)
            for ti in range(NT):
                ps = self.ps()
                self.mmg(ps.d[0], [(ps.t[:, 0:32], self.XB.t[:, k, ti * 128:(ti + 1) * 128], wdt.t[:, k, 0:32], k == 0, k == 15) for k in range(16)],
                         R=wdt.d + self.XB.d)
                d = DT.t[:, ti, :]
                self.op(DVE, lambda: DVE.h.tensor_tensor(out=d, in0=ps.t[:, 0:32], in1=self.ROW.t[:, R_DTB + e * 32:R_DTB + (e + 1) * 32], op=ALU.add),
                        R=ps.d + self.ROW.d, W=DT.d)
                self.op(ACT, lambda: ACT.h.activation(out=d, in_=d, func=AF.Exp), R=DT.d, W=DT.d)
                self.op(ACT, lambda: ACT.h.activation(out=d, in_=d, func=AF.Ln, bias=1.0), R=DT.d, W=DT.d)
                a = AA.t[:, ti, :]
                self.op(DVE, lambda: DVE.h.tensor_tensor(out=a, in0=d, in1=A_e, op=ALU.mult), R=DT.d + self.AH.d, W=AA.d)
                samp = (ti >= npt)
                tri = CON.t[:, (C_TRIS if samp else C_TRI):(C_TRIS if samp else C_TRI) + 128]
                one = CON.t[:, (C_SAME if samp else C_ONES):(C_SAME if samp else C_ONES) + 128]
                ps1 = self.ps()
                self.mmg(ps1.d[0], [(ps1.t[:, 0:32], tri, a, True, True)], R=AA.d + CON.d)
                ps2 = self.ps()
                self.mmg(ps2.d[0], [(ps2.t[:, 0:32], one, a, True, True)], R=AA.d + CON.d)
                self.op(ACT, lambda: ACT.h.copy(ACS.t[:, ti, :], ps1.t[:, 0:32]), R=ps1.d, W=ACS.d)
                self.op(ACT, lambda: ACT.h.mul(NACS.t[:, ti, :], ps1.t[:, 0:32], -1.0), R=ps1.d, W=NACS.d)
                self.op(ACT, lambda: ACT.h.activation(out=EACS.t[:, ti, :], in_=ps1.t[:, 0:32], func=AF.Exp), R=ps1.d, W=EACS.d)
                self.op(ACT, lambda: ACT.h.activation(out=CDEC.t[:, ti, :], in_=ps2.t[:, 0:32], func=AF.Exp), R=ps2.d, W=CDEC.d)
                self.op(DVE, lambda: DVE.h.tensor_tensor(out=DEND.t[:, ti, :], in0=ps2.t[:, 0:32], in1=ACS.t[:, ti, :], op=ALU.subtract), R=ps2.d + ACS.d, W=DEND.d)
                self.op(ACT, lambda: ACT.h.activation(out=DEND.t[:, ti, :], in_=DEND.t[:, ti, :], func=AF.Exp), R=DEND.d, W=DEND.d)
            it = 0
            for g in range(8):
                wa = self.load_slab(("w_in", e, g), 16, 512)
                wb = None
                if g % 2 == 0:
                    wbc = self.load_slab(("w_in", e, 8 + g // 2), 16, 512)
                    self.wbc = wbc
                wbc = self.wbc
                bo = (g % 2) * 256
                for ti in range(NT):
                    ps = self.ps()
                    self.mmg(ps.d[0], [(ps.t[:, 0:256], self.XB.t[:, k, ti * 128:(ti + 1) * 128], wa.t[:, k, 0:256], k == 0, k == 15) for k in range(16)],
                             R=wa.d + self.XB.d)
                    self.op(ACT, lambda: ACT.h.activation(out=ZG.t[:, ti, :], in_=ps.t[:, 0:256], func=AF.Silu), R=ps.d, W=ZG.d)
                cids = [2 * g, 2 * g + 1, 16 + g, 24 + g]
                for j in range(4):
                    c32 = cids[j]
                    if Tp:
                        self.op(ACT, lambda: ACT.h.copy(XP.t[:, j, 0:3], self.CVH[e].t[:, c32, :]), R=[self.CVH[e].d[c32]], W=[XP.d[j]])
                    if Ts:
                        self.dma(self.PQ, self.SL, XPS.t[:, j, :, 0:3], self.convT_d[e, :, c32, :, :], W=[XPS.d[j]])
                for j in range(4):
                    wsrc, col = (wa, 256 + j * 128) if j < 2 else (wbc, bo + (j - 2) * 128)
                    for (s0, sl) in self.segs:
                        ps = self.ps()
                        self.mmg(ps.d[0], [(ps.t[:, 0:sl], wsrc.t[:, k, col:col + 128], self.XB.t[:, k, s0:s0 + sl], k == 0, k == 15) for k in range(16)],
                                 R=wsrc.d + self.XB.d)
                        if s0 < Tp:
                            self.op(ACT, lambda: ACT.h.copy(XP.t[:, j, 3 + s0:3 + s0 + sl], ps.t[:, 0:sl]), R=ps.d, W=[XP.d[j]])
                        else:
                            self.op(ACT, lambda: ACT.h.copy(XPS.t[:, j, :, 3:3 + ST], ps.t[:, 0:128].rearrange("p (b t) -> p b t", t=ST)), R=ps.d, W=[XPS.d[j]])
                for j in range(4):
                    c32 = cids[j]
                    wv = [self.VEC.t[:, OFF_CVW + (e * 32 + c32) * 4 + k:OFF_CVW + (e * 32 + c32) * 4 + k + 1] for k in range(4)]
                    bv = self.VEC.t[:, OFF_CVB + e * 32 + c32:OFF_CVB + e * 32 + c32 + 1]
                    for part in range(2):
                        if (part == 0 and not Tp) or (part == 1 and not Ts):
                            continue
                        if part == 0:
                            ins = [XP.t[:, j, k:k + Tp] for k in range(4)]
                            outs_ = (XC.t[:, j, 0:Tp] if j < 3 else None)
                            bview = (BCb.t[:, j - 2, 0:Tp] if j >= 2 else None)
                            tmpv = self.cvtmp.t[:, 0:Tp]
                            rd = [XP.d[j]]
                        else:
                            ins = [XPS.t[:, j, :, k:k + ST] for k in range(4)]
                            outs_ = (XC.t[:, j, Tp:T].rearrange("p (b t) -> p b t", t=ST) if j < 3 else None)
                            bview = (BCb.t[:, j - 2, Tp:T].rearrange("p (b t) -> p b t", t=ST) if j >= 2 else None)
                            tmpv = self.cvtmp.t[:, 0:128].rearrange("p (b t) -> p b t", t=ST)
                            rd = [XPS.d[j]]
                        td = self.cvtmp.d
                        self.op(DVE, lambda: DVE.h.tensor_scalar(out=tmpv, in0=ins[0], scalar1=wv[0], scalar2=None, op0=ALU.mult), R=rd + self.VEC.d, W=td)
                        for k in range(1, 4):
                            self.op(DVE, lambda k=k: DVE.h.scalar_tensor_tensor(out=tmpv, in0=ins[k], scalar=wv[k], in1=tmpv, op0=ALU.mult, op1=ALU.add), R=rd + td, W=td)
                        if outs_ is not None:
                            self.op(ACT, lambda: ACT.h.activation(out=outs_, in_=tmpv, func=AF.Silu, bias=bv), R=td + self.VEC.d, W=[XC.d[j]])
                        if bview is not None:
                            self.op(ACT, lambda: ACT.h.activation(out=bview, in_=tmpv, func=AF.Silu, bias=bv), R=td + self.VEC.d, W=[BCb.d[j - 2]])
                    if Tp:
                        self.op(ACT, lambda: ACT.h.copy(self.CVH[e].t[:, c32, :], XP.t[:, j, Tp:Tp + 3]), R=[XP.d[j]], W=[self.CVH[e].d[c32]])
                        if self.last:
                            self.dma(self.PQ, self.SO, self.cbT_p[e, :, c32, :], self.CVH[e].t[:, c32, :], R=[self.CVH[e].d[c32]])
                    if Ts:
                        self.dma(self.PQ, self.SO, self.cbT_s[e, :, c32, :, :], XPS.t[:, j, :, ST:ST + 3], R=[XPS.d[j]])
                for ti in range(NT):
                    for j in range(3):
                        pst = self.ps()
                        self.tr(pst, 128, XC.t[:, j, ti * 128:(ti + 1) * 128], R=[XC.d[j]])
                        if j < 2:
                            self.op(ACT, lambda: ACT.h.copy(XTK.t[:, ti, j * 128:(j + 1) * 128], pst.t[:, 0:128]), R=pst.d, W=XTK.d)
                        else:
                            self.op(ACT, lambda: ACT.h.copy(BTK.t[:, ti, :], pst.t[:, 0:128]), R=pst.d, W=BTK.d)
                hsd = [self.HS[e].d[g]]
                hst = self.HS[e].t[:, g, :]
                for ti in range(NT):
                    samp = ti >= npt
                    k2 = it % 2; it += 1
                    cols = slice(ti * 128, (ti + 1) * 128)
                    hsl = slice(4 * g, 4 * g + 4)
                    tri = CON.t[:, (C_TRIS if samp else C_TRI):(C_TRIS if samp else C_TRI) + 128]
                    mneg = CON.t[:, (C_MNEGS if samp else C_MNEG):(C_MNEGS if samp else C_MNEG) + 128]
                    pcb = self.ps()
                    self.mmg(pcb.d[0], [(pcb.t[:, 0:128], BCb.t[:, 0, cols], BCb.t[:, 1, cols], True, True)], R=BCb.d)
                    at = AT[k2]
                    self.op(DVE, lambda: DVE.h.tensor_tensor(out=at.t[:], in0=tri.unsqueeze(1).to_broadcast([128, 4, 128]),
                                                             in1=AA.t[:, ti, hsl].unsqueeze(2).to_broadcast([128, 4, 128]), op=ALU.mult),
                            R=AA.d + CON.d, W=at.d)
                    pd = self.ps()
                    self.mmg(pd.d[0], [(pd.t[:, :], CON.t[:, C_ONES:C_ONES + 128], at.t[:].rearrange("p h l -> p (h l)"), True, False),
                                       (pd.t[:, :].rearrange("p (h l) -> p h l", l=128), CON.t[:, C_ID:C_ID + 128], mneg.unsqueeze(1).to_broadcast([128, 4, 128]), False, True)],
                             R=at.d + CON.d)
                    lt = LT[k2]
                    for h in range(4):
                        self.op(ACT, lambda h=h: ACT.h.activation(out=lt.t[:, h, :], in_=pd.t[:, h * 128:(h + 1) * 128], func=AF.Exp,
                                                                 bias=NACS.t[:, ti, 4 * g + h:4 * g + h + 1]), R=pd.d + NACS.d, W=lt.d)
                    mt = MT[k2]
                    self.op(DVE, lambda: DVE.h.tensor_tensor(out=mt.t[:], in0=lt.t[:], in1=pcb.t[:, 0:128].unsqueeze(1).to_broadcast([128, 4, 128]), op=ALU.mult),
                            R=lt.d + pcb.d, W=mt.d)
                    xdt = XDT[k2]; xdd = XDD[k2]; y1 = Y1[k2]; y2 = Y2[k2]
                    xs3 = XTK.t[:, ti, :].rearrange("p (h q) -> p h q", q=64)
                    self.op(DVE, lambda: DVE.h.tensor_tensor(out=y1.t[:].rearrange("p (h q) -> p h q", q=64), in0=xs3,
                                                             in1=DT.t[:, ti, hsl].unsqueeze(2).to_broadcast([128, 4, 64]), op=ALU.mult), R=XTK.d + DT.d, W=y1.d)
                    self.op(ACT, lambda: ACT.h.copy(xdt.t[:], y1.t[:]), R=y1.d, W=xdt.d)
                    self.op(DVE, lambda: DVE.h.tensor_tensor(out=xdd.t[:].rearrange("p (h q) -> p h q", q=64), in0=y1.t[:].rearrange("p (h q) -> p h q", q=64),
                                                             in1=DEND.t[:, ti, hsl].unsqueeze(2).to_broadcast([128, 4, 64]), op=ALU.mult), R=y1.d + DEND.d, W=xdd.d)
                    py = self.ps()
                    self.mmg(py.d[0], [(py.t[:, h * 64:(h + 1) * 64], mt.t[:, h, :], xdt.t[:, h * 64:(h + 1) * 64], True, True) for h in range(4)], R=mt.d + xdt.d)
                    po = self.ps()
                    if not samp:
                        self.op(ACT, lambda: ACT.h.copy(HB.t[:], hst), R=hsd, W=HB.d)
                        self.mmg(po.d[0], [(po.t[:, 0:256], BCb.t[:, 1, cols], HB.t[:], True, True)], R=BCb.d + HB.d)
                    else:
                        h0 = self.H0T[0]
                        self.dma(PQ, self.SW, h0.t[:], self.ssdT_d[e, g], W=h0.d)
                        self.op(DVE, lambda: DVE.h.tensor_tensor(out=CM.t[:], in0=BCb.t[:, 1, cols].unsqueeze(1).to_broadcast([128, NSEQ, 128]), in1=self.SMTB.t[:], op=ALU.mult),
                                R=BCb.d + self.SMTB.d, W=CM.d)
                        self.mmg(po.d[0], [(po.t[:, 0:256], CM.t[:, b, :], h0.t[:, b, :], b == 0, b == NSEQ - 1) for b in range(NSEQ)], R=CM.d + h0.d)
                    self.op(DVE, lambda: DVE.h.tensor_tensor(out=y1.t[:].rearrange("p (h q) -> p h q", q=64), in0=po.t[:, 0:256].rearrange("p (h q) -> p h q", q=64),
                                                             in1=EACS.t[:, ti, hsl].unsqueeze(2).to_broadcast([128, 4, 64]), op=ALU.mult), R=po.d + EACS.d + xdt.d, W=y1.d)
                    self.op(DVE, lambda: DVE.h.tensor_tensor(out=y1.t[:], in0=y1.t[:], in1=py.t[:, 0:256], op=ALU.add), R=y1.d + py.d, W=y1.d)
                    self.op(DVE, lambda: DVE.h.tensor_tensor(out=y2.t[:].rearrange("p (h q) -> p h q", q=64), in0=xs3,
                                                             in1=self.ROW.t[:, R_DSK + e * 32 + 4 * g:R_DSK + e * 32 + 4 * g + 4].unsqueeze(2).to_broadcast([128, 4, 64]), op=ALU.mult),
                            R=XTK.d + self.ROW.d, W=y2.d)
                    self.op(DVE, lambda: DVE.h.tensor_tensor(out=y1.t[:], in0=y1.t[:], in1=y2.t[:], op=ALU.add), R=y1.d + y2.d, W=y1.d)
                    self.op(DVE, lambda: DVE.h.tensor_tensor(out=y1.t[:], in0=y1.t[:], in1=ZG.t[:, ti, :], op=ALU.mult), R=y1.d + ZG.d, W=y1.d)
                    ss = SS[k2]
                    self.op(ACT, lambda: ACT.h.activation(out=y2.t[:], in_=y1.t[:], func=AF.Square, accum_out=ss.t[:]), R=y1.d, W=y2.d + ss.d)
                    self.op(ACT, lambda: ACT.h.activation(out=ss.t[:], in_=ss.t[:], func=AF.Ln, bias=RMS_EPS, scale=1.0 / 256), R=ss.d, W=ss.d)
                    self.op(ACT, lambda: ACT.h.activation(out=ss.t[:], in_=ss.t[:], func=AF.Exp, scale=-0.5), R=ss.d, W=ss.d)
                    self.op(DVE, lambda: DVE.h.tensor_scalar(out=y1.t[:], in0=y1.t[:], scalar1=ss.t[:], scalar2=None, op0=ALU.mult), R=y1.d + ss.d, W=y1.d)
                    for j in range(2):
                        pst = self.ps()
                        self.tr(pst, 128, y1.t[:, j * 128:(j + 1) * 128], R=y1.d)
                        c = 2 * g + j
                        ng = self.VEC.t[:, OFF_NRG + e * 16 + c:OFF_NRG + e * 16 + c + 1]
                        self.op(ACT, lambda: ACT.h.activation(out=YB.t[:, c, cols], in_=pst.t[:, 0:128], func=AF.Copy, scale=ng), R=pst.d + self.VEC.d, W=[YB.d[c]])
                    if not samp:
                        pss = self.ps()
                        self.mmg(pss.d[0], [(pss.t[:, 0:256], BTK.t[:, ti, :], xdd.t[:], True, True)], R=BTK.d + xdd.d)
                        self.op(DVE, lambda: DVE.h.tensor_tensor(out=HTMP.t[:].rearrange("p (h q) -> p h q", q=64), in0=hst.rearrange("p (h q) -> p h q", q=64),
                                                                 in1=CDEC.t[:, ti, hsl].unsqueeze(2).to_broadcast([128, 4, 64]), op=ALU.mult), R=hsd + CDEC.d, W=HTMP.d)
                        self.op(DVE, lambda: DVE.h.tensor_tensor(out=hst, in0=HTMP.t[:], in1=pss.t[:, 0:256], op=ALU.add), R=HTMP.d + pss.d + HB.d, W=hsd)
                    else:
                        self.op(DVE, lambda: DVE.h.tensor_tensor(out=BM.t[:], in0=BTK.t[:, ti, :].unsqueeze(1).to_broadcast([128, NSEQ, 128]),
                                                                 in1=CON.t[:, C_SIND:C_SIND + NSEQ].unsqueeze(2).to_broadcast([128, NSEQ, 128]), op=ALU.mult),
                                R=BTK.d + CON.d, W=BM.d)
                        self.op(DVE, lambda: DVE.h.tensor_copy(AEX.t[:].rearrange("p (h q) -> p h q", q=64), AA.t[:, ti, hsl].unsqueeze(2).to_broadcast([128, 4, 64])),
                                R=AA.d, W=AEX.d)
                        for hf in range(2):
                            pc = self.ps()
                            self.mmg(pc.d[0], [(pc.t[:, 0:NSEQ], AEX.t[:, hf * 128:(hf + 1) * 128], CON.t[:, C_SIND:C_SIND + NSEQ], True, True)], R=AEX.d + CON.d)
                            self.op(ACT, lambda: ACT.h.activation(out=CDC.t[:, hf, :], in_=pc.t[:, 0:NSEQ], func=AF.Exp), R=pc.d, W=CDC.d)
                            h0n = H0N[hf]; hno = HNO[hf]
                            self.dma(self.PQ, self.SL, h0n.t[:], self.ssdN_d[e, g, hf], W=h0n.d)
                            for b in range(NSEQ):
                                pn = self.ps()
                                self.mmg(pn.d[0], [(pn.t[:, 0:128], xdd.t[:, hf * 128:(hf + 1) * 128], BM.t[:, b, :], True, True)], R=xdd.d + BM.d)
                                self.op(DVE, lambda: DVE.h.scalar_tensor_tensor(out=hno.t[:, b, :], in0=h0n.t[:, b, :], scalar=CDC.t[:, hf, b:b + 1], in1=pn.t[:, 0:128],
                                                                                op0=ALU.mult, op1=ALU.add), R=h0n.d + CDC.d + pn.d, W=hno.d)
                            self.dma(self.PQ, self.SO, self.hN_s[e, g, hf], hno.t[:], R=hno.d)
                if self.last and Tp:
                    self.dma(self.PQ, self.SO, self.hT_p[e, :, g, :], hst, R=hsd)
            for s in range(4):
                self.linear(("w_out", e, s), 16, YB, self.acc_epi(s * 4, False))
            self.barrier()


def _slabs(W, ncols=512):
    K, N = W.shape
    kc = K // 128
    ns = N // ncols
    return np.ascontiguousarray(W.reshape(kc, 128, ns, ncols).transpose(2, 1, 0, 3))


def _fm(v, nch):
    sh = v.shape[:-1]
    a = v.reshape(*sh, nch, 128)
    return np.moveaxis(a, -1, 0)


def _consts():
    c = np.zeros((128, NCON), np.float32)
    idx = np.arange(128)
    c[:, C_ID:C_ID + 128] = np.eye(128)
    tri = (idx[:, None] <= idx[None, :]).astype(np.float32)
    same = (idx[:, None] // ST == idx[None, :] // ST).astype(np.float32)
    c[:, C_TRI:C_TRI + 128] = tri
    c[:, C_TRIS:C_TRIS + 128] = tri * same
    c[:, C_MNEG:C_MNEG + 128] = np.where(tri > 0, 0.0, -30000.0)
    c[:, C_MNEGS:C_MNEGS + 128] = np.where(tri * same > 0, 0.0, -30000.0)
    c[:, C_ONES:C_ONES + 128] = 1.0
    c[:, C_SAME:C_SAME + 128] = same
    sind = (idx[:, None] // ST == np.arange(NSEQ)[None, :]).astype(np.float32)
    c[:, C_SIND:C_SIND + NSEQ] = sind
    for gi in range(4):
        w = 2 << gi
        c[:, C_ICNT + gi * 16:C_ICNT + gi * 16 + 16] = 1.0 / np.minimum(np.arange(16) + 1, w)
    c[:, C_SMT:C_SMT + 16 * 128] = np.broadcast_to(sind.T.reshape(1, 16 * 128), (128, 16 * 128))
    return c


def prepare_shared(inp):
    sh = {}
    w_in = inp["w_in_even"]
    slabs = []
    for e in range(2):
        W = w_in[e]
        z = W[:, 0:2048]; xbc = W[:, 2048:6144]; gb = W[:, 6176:8224]; gc = W[:, 8224:10272]; hs_ = W[:, 10272:12320]
        cols = []
        for g in range(8):
            cols.append(np.concatenate([z[:, g * 256:(g + 1) * 256], xbc[:, g * 256:(g + 1) * 256]], 1))
        for j in range(4):
            g0, g1 = 2 * j, 2 * j + 1
            cols.append(np.concatenate([xbc[:, 2048 + g0 * 128:2048 + (g0 + 1) * 128], xbc[:, 3072 + g0 * 128:3072 + (g0 + 1) * 128],
                                        xbc[:, 2048 + g1 * 128:2048 + (g1 + 1) * 128], xbc[:, 3072 + g1 * 128:3072 + (g1 + 1) * 128]], 1))
        for j in range(4):
            cols.append(hs_[:, j * 512:(j + 1) * 512]); cols.append(gc[:, j * 512:(j + 1) * 512]); cols.append(gb[:, j * 512:(j + 1) * 512])
        slabs.append(np.stack([_slabs(cm)[0] for cm in cols]))
    sh["w_in_s"] = np.stack(slabs)
    sh["w_dt_s"] = np.stack([_slabs(w_in[e][:, 6144:6176], 32)[0] for e in range(2)])
    wo = inp["w_out_even"]
    sh["w_out_s"] = np.stack([np.concatenate([_slabs(wo[e][0:2048]), _slabs(wo[e][2048:4096])]) for e in range(2)])
    sh["w_pool_s"] = np.stack([np.stack([_slabs(inp["w_pool"][o, g])[0] for g in range(4)]) for o in range(2)])
    for nm, key in (("wq_s", "wq_x"), ("wk_s", "wk_x"), ("wv_s", "wv_x"), ("wo_s", "wo_x")):
        sh[nm] = np.stack([_slabs(inp[key][l]) for l in range(4)])
    sh["w_up_s"] = np.stack([_slabs(inp["w_up"][l]) for l in range(4)])
    sh["w_down_s"] = np.stack([np.concatenate([_slabs(inp["w_down"][l][q * 2048:(q + 1) * 2048]) for q in range(4)]) for l in range(4)])
    vec = np.zeros((128, NVEC), np.float32)
    vec[:, OFF_LNG:OFF_LNG + 192] = _fm(inp["ln_g"], 16).reshape(128, 192)
    vec[:, OFF_LNB:OFF_LNB + 192] = _fm(inp["ln_b"], 16).reshape(128, 192)
    vec[:, OFF_PSC:OFF_PSC + 32] = _fm(inp["pool_scale"], 16).reshape(128, 32)
    vec[:, OFF_SCW:OFF_SCW + 96] = _fm(inp["sc_conv_w"], 16).transpose(0, 1, 3, 2).reshape(128, 96)
    vec[:, OFF_CVW:OFF_CVW + 256] = _fm(inp["ssd_conv_w"], 32).transpose(0, 1, 3, 2).reshape(128, 256)
    vec[:, OFF_CVB:OFF_CVB + 64] = _fm(inp["ssd_conv_b"], 32).reshape(128, 64)
    vec[:, OFF_NRG:OFF_NRG + 32] = _fm(inp["ssd_norm_g"], 16).reshape(128, 32)
    sh["vecs"] = vec
    rows = np.zeros((128, NROW), np.float32)
    rows[:, R_DTB:R_DTB + 64] = inp["ssd_dt_bias"].reshape(1, 64)
    rows[:, R_ALOG:R_ALOG + 64] = inp["ssd_a_log"].reshape(1, 64)
    rows[:, R_DSK:R_DSK + 64] = inp["ssd_d"].reshape(1, 64)
    sh["rows"] = rows
    sh["consts"] = _consts()
    return sh


def prepare_core(inp, seq, sb0, TPp):
    m = {}
    xp = inp["x_prompt"][seq, :TPp]
    xs = inp["x_sample"][sb0:sb0 + NSEQ].reshape(NSEQ * ST, D)
    xa = np.concatenate([xp, xs], 0)
    m["xT"] = np.ascontiguousarray(xa.T.reshape(16, 128, -1).transpose(1, 0, 2))
    m["memT"] = np.ascontiguousarray(inp["mem_prompt"][seq].T.reshape(16, 128, 256).transpose(1, 0, 2))
    ck = inp["cache_mem_k"][:, sb0:sb0 + NSEQ]
    m["kT_s"] = np.ascontiguousarray(ck.reshape(4, NSEQ, 256, 4, 4, 128).transpose(0, 1, 3, 5, 4, 2))
    cv = inp["cache_mem_v"][:, sb0:sb0 + NSEQ]
    m["v_s"] = np.ascontiguousarray(cv.reshape(4, NSEQ, 2, 128, 4, 512).transpose(0, 1, 4, 3, 2, 5))
    s = inp["state_ssd"][:, sb0:sb0 + NSEQ]
    m["ssdT_s"] = np.ascontiguousarray(s.reshape(2, NSEQ, 8, 256, 128).transpose(0, 2, 4, 1, 3))
    m["ssdN_s"] = np.ascontiguousarray(s.reshape(2, NSEQ, 8, 2, 128, 128).transpose(0, 2, 3, 4, 1, 5))
    cs = inp["state_ssd_conv"][:, sb0:sb0 + NSEQ]
    m["convT_s"] = np.ascontiguousarray(cs.reshape(2, NSEQ, 3, 32, 128).transpose(0, 4, 3, 1, 2))
    ss = inp["state_short_conv"][:, sb0:sb0 + NSEQ]
    m["scT_s"] = np.ascontiguousarray(ss.reshape(2, NSEQ, 2, 16, 128).transpose(0, 4, 3, 1, 2))
    sp = inp["state_pool"][:, sb0:sb0 + NSEQ]
    m["poolT_s"] = np.ascontiguousarray(sp.reshape(2, NSEQ, 15, 16, 128).transpose(0, 4, 3, 1, 2))
    return m


def unT(a):
    return a.transpose(2, 1, 0).reshape(a.shape[2], -1)


BLOCKS = [(4, False), (4, False), (4, False), (3, False), (1, True)]
_prog = {}


def kernel(**inp):
    inp = {k: np.asarray(v) for k, v in inp.items()}
    key = "full"
    if key not in _prog:
        b = Builder(BLOCKS, 4)
        b.cvtmp = None
        _prog[key] = (b, b.build())
    b, nc = _prog[key]
    sh = prepare_shared(inp)
    in_maps = []
    for core in range(8):
        m = dict(sh)
        m.update(prepare_core(inp, core % 4, core * NSEQ, 2048))
        in_maps.append(m)
    res = run_bass_kernel_spmd(nc, in_maps, core_ids=list(range(8))).results
    return assemble(res, 2048)


def assemble(res, TPp):
    nb = len(res)
    npr = min(4, nb)
    y_prompt = np.stack([unT(res[c]["yT"][:, :, :TPp]) for c in range(npr)])
    if res[0]["yT"].shape[2] > TPp:
        y_sample = np.concatenate([unT(res[c]["yT"][:, :, TPp:]).reshape(NSEQ, ST, D) for c in range(nb)])
    else:
        y_sample = np.zeros((nb * NSEQ, ST, D), np.float32)
    h_p = np.stack([res[c]["hT_p"].reshape(2, 128, 32, 64).transpose(0, 2, 3, 1) for c in range(npr)], 1)
    h_s = np.concatenate([res[c]["hN_s"].transpose(0, 4, 1, 2, 3, 5).reshape(2, NSEQ, 32, 64, 128) for c in range(nb)], 1)
    cb_p = np.stack([res[c]["cbT_p"].transpose(0, 3, 2, 1).reshape(2, 3, 4096) for c in range(npr)], 1)
    cb_s = np.concatenate([res[c]["cbT_s"].transpose(0, 3, 4, 2, 1).reshape(2, NSEQ, 3, 4096) for c in range(nb)], 1)
    sb_p = np.stack([res[c]["sbT_p"].transpose(0, 3, 2, 1).reshape(2, 2, 2048) for c in range(npr)], 1)
    sb_s = np.concatenate([res[c]["sbT_s"].transpose(0, 3, 4, 2, 1).reshape(2, NSEQ, 2, 2048) for c in range(nb)], 1)
    pb_p = np.stack([res[c]["pbT_p"].transpose(0, 3, 2, 1).reshape(2, 15, 2048) for c in range(npr)], 1)
    pb_s = np.concatenate([res[c]["pbT_s"].transpose(0, 3, 4, 2, 1).reshape(2, NSEQ, 15, 2048) for c in range(nb)], 1)
    mk_p = np.stack([res[c]["mk_p"].reshape(4, 256, 4, 512) for c in range(npr)], 1)
    mv_p = np.stack([res[c]["mv_p"].reshape(4, 256, 4, 512) for c in range(npr)], 1)
    outs = (y_prompt, y_sample, h_p, h_s, cb_p, cb_s, sb_p, sb_s, pb_p, pb_s, mk_p, mv_p)
    return tuple(np.ascontiguousarray(o, dtype=np.float32) for o in outs)
```

```python
import numpy as np
import ml_dtypes
from contextlib import ExitStack
import concourse.bass as bass
import concourse.mybir as mybir
from concourse.bass_utils import run_bass_kernel_spmd

F32 = mybir.dt.float32
BF16 = mybir.dt.bfloat16
AF = mybir.ActivationFunctionType
ALU = mybir.AluOpType
AX = mybir.AxisListType

D = 2048
DEPTH = 4
ALPHA = (2.0 * DEPTH) ** 0.25
LN_EPS = 1e-5
RMS_EPS = 1e-5
NSEQ = 16
ST = 8
XSCALE = 512 ** -0.5

OFF_LNG = 0
OFF_LNB = OFF_LNG + 4 * 3 * 16
OFF_PSC = OFF_LNB + 4 * 3 * 16
OFF_SCW = OFF_PSC + 2 * 16
OFF_CVW = OFF_SCW + 2 * 16 * 3
OFF_CVB = OFF_CVW + 2 * 32 * 4
OFF_NRG = OFF_CVB + 2 * 32
NVEC = OFF_NRG + 2 * 16
C_ID = 0; C_TRI = 128; C_TRIS = 256; C_MNEG = 384; C_MNEGS = 512; C_ONES = 640; C_SAME = 768
C_SIND = 896; C_ICNT = 912; C_SMT = 976; NCON = C_SMT + 16 * 128
R_DTB = 0; R_ALOG = 64; R_DSK = 128; NROW = 192


class Dep:
    __slots__ = ("w", "r", "ps")

    def __init__(self):
        self.w = None
        self.r = {}
        self.ps = False


class Eng:
    def __init__(self, h, sem, is_pe=False):
        self.h = h; self.sem = sem; self.cnt = 0; self.seen = {}; self.is_pe = is_pe


class Stream:
    def __init__(self, sems):
        self.sems = sems; self.cnts = [0] * len(sems); self.idx = 0


class Tl:
    def __init__(self, t, n):
        self.t = t
        self.d = [Dep() for _ in range(n)]


class Builder:
    def __init__(self, blocks, n_layers, first_seq_block=True):
        self.blocks = blocks
        self.n_layers = n_layers
        self.TP_total = sum(b[0] for b in blocks) * 128
        assert self.TP_total in (2048,) or len(blocks) <= 2, "prompt tiles must cover the full sequence"
        self.has_sample = any(b[1] for b in blocks)
        self.TC = self.TP_total + (128 if self.has_sample else 0)
        self.TMAX = max(b[0] * 128 + (128 if b[1] else 0) for b in blocks)
        self.nc = bass.Bass("TRN2", target_bir_lowering=False)
        self.es = ExitStack()
        self.ins = {}
        self.outs = {}
        self.nsem = 0

    def sem(self):
        self.nsem += 1
        return self.es.enter_context(self.nc.semaphore(f"s{self.nsem}"))

    def din(self, name, shape):
        self.ins[name] = shape
        return self.nc.dram_tensor(name, list(shape), F32, kind="ExternalInput").ap()

    def dout(self, name, shape):
        self.outs[name] = shape
        return self.nc.dram_tensor(name, list(shape), F32, kind="ExternalOutput").ap()

    def tile(self, name, shape, dt, n=1, stack=None):
        self.ntile = getattr(self, "ntile", 0) + 1
        t = (stack or self.es).enter_context(self.nc.sbuf_tensor(f"{name}_{self.ntile}", list(shape), dt))
        return Tl(t, n)

    def wait_for(self, E, pairs):
        best = {}
        for sem, v in pairs:
            k = sem.num
            if k not in best or best[k][1] < v:
                best[k] = (sem, v)
        for k, (sem, v) in best.items():
            if E.is_pe and sem is E.sem:
                continue
            if E.seen.get(k, 0) >= v:
                continue
            E.h.wait_ge(sem, v)
            E.seen[k] = v

    @staticmethod
    def pairs(R, W):
        for d in R:
            if d.w:
                yield d.w
            if d.ps:
                yield from d.r.values()
        for d in W:
            if d.w:
                yield d.w
            yield from d.r.values()

    def op(self, E, fn, R=(), W=()):
        self.wait_for(E, self.pairs(R, W))
        ins = fn()
        E.cnt += 1
        ins.then_inc(E.sem, 1)
        m = (E.sem, E.cnt)
        for d in W:
            d.w = m; d.r = {}
        for d in R:
            d.r[E.sem.num] = m

    def dma(self, Q, S, out, in_, R=(), W=()):
        j = S.idx % len(S.sems)
        S.idx += 1
        pr = list(self.pairs(R, W))
        if S.cnts[j]:
            pr.append((S.sems[j], S.cnts[j]))
        self.wait_for(Q, pr)
        Q.h.dma_start(out=out, in_=in_).then_inc(S.sems[j], 16)
        S.cnts[j] += 16
        m = (S.sems[j], S.cnts[j])
        for d in W:
            d.w = m; d.r = {}
        for d in R:
            d.r[S.sems[j].num] = m

    def mmg(self, psd, mms, R):
        PE = self.PE
        self.wait_for(PE, self.pairs(R, [psd]))
        ins = None
        for (o, l, r, st, sp) in mms:
            ins = PE.h.matmul(o, l, r, start=st, stop=sp)
        PE.cnt += 1
        ins.then_inc(PE.sem, 1)
        m = (PE.sem, PE.cnt)
        psd.w = m; psd.r = {}
        for d in R:
            d.r[PE.sem.num] = m

    def tr(self, ps, n_out_part, in_ap, R, ncols=128):
        PE = self.PE
        self.wait_for(PE, self.pairs(list(R) + [self.CON.d[0]], [ps.d[0]]))
        ins = PE.h.transpose(ps.t[:, 0:128], in_ap, self.CON.t[:, C_ID:C_ID + 128])
        PE.cnt += 1
        ins.then_inc(PE.sem, 1)
        m = (PE.sem, PE.cnt)
        ps.d[0].w = m; ps.d[0].r = {}
        for d in R:
            d.r[PE.sem.num] = m

    def ps(self):
        p = self.PS[self.psi % 8]
        self.psi += 1
        return p

    def ws(self):
        w = self.WS[self.wsi % len(self.WS)]
        self.wsi += 1
        return w

    def barrier(self):
        engs = [self.PE, self.ACT, self.DVE, self.PQ]
        targets = [(e.sem, e.cnt) for e in (self.PE, self.ACT, self.DVE) if e.cnt > 0]
        for s in (self.SL, self.SO):
            targets += [(sm, c) for sm, c in zip(s.sems, s.cnts) if c > 0]
        for e in engs:
            self.wait_for(e, targets)

    def build(self):
        nc = self.nc
        TC, TMAX = self.TC, self.TMAX
        NL = self.n_layers
        self.xT = self.din("xT", (128, 16, TC))
        self.vecs_d = self.din("vecs", (128, NVEC))
        self.rows_d = self.din("rows", (128, NROW))
        self.con_d = self.din("consts", (128, NCON))
        self.memT_d = self.din("memT", (128, 16, 256))
        self.w_in_d = self.din("w_in_s", (2, 24, 128, 16, 512))
        self.w_dt_d = self.din("w_dt_s", (2, 128, 16, 32))
        self.w_out_d = self.din("w_out_s", (2, 8, 128, 16, 512))
        self.w_pool_d = self.din("w_pool_s", (2, 4, 128, 4, 512))
        self.wq_d = self.din("wq_s", (4, 4, 128, 16, 512))
        self.wk_d = self.din("wk_s", (4, 4, 128, 16, 512))
        self.wv_d = self.din("wv_s", (4, 4, 128, 16, 512))
        self.wo_d = self.din("wo_s", (4, 4, 128, 16, 512))
        self.wup_d = self.din("w_up_s", (4, 16, 128, 16, 512))
        self.wdn_d = self.din("w_down_s", (4, 16, 128, 16, 512))
        self.kT_d = self.din("kT_s", (4, NSEQ, 4, 128, 4, 256))
        self.v_d = self.din("v_s", (4, NSEQ, 4, 128, 2, 512))
        self.ssdT_d = self.din("ssdT_s", (2, 8, 128, NSEQ, 256))
        self.ssdN_d = self.din("ssdN_s", (2, 8, 2, 128, NSEQ, 128))
        self.convT_d = self.din("convT_s", (2, 8, 128, 4, NSEQ, 3))
        self.scT_d = self.din("scT_s", (2, 128, 16, NSEQ, 2))
        self.poolT_d = self.din("poolT_s", (2, 128, 16, NSEQ, 15))
        self.yT = self.dout("yT", (128, 16, TC))
        self.hT_p = self.dout("hT_p", (2, 128, 8, 256))
        self.hN_s = self.dout("hN_s", (2, 8, 2, 128, NSEQ, 128))
        self.cbT_p = self.dout("cbT_p", (2, 128, 32, 3))
        self.cbT_s = self.dout("cbT_s", (2, 8, 128, 4, NSEQ, 3))
        self.sbT_p = self.dout("sbT_p", (2, 128, 16, 2))
        self.sbT_s = self.dout("sbT_s", (2, 128, 16, NSEQ, 2))
        self.pbT_p = self.dout("pbT_p", (2, 128, 16, 15))
        self.pbT_s = self.dout("pbT_s", (2, 128, 16, NSEQ, 15))
        self.mk_p = self.dout("mk_p", (4, 256, 2048))
        self.mv_p = self.dout("mv_p", (4, 256, 2048))
        self.PE = Eng(nc.tensor, self.sem(), True)
        self.ACT = Eng(nc.scalar, self.sem())
        self.DVE = Eng(nc.vector, self.sem())
        self.SP = Eng(nc.sync, self.sem())
        self.PQ = Eng(nc.gpsimd, self.sem())
        self.SL = Stream([self.sem() for _ in range(8)])
        self.SO = Stream([self.sem() for _ in range(8)])
        self.SW = Stream([self.sem() for _ in range(6)])
        self.SWB = Stream([self.sem() for _ in range(4)])
        self.SCS = Stream([self.sem() for _ in range(4)])
        self.wd = {"w_in": self.w_in_d, "w_dt": self.w_dt_d, "w_out": self.w_out_d, "w_pool": self.w_pool_d, "wq": self.wq_d, "wk": self.wk_d,
                   "wv": self.wv_d, "wo": self.wo_d, "w_up": self.wup_d, "w_dn": self.wdn_d}
        self.wc = {k: nc.dram_tensor("c_" + k, list(v.shape), BF16, kind="Internal").ap() for k, v in self.wd.items()}
        self.cdeps = {}
        T = TMAX
        self.XR = self.tile("XR", (128, 16, T), F32, 16)
        self.XB = self.tile("XB", (128, 16, T), BF16, 16)
        self.WS = [self.tile(f"WS{i}", (128, 16, 512), BF16) for i in range(2)]
        self.wsi = 0
        self.CON = self.tile("CON", (128, C_SMT), F32)
        self.VEC = self.tile("VEC", (128, NVEC), F32)
        self.ROW = self.tile("ROW", (128, NROW), F32)
        self.ONESB = self.tile("ONESB", (128, 128), BF16)
        self.SMTB = self.tile("SMTB", (128, NSEQ, 128), BF16)
        self.MEMT = self.tile("MEMT", (128, 16, 256), BF16)
        self.HS = [self.tile(f"HS{e}", (128, 8, 256), F32, 8) for e in range(2)]
        self.H0T = [self.tile(f"H0T{i}", (128, NSEQ, 256), BF16) for i in range(1)]
        self.CVH = [self.tile(f"CVH{e}", (128, 32, 3), F32, 32) for e in range(2)]
        self.SCH = [self.tile(f"SCH{e}", (128, 16, 2), F32, 16) for e in range(2)]
        self.PH = [self.tile(f"PH{o}", (128, 16, 15), F32, 16) for o in range(2)]
        self.AH = self.tile("AH", (128, 64), F32)
        self.PS = []
        for i in range(8):
            t = self.es.enter_context(nc.psum_tensor(f"PS{i}", [128, 512], F32))
            self.PS.append(Tl(t, 1))
            self.PS[-1].d[0].ps = True
        self.psi = 0
        SP, ACT, DVE, PQ = self.SP, self.ACT, self.DVE, self.PQ
        self.dma(self.PQ, self.SL, self.CON.t[:], self.con_d[:, 0:C_SMT], W=self.CON.d)
        self.dma(self.PQ, self.SL, self.VEC.t[:], self.vecs_d, W=self.VEC.d)
        self.dma(self.PQ, self.SL, self.ROW.t[:], self.rows_d, W=self.ROW.d)
        self.dma(PQ, self.SW, self.MEMT.t[:], self.memT_d, W=self.MEMT.d)
        self.op(DVE, lambda: DVE.h.tensor_scalar(out=self.ONESB.t[:], in0=self.CON.t[:, C_ONES:C_ONES + 128], scalar1=1.0 / D,
                                                 scalar2=None, op0=ALU.mult), R=self.CON.d, W=self.ONESB.d)
        with ExitStack() as st0:
            smt = self.tile("SMTF", (128, 16 * 128), F32, stack=st0)
            self.dma(self.PQ, self.SL, smt.t[:], self.con_d[:, C_SMT:C_SMT + 16 * 128], W=smt.d)
            self.op(DVE, lambda: DVE.h.tensor_copy(self.SMTB.t[:], smt.t[:].rearrange("p (b l) -> p b l", l=128)), R=smt.d, W=self.SMTB.d)
            self.DVE.h.wait_ge(self.DVE.sem, self.DVE.cnt)
            self.barrier()
        self.op(ACT, lambda: ACT.h.activation(out=self.AH.t[:], in_=self.ROW.t[:, R_ALOG:R_ALOG + 64], func=AF.Exp), R=self.ROW.d, W=self.AH.d)
        self.op(ACT, lambda: ACT.h.mul(self.AH.t[:], self.AH.t[:], -1.0), R=self.AH.d, W=self.AH.d)
        for tl in self.HS + self.CVH + self.SCH + self.PH:
            self.op(DVE, lambda tl=tl: DVE.h.memset(tl.t[:], 0.0), W=tl.d)
        pcol = 0
        for bi, (npt, hs) in enumerate(self.blocks):
            self.run_block(bi, npt, hs, pcol, first=(bi == 0), last=(bi == len(self.blocks) - 1))
            pcol += npt * 128
        self.barrier()
        for sm, c in zip(self.SO.sems, self.SO.cnts):
            if c:
                self.PQ.h.wait_ge(sm, c)
        return nc

    def run_block(self, bi, npt, hs, pcol, first, last):
        SP, ACT, DVE = self.SP, self.ACT, self.DVE
        Tp = npt * 128
        Ts = 128 if hs else 0
        T = Tp + Ts
        self.Tp, self.Ts, self.T, self.npt, self.hs = Tp, Ts, T, npt, hs
        self.first, self.last = first, last
        segs = []
        o = 0
        while o < Tp:
            l = min(512, Tp - o); segs.append((o, l)); o += l
        if hs:
            segs.append((Tp, 128))
        self.segs = segs
        self.msegs = []
        o = 0
        while o < T:
            l = min(512, T - o); self.msegs.append((o, l)); o += l
        for c in range(16):
            self.dma(self.PQ, self.SL, self.XR.t[:, c, 0:Tp], self.xT[:, c, pcol:pcol + Tp], W=[self.XR.d[c]])
            if hs:
                self.dma(self.PQ, self.SL, self.XR.t[:, c, Tp:T], self.xT[:, c, self.TP_total:self.TP_total + 128], W=[self.XR.d[c]])
            self.op(ACT, lambda c=c: ACT.h.copy(self.XB.t[:, c, 0:T], self.XR.t[:, c, 0:T]), R=[self.XR.d[c]], W=[self.XB.d[c]])
        import os
        stop = int(os.environ.get("KSTOP", "1000"))
        ph = [0]

        def run(f, *a):
            if ph[0] < stop:
                f(*a)
            ph[0] += 1
        for l in range(self.n_layers):
            if l % 2 == 0:
                run(self.even_sc, l // 2)
                run(self.even_ssd, l // 2)
            else:
                run(self.pool, l // 2)
            run(self.ln, l, 0)
            run(self.xattn, l)
            run(self.ln, l, 1)
            run(self.mlp, l)
            run(self.ln, l, 2)
        for c in range(16):
            self.dma(self.PQ, self.SO, self.yT[:, c, pcol:pcol + Tp], self.XR.t[:, c, 0:Tp], R=[self.XR.d[c]])
            if hs:
                self.dma(self.PQ, self.SO, self.yT[:, c, self.TP_total:self.TP_total + 128], self.XR.t[:, c, Tp:T], R=[self.XR.d[c]])
        self.barrier()

    def load_slab(self, key, kc, ncols):
        w = self.ws()
        name, idx = key[0], tuple(key[1:])
        src = self.wd[name][idx]
        cch = self.wc[name][idx]
        dep = self.cdeps.setdefault(key, Dep())
        if self.first:
            self.dma(self.PQ, self.SW, w.t[:, 0:kc, 0:ncols], src, W=w.d)
            if len(self.blocks) > 1:
                self.dma(self.PQ, self.SCS, cch, w.t[:, 0:kc, 0:ncols], R=w.d, W=[dep])
        else:
            self.dma(self.SP, self.SWB, w.t[:, 0:kc, 0:ncols], cch, R=[dep], W=w.d)
        return w

    def linear(self, src, kc, X, epi, ncols=512, w=None, segs=None):
        if w is None:
            w = self.load_slab(src, kc, ncols)
        for nci in range(ncols // 128):
            for (s0, sl) in (segs or self.msegs):
                ps = self.ps()
                mms = [(ps.t[:, 0:sl], w.t[:, k, nci * 128:(nci + 1) * 128], X.t[:, k, s0:s0 + sl], k == 0, k == kc - 1)
                       for k in range(kc)]
                self.mmg(ps.d[0], mms, R=w.d + X.d[0:kc])
                epi(nci, s0, sl, ps)
        return w

    def acc_epi(self, cbase, firstacc):
        DVE = self.DVE

        def epi(nci, s0, sl, ps):
            c = cbase + nci
            xr = self.XR.t[:, c, s0:s0 + sl]
            if firstacc:
                self.op(DVE, lambda: DVE.h.scalar_tensor_tensor(out=xr, in0=xr, scalar=ALPHA, in1=ps.t[:, 0:sl], op0=ALU.mult, op1=ALU.add),
                        R=[ps.d[0], self.XR.d[c]], W=[self.XR.d[c]])
            else:
                self.op(DVE, lambda: DVE.h.tensor_tensor(out=xr, in0=xr, in1=ps.t[:, 0:sl], op=ALU.add),
                        R=[ps.d[0], self.XR.d[c]], W=[self.XR.d[c]])
        return epi

    def ln(self, l, i):
        ACT, DVE = self.ACT, self.DVE
        T = self.T
        with ExitStack() as st:
            tA = [self.tile(f"lnA{j}", (128, T), BF16, stack=st) for j in range(2)]
            tB = [self.tile(f"lnB{j}", (128, T), BF16, stack=st) for j in range(2)]
            mean = self.tile("lnmean", (128, T), F32, stack=st)
            rstd = self.tile("lnrstd", (128, T), F32, stack=st)
            nmr = self.tile("lnnmr", (128, T), F32, stack=st)
            tmp = [self.tile(f"lntmp{j}", (128, T), F32, stack=st) for j in range(2)]
            pm = [self.ps() for _ in self.msegs]
            pq = [self.ps() for _ in self.msegs]
            for c in range(16):
                a = tA[c % 2]; b = tB[c % 2]
                self.op(DVE, lambda: DVE.h.tensor_copy(a.t[:], self.XR.t[:, c, 0:T]), R=[self.XR.d[c]], W=a.d)
                self.op(ACT, lambda: ACT.h.activation(out=b.t[:], in_=self.XR.t[:, c, 0:T], func=AF.Square), R=[self.XR.d[c]], W=b.d)
                for si, (s0, sl) in enumerate(self.msegs):
                    self.mmg(pm[si].d[0], [(pm[si].t[:, 0:sl], self.ONESB.t[:], a.t[:, s0:s0 + sl], c == 0, c == 15)], R=a.d + self.ONESB.d)
                    self.mmg(pq[si].d[0], [(pq[si].t[:, 0:sl], self.ONESB.t[:], b.t[:, s0:s0 + sl], c == 0, c == 15)], R=b.d + self.ONESB.d)
            for si, (s0, sl) in enumerate(self.msegs):
                mn = mean.t[:, s0:s0 + sl]; rs = rstd.t[:, s0:s0 + sl]; nm = nmr.t[:, s0:s0 + sl]
                self.op(ACT, lambda: ACT.h.copy(mn, pm[si].t[:, 0:sl]), R=pm[si].d, W=mean.d)
                self.op(DVE, lambda: DVE.h.tensor_tensor(out=nm, in0=mn, in1=mn, op=ALU.mult), R=mean.d, W=nmr.d)
                self.op(DVE, lambda: DVE.h.tensor_tensor(out=rs, in0=pq[si].t[:, 0:sl], in1=nm, op=ALU.subtract), R=pq[si].d + nmr.d, W=rstd.d)
                self.op(ACT, lambda: ACT.h.activation(out=rs, in_=rs, func=AF.Ln, bias=LN_EPS), R=rstd.d, W=rstd.d)
                self.op(ACT, lambda: ACT.h.activation(out=rs, in_=rs, func=AF.Exp, scale=-0.5), R=rstd.d, W=rstd.d)
                self.op(DVE, lambda: DVE.h.scalar_tensor_tensor(out=nm, in0=mn, scalar=-1.0, in1=rs, op0=ALU.mult, op1=ALU.mult),
                        R=mean.d + rstd.d, W=nmr.d)
            for c in range(16):
                t = tmp[c % 2]
                g = self.VEC.t[:, OFF_LNG + (l * 3 + i) * 16 + c:OFF_LNG + (l * 3 + i) * 16 + c + 1]
                bb = self.VEC.t[:, OFF_LNB + (l * 3 + i) * 16 + c:OFF_LNB + (l * 3 + i) * 16 + c + 1]
                self.op(DVE, lambda: DVE.h.tensor_tensor(out=t.t[:], in0=self.XR.t[:, c, 0:T], in1=rstd.t[:], op=ALU.mult),
                        R=[self.XR.d[c]] + rstd.d, W=t.d)
                self.op(DVE, lambda: DVE.h.tensor_tensor(out=t.t[:], in0=t.t[:], in1=nmr.t[:], op=ALU.add), R=t.d + nmr.d, W=t.d)
                self.op(ACT, lambda: ACT.h.activation(out=self.XR.t[:, c, 0:T], in_=t.t[:], func=AF.Identity, bias=bb, scale=g),
                        R=t.d + self.VEC.d, W=[self.XR.d[c]])
                self.op(ACT, lambda: ACT.h.activation(out=self.XB.t[:, c, 0:T], in_=t.t[:], func=AF.Identity, bias=bb, scale=g),
                        R=t.d + self.VEC.d, W=[self.XB.d[c]])
            self.barrier()

    def mlp(self, l):
        ACT, DVE = self.ACT, self.DVE
        T = self.T
        with ExitStack() as st:
            YB = self.tile("mlpY", (128, 16, T), BF16, 16, stack=st)
            rt = [self.tile(f"mlpr{j}", (128, 512), F32, stack=st) for j in range(3)]
            cnt = [0]
            for q in range(4):
                for s in range(4):
                    def epi(nci, s0, sl, ps, s=s):
                        c = s * 4 + nci
                        r = rt[cnt[0] % 3]; cnt[0] += 1
                        self.op(ACT, lambda: ACT.h.activation(out=r.t[:, 0:sl], in_=ps.t[:, 0:sl], func=AF.Relu), R=ps.d, W=r.d)
                        self.op(DVE, lambda: DVE.h.tensor_tensor(out=YB.t[:, c, s0:s0 + sl], in0=r.t[:, 0:sl], in1=r.t[:, 0:sl], op=ALU.mult),
                                R=r.d, W=[YB.d[c]])
                    self.linear(("w_up", l, q * 4 + s), 16, self.XB, epi)
                for s in range(4):
                    self.linear(("w_dn", l, q * 4 + s), 16, YB, self.acc_epi(s * 4, q == 0))
            self.barrier()

    def softmax_pt(self, ps_s, PT, col0, st_tiles, idx):
        ACT, DVE = self.ACT, self.DVE
        mx, nb, p, rs = st_tiles
        k = idx % 2
        self.op(DVE, lambda: DVE.h.reduce_max(out=mx[k].t[:], in_=ps_s.t[:, 0:256], axis=AX.X), R=ps_s.d, W=mx[k].d)
        self.op(DVE, lambda: DVE.h.tensor_scalar(out=nb[k].t[:], in0=mx[k].t[:], scalar1=-XSCALE, scalar2=None, op0=ALU.mult), R=mx[k].d, W=nb[k].d)
        self.op(ACT, lambda: ACT.h.activation(out=p[k].t[:], in_=ps_s.t[:, 0:256], func=AF.Exp, bias=nb[k].t[:], scale=XSCALE, accum_out=rs[k].t[:]),
                R=ps_s.d + nb[k].d, W=p[k].d + rs[k].d)
        self.op(DVE, lambda: DVE.h.reciprocal(out=rs[k].t[:], in_=rs[k].t[:]), R=rs[k].d, W=rs[k].d)
        self.op(DVE, lambda: DVE.h.tensor_scalar(out=p[k].t[:], in0=p[k].t[:], scalar1=rs[k].t[:], scalar2=None, op0=ALU.mult),
                R=p[k].d + rs[k].d, W=p[k].d)
        for mt in range(2):
            pst = self.ps()
            self.tr(pst, 128, p[k].t[:, mt * 128:(mt + 1) * 128], R=p[k].d)
            self.op(ACT, lambda: ACT.h.copy(PT.t[:, mt, col0:col0 + 128], pst.t[:, 0:128]), R=pst.d, W=[PT.d[mt]])

    def xattn(self, l):
        ACT, DVE, SP, PQ = self.ACT, self.DVE, self.SP, self.PQ
        T, Tp, Ts = self.T, self.Tp, self.Ts
        with ExitStack() as st:
            YB = self.tile("xaY", (128, 16, T), BF16, 16, stack=st)
            QT = self.tile("xaQ", (128, 4, T), BF16, 4, stack=st)
            KT = self.tile("xaK", (128, 4, 256), BF16, 4, stack=st)
            VH = self.tile("xaV", (128, 2, 512), BF16, 2, stack=st)
            PT = self.tile("xaPT", (128, 2, T), BF16, 2, stack=st)
            QM = [self.tile(f"xaQM{j}", (128, 4, 128), BF16, stack=st) for j in range(2)]
            GB = 4
            self.KTB = [self.tile(f"KTB{i}", (128, GB, 4, 256) if Ts else (128, 1, 1, 1), BF16, stack=st) for i in range(2)]
            self.VB = [self.tile(f"VB{i}", (128, GB, 2, 512) if Ts else (128, 1, 1, 1), BF16, stack=st) for i in range(2)]
            stg = [self.tile(f"xaS{j}", (128, 512), F32, stack=st) for j in range(2)]
            mx = [self.tile(f"xamx{j}", (128, 1), F32, stack=st) for j in range(2)]
            nb = [self.tile(f"xanb{j}", (128, 1), F32, stack=st) for j in range(2)]
            p = [self.tile(f"xap{j}", (128, 256), F32, stack=st) for j in range(2)]
            rs = [self.tile(f"xars{j}", (128, 1), F32, stack=st) for j in range(2)]
            sm = (mx, nb, p, rs)
            if Ts:
                for q in QM:
                    self.op(DVE, lambda q=q: DVE.h.memset(q.t[:], 0.0), W=q.d)
            sidx = 0
            stgi = 0
            import os
            XS = int(os.environ.get("XSTOP", "100"))
            for hd in range(4):
                def qepi(nci, s0, sl, ps):
                    self.op(ACT, lambda: ACT.h.copy(QT.t[:, nci, s0:s0 + sl], ps.t[:, 0:sl]), R=ps.d, W=[QT.d[nci]])
                self.linear(("wq", l, hd), 16, self.XB, qepi)
                if Tp and XS >= 2:
                    wk = self.load_slab(("wk", l, hd), 16, 512)
                    for nci in range(4):
                        ps = self.ps()
                        self.mmg(ps.d[0], [(ps.t[:, 0:256], wk.t[:, k, nci * 128:(nci + 1) * 128], self.MEMT.t[:, k, :], k == 0, k == 15) for k in range(16)],
                                 R=wk.d + self.MEMT.d)
                        self.op(ACT, lambda: ACT.h.copy(KT.t[:, nci, :], ps.t[:, 0:256]), R=ps.d, W=[KT.d[nci]])
                    if self.first and XS >= 3:
                        for mt in range(2):
                            ps = self.ps()
                            self.mmg(ps.d[0], [(ps.t[:, :], self.MEMT.t[:, k, mt * 128:(mt + 1) * 128], wk.t[:, k, :], k == 0, k == 15) for k in range(16)],
                                     R=wk.d + self.MEMT.d)
                            sg = stg[stgi % 2]; stgi += 1
                            self.op(ACT, lambda: ACT.h.copy(sg.t[:], ps.t[:, :]), R=ps.d, W=sg.d)
                            self.dma(self.PQ, self.SO, self.mk_p[l, mt * 128:(mt + 1) * 128, hd * 512:(hd + 1) * 512], sg.t[:], R=sg.d)
                    wv = self.load_slab(("wv", l, hd), 16, 512)
                    for mt in range(2 if XS >= 4 else 0):
                        ps = self.ps()
                        self.mmg(ps.d[0], [(ps.t[:, :], self.MEMT.t[:, k, mt * 128:(mt + 1) * 128], wv.t[:, k, :], k == 0, k == 15) for k in range(16)],
                                 R=wv.d + self.MEMT.d)
                        self.op(ACT, lambda: ACT.h.copy(VH.t[:, mt, :], ps.t[:, :]), R=ps.d, W=[VH.d[mt]])
                        if self.first:
                            sg = stg[stgi % 2]; stgi += 1
                            self.op(ACT, lambda: ACT.h.copy(sg.t[:], ps.t[:, :]), R=ps.d, W=sg.d)
                            self.dma(self.PQ, self.SO, self.mv_p[l, mt * 128:(mt + 1) * 128, hd * 512:(hd + 1) * 512], sg.t[:], R=sg.d)
                    for ti in range(self.npt if XS >= 5 else 0):
                        ps = self.ps()
                        self.mmg(ps.d[0], [(ps.t[:, 0:256], QT.t[:, dc, ti * 128:(ti + 1) * 128], KT.t[:, dc, :], dc == 0, dc == 3) for dc in range(4)],
                                 R=QT.d + KT.d)
                        self.softmax_pt(ps, PT, ti * 128, sm, sidx); sidx += 1
                    for dc in range(4 if XS >= 6 else 0):
                        for (s0, sl) in self.segs:
                            if s0 >= Tp:
                                continue
                            ps = self.ps()
                            self.mmg(ps.d[0], [(ps.t[:, 0:sl], VH.t[:, mt, dc * 128:(dc + 1) * 128], PT.t[:, mt, s0:s0 + sl], mt == 0, mt == 1) for mt in range(2)],
                                     R=VH.d + PT.d)
                            c = hd * 4 + dc
                            self.op(ACT, lambda: ACT.h.copy(YB.t[:, c, s0:s0 + sl], ps.t[:, 0:sl]), R=ps.d, W=[YB.d[c]])
                if Ts:
                    pss = self.ps()
                    for b in range(NSEQ):
                        kbt = self.KTB[(b // GB) % 2]
                        if b % GB == 0:
                            self.dma(PQ, self.SW, kbt.t[:], self.kT_d[l, b:b + GB, hd].rearrange("b p d m -> p b d m"), W=kbt.d)
                        kb = Tl(kbt.t[:, b % GB], 0); kb.d = kbt.d
                        qm = QM[b % 2]
                        cs = slice(b * ST, (b + 1) * ST)
                        self.op(DVE, lambda: DVE.h.tensor_copy(qm.t[:, :, cs], QT.t[:, :, Tp + b * ST:Tp + (b + 1) * ST]), R=QT.d, W=qm.d)
                        self.mmg(pss.d[0], [(pss.t[:, 0:256], qm.t[:, dc, :], kb.t[:, dc, :], (b == 0 and dc == 0), (b == NSEQ - 1 and dc == 3)) for dc in range(4)],
                                 R=qm.d + kb.d)
                        self.op(DVE, lambda: DVE.h.memset(qm.t[:, :, cs], 0.0), W=qm.d)
                    self.softmax_pt(pss, PT, Tp, sm, sidx); sidx += 1
                    pso = [self.ps() for _ in range(4)]
                    for b in range(NSEQ):
                        vbt = self.VB[(b // GB) % 2]
                        if b % GB == 0:
                            self.dma(PQ, self.SW, vbt.t[:], self.v_d[l, b:b + GB, hd].rearrange("b p t d -> p b t d"), W=vbt.d)
                        vb = Tl(vbt.t[:, b % GB], 0); vb.d = vbt.d
                        for dc in range(4):
                            self.mmg(pso[dc].d[0], [(pso[dc].t[:, b * ST:(b + 1) * ST], vb.t[:, mt, dc * 128:(dc + 1) * 128],
                                                     PT.t[:, mt, Tp + b * ST:Tp + (b + 1) * ST], mt == 0, mt == 1) for mt in range(2)],
                                     R=vb.d + PT.d)
                    for dc in range(4):
                        c = hd * 4 + dc
                        self.op(ACT, lambda: ACT.h.copy(YB.t[:, c, Tp:T], pso[dc].t[:, 0:128]), R=pso[dc].d, W=[YB.d[c]])
            for s in range(4 if XS >= 7 else 0):
                self.linear(("wo", l, s), 16, YB, self.acc_epi(s * 4, True))
            self.barrier()

    def pool(self, o):
        ACT, DVE, SP = self.ACT, self.DVE, self.SP
        T, Tp, Ts = self.T, self.Tp, self.Ts
        with ExitStack() as st:
            YB = self.tile("plY", (128, 16, T), BF16, 16, stack=st)
            FP = [self.tile(f"plF{j}", (128, 15 + max(Tp, 1)), F32, stack=st) for j in range(2)]
            SA = [self.tile(f"plA{j}", (128, 15 + max(Tp, 1)), F32, stack=st) for j in range(2)]
            SB_ = [self.tile(f"plB{j}", (128, 15 + max(Tp, 1)), F32, stack=st) for j in range(2)]
            FS = [self.tile(f"plFS{j}", (128, NSEQ, 23), F32, stack=st) for j in range(2)]
            SAs = [self.tile(f"plAs{j}", (128, NSEQ, 23), F32, stack=st) for j in range(2)]
            SBs = [self.tile(f"plBs{j}", (128, NSEQ, 23), F32, stack=st) for j in range(2)]
            tm = [self.tile(f"pltm{j}", (128, 512), F32, stack=st) for j in range(2)]
            PSI = [self.tile(f"plSI{j}", (128, NSEQ, 15), F32, stack=st) for j in range(2)]
            PSO_ = [self.tile(f"plSO{j}", (128, NSEQ, 15), F32, stack=st) for j in range(2)]
            for c in range(16):
                gi = c // 4
                w = 2 << gi
                nst = gi + 1
                k = c % 2
                if Tp:
                    f = FP[k]; a = SA[k]; b = SB_[k]
                    self.op(ACT, lambda: ACT.h.copy(f.t[:, 0:15], self.PH[o].t[:, c, :]), R=[self.PH[o].d[c]], W=f.d)
                    self.op(ACT, lambda: ACT.h.copy(f.t[:, 15:15 + Tp], self.XR.t[:, c, 0:Tp]), R=[self.XR.d[c]], W=f.d)
                    src = f; lo = 0
                    for s_ in range(nst):
                        sh = 1 << s_
                        dst = a if s_ % 2 == 0 else b
                        nlo = lo + sh
                        self.op(DVE, lambda: DVE.h.tensor_tensor(out=dst.t[:, nlo:15 + Tp], in0=src.t[:, nlo:15 + Tp], in1=src.t[:, nlo - sh:15 + Tp - sh], op=ALU.add),
                                R=src.d, W=dst.d)
                        src = dst; lo = nlo
                    self.op(DVE, lambda: DVE.h.scalar_tensor_tensor(out=YB.t[:, c, 0:Tp], in0=src.t[:, 15:15 + Tp], scalar=1.0 / w, in1=f.t[:, 15:15 + Tp],
                                                                    op0=ALU.mult, op1=ALU.subtract), R=src.d + f.d, W=[YB.d[c]])
                    if self.first:
                        t_ = tm[k]
                        self.op(DVE, lambda: DVE.h.tensor_tensor(out=t_.t[:, 0:16], in0=src.t[:, 15:31], in1=self.CON.t[:, C_ICNT + gi * 16:C_ICNT + gi * 16 + 16], op=ALU.mult),
                                R=src.d + self.CON.d, W=t_.d)
                        self.op(DVE, lambda: DVE.h.tensor_tensor(out=YB.t[:, c, 0:16], in0=t_.t[:, 0:16], in1=f.t[:, 15:31], op=ALU.subtract),
                                R=t_.d + f.d, W=[YB.d[c]])
                    self.op(ACT, lambda: ACT.h.copy(self.PH[o].t[:, c, :], f.t[:, Tp:Tp + 15]), R=f.d, W=[self.PH[o].d[c]])
                    if self.last:
                        self.dma(self.PQ, self.SO, self.pbT_p[o, :, c, :], self.PH[o].t[:, c, :], R=[self.PH[o].d[c]])
                if Ts:
                    f = FS[k]; a = SAs[k]; b = SBs[k]
                    self.dma(self.PQ, self.SL, PSI[k].t[:], self.poolT_d[o, :, c, :, :], W=PSI[k].d)
                    self.op(ACT, lambda: ACT.h.copy(f.t[:, :, 0:15], PSI[k].t[:]), R=PSI[k].d, W=f.d)
                    self.op(ACT, lambda: ACT.h.copy(f.t[:, :, 15:23], self.XR.t[:, c, Tp:T].rearrange("p (b t) -> p b t", t=ST)), R=[self.XR.d[c]], W=f.d)
                    src = f; lo = 0
                    for s_ in range(nst):
                        sh = 1 << s_
                        dst = a if s_ % 2 == 0 else b
                        nlo = lo + sh
                        self.op(DVE, lambda: DVE.h.tensor_tensor(out=dst.t[:, :, nlo:23], in0=src.t[:, :, nlo:23], in1=src.t[:, :, nlo - sh:23 - sh], op=ALU.add),
                                R=src.d, W=dst.d)
                        src = dst; lo = nlo
                    self.op(DVE, lambda: DVE.h.scalar_tensor_tensor(out=YB.t[:, c, Tp:T].rearrange("p (b t) -> p b t", t=ST), in0=src.t[:, :, 15:23], scalar=1.0 / w,
                                                                    in1=f.t[:, :, 15:23], op0=ALU.mult, op1=ALU.subtract), R=src.d + f.d, W=[YB.d[c]])
                    self.op(ACT, lambda: ACT.h.copy(PSO_[k].t[:], f.t[:, :, 8:23]), R=f.d, W=PSO_[k].d)
                    self.dma(self.PQ, self.SO, self.pbT_s[o, :, c, :, :], PSO_[k].t[:], R=PSO_[k].d)
            for gi in range(4):
                def epi(nci, s0, sl, ps, gi=gi):
                    c = gi * 4 + nci
                    t_ = tm[nci % 2]
                    sc = self.VEC.t[:, OFF_PSC + o * 16 + c:OFF_PSC + o * 16 + c + 1]
                    self.op(ACT, lambda: ACT.h.activation(out=t_.t[:, 0:sl], in_=ps.t[:, 0:sl], func=AF.Copy, scale=sc), R=ps.d + self.VEC.d, W=t_.d)
                    xr = self.XR.t[:, c, s0:s0 + sl]
                    self.op(DVE, lambda: DVE.h.scalar_tensor_tensor(out=xr, in0=xr, scalar=ALPHA, in1=t_.t[:, 0:sl], op0=ALU.mult, op1=ALU.add),
                            R=t_.d + [self.XR.d[c]], W=[self.XR.d[c]])
                Xv = Tl(YB.t[:, gi * 4:(gi + 1) * 4, :], 0)
                Xv.d = YB.d[gi * 4:(gi + 1) * 4]
                self.linear(("w_pool", o, gi), 4, Xv, epi)
            self.barrier()

    def even_sc(self, e):
        ACT, DVE, SP = self.ACT, self.DVE, self.SP
        T, Tp, Ts = self.T, self.Tp, self.Ts
        with ExitStack() as st:
            YB = self.tile("scY", (128, 16, T), BF16, 16, stack=st)
            U = self.tile("scU", (128, 4, 2 + max(Tp, 1)), F32, 4, stack=st)
            US = self.tile("scUS", (128, 4, NSEQ, 2 + ST), F32, 4, stack=st)
            V = self.tile("scV", (128, 512), F32, stack=st)
            VS = self.tile("scVS", (128, NSEQ, ST), F32, stack=st)
            SSI = self.tile("scSI", (128, 4, NSEQ, 2), F32, stack=st)
            SSO = self.tile("scSO", (128, 4, NSEQ, 2), F32, stack=st)
            for j in range(4):
                def uview(nci, s0, sl):
                    if s0 < Tp:
                        return U.t[:, nci, 2 + s0:2 + s0 + sl]
                    return US.t[:, nci, :, 2:2 + ST]

                def pview(ps, s0, sl):
                    if s0 < Tp:
                        return ps.t[:, 0:sl]
                    return ps.t[:, 0:128].rearrange("p (b t) -> p b t", t=ST)

                def epi_h(nci, s0, sl, ps):
                    self.op(ACT, lambda: ACT.h.copy(uview(nci, s0, sl), pview(ps, s0, sl)), R=ps.d, W=[U.d[nci]])

                def epi_g(nci, s0, sl, ps):
                    uv = uview(nci, s0, sl)
                    self.op(DVE, lambda: DVE.h.tensor_tensor(out=uv, in0=uv, in1=pview(ps, s0, sl), op=ALU.mult), R=ps.d + [U.d[nci]], W=[U.d[nci]])

                def epi_b(nci, s0, sl, ps, j=j):
                    c = j * 4 + nci
                    wv = [self.VEC.t[:, OFF_SCW + (e * 16 + c) * 3 + k:OFF_SCW + (e * 16 + c) * 3 + k + 1] for k in range(3)]
                    if s0 < Tp:
                        v = V.t[:, 0:sl]
                        ins = [U.t[:, nci, s0 + k:s0 + k + sl] for k in range(3)]
                        yv = YB.t[:, c, s0:s0 + sl]
                        vd = V.d
                    else:
                        v = VS.t[:]
                        ins = [US.t[:, nci, :, k:k + ST] for k in range(3)]
                        yv = YB.t[:, c, Tp:T].rearrange("p (b t) -> p b t", t=ST)
                        vd = VS.d
                    self.op(DVE, lambda: DVE.h.tensor_scalar(out=v, in0=ins[0], scalar1=wv[0], scalar2=None, op0=ALU.mult), R=[U.d[nci]] + self.VEC.d, W=vd)
                    self.op(DVE, lambda: DVE.h.scalar_tensor_tensor(out=v, in0=ins[1], scalar=wv[1], in1=v, op0=ALU.mult, op1=ALU.add), R=[U.d[nci]] + vd, W=vd)
                    self.op(DVE, lambda: DVE.h.scalar_tensor_tensor(out=v, in0=ins[2], scalar=wv[2], in1=v, op0=ALU.mult, op1=ALU.add), R=[U.d[nci]] + vd, W=vd)
                    self.op(DVE, lambda: DVE.h.tensor_tensor(out=yv, in0=v, in1=pview(ps, s0, sl), op=ALU.mult), R=ps.d + vd, W=[YB.d[c]])
                for nci in range(4):
                    c = j * 4 + nci
                    if Tp:
                        self.op(ACT, lambda: ACT.h.copy(U.t[:, nci, 0:2], self.SCH[e].t[:, c, :]), R=[self.SCH[e].d[c]], W=[U.d[nci]])
                if Ts:
                    self.dma(self.PQ, self.SL, SSI.t[:], self.scT_d[e, :, j * 4:(j + 1) * 4, :, :], W=SSI.d)
                    for nci in range(4):
                        self.op(ACT, lambda: ACT.h.copy(US.t[:, nci, :, 0:2], SSI.t[:, nci]), R=SSI.d, W=[U.d[nci]])
                self.linear(("w_in", e, 12 + j * 3 + 0), 16, self.XB, epi_h, segs=self.segs)
                self.linear(("w_in", e, 12 + j * 3 + 1), 16, self.XB, epi_g, segs=self.segs)
                for nci in range(4):
                    c = j * 4 + nci
                    if Tp:
                        self.op(ACT, lambda: ACT.h.copy(self.SCH[e].t[:, c, :], U.t[:, nci, Tp:Tp + 2]), R=[U.d[nci]], W=[self.SCH[e].d[c]])
                        if self.last:
                            self.dma(self.PQ, self.SO, self.sbT_p[e, :, c, :], self.SCH[e].t[:, c, :], R=[self.SCH[e].d[c]])
                    if Ts:
                        self.op(ACT, lambda: ACT.h.copy(SSO.t[:, nci], US.t[:, nci, :, ST:ST + 2]), R=[U.d[nci]], W=SSO.d)
                if Ts:
                    self.dma(self.PQ, self.SO, self.sbT_s[e, :, j * 4:(j + 1) * 4, :, :], SSO.t[:], R=SSO.d)
                self.linear(("w_in", e, 12 + j * 3 + 2), 16, self.XB, epi_b, segs=self.segs)
            for s in range(4):
                self.linear(("w_out", e, 4 + s), 16, YB, self.acc_epi(s * 4, True))
            self.barrier()

    def even_ssd(self, e):
        ACT, DVE, SP, PQ = self.ACT, self.DVE, self.SP, self.PQ
        T, Tp, Ts, npt = self.T, self.Tp, self.Ts, self.npt
        NT = npt + (1 if Ts else 0)
        CON = self.CON
        with ExitStack() as st:
            tl = lambda name, shape, dt, n=1: self.tile(name, shape, dt, n, stack=st)
            YB = tl("sdY", (128, 16, T), BF16, 16)
            DT = tl("sdDT", (128, NT, 32), F32)
            AA = tl("sdA", (128, NT, 32), F32)
            ACS = tl("sdACS", (128, NT, 32), F32)
            NACS = tl("sdNACS", (128, NT, 32), F32)
            EACS = tl("sdEACS", (128, NT, 32), F32)
            DEND = tl("sdDEND", (128, NT, 32), F32)
            CDEC = tl("sdCDEC", (128, NT, 32), F32)
            ZG = tl("sdZG", (128, NT, 256), F32)
            XP = tl("sdXP", (128, 4, 3 + max(Tp, 1)), F32, 4)
            NQ = NSEQ if Ts else 1
            NH = NSEQ // 2 if Ts else 1
            XPS = tl("sdXPS", (128, 4, NQ, 3 + ST), F32, 4)
            XC = tl("sdXC", (128, 3, T), F32, 3)
            BCb = tl("sdBCb", (128, 2, T), BF16, 2)
            XTK = tl("sdXTK", (128, NT, 256), F32)
            BTK = tl("sdBTK", (128, NT, 128), BF16)
            AT = [tl(f"sdAT{j}", (128, 4, 128), F32) for j in range(2)]
            LT = [tl(f"sdLT{j}", (128, 4, 128), F32) for j in range(2)]
            MT = [tl(f"sdMT{j}", (128, 4, 128), BF16) for j in range(2)]
            XDT = [tl(f"sdXDT{j}", (128, 256), BF16) for j in range(2)]
            XDD = [tl(f"sdXDD{j}", (128, 256), BF16) for j in range(2)]
            Y1 = [tl(f"sdY1{j}", (128, 256), F32) for j in range(2)]
            Y2 = [tl(f"sdY2{j}", (128, 256), F32) for j in range(2)]
            SS = [tl(f"sdSS{j}", (128, 1), F32) for j in range(2)]
            HB = tl("sdHB", (128, 256), BF16)
            HTMP = tl("sdHTMP", (128, 256), F32)
            CM = tl("sdCM", (128, NH, 128), BF16)
            BM = tl("sdBM", (128, NH, 128), BF16)
            AEX = tl("sdAEX", (128, 256), F32)
            CDC = tl("sdCDC", (128, 2, NSEQ), F32)
            H0N = tl("sdH0N", (128, NH, 128), F32)
            HNO = tl("sdHNO", (128, NH, 128), F32)
            A_e = self.AH.t[:, e * 32:(e + 1) * 32]
            CSI = tl("sdCSI", (128, 4, NQ, 3), F32)
            CSO = tl("sdCSO", (128, 4, NQ, 3), F32)
            self.cvtmp = tl("sdCVT", (128, max(Tp, 128)), F32)
            wdt = self.load_slab(("w_dt", e), 16, 32)
            for ti in range(NT):
                ps = self.ps()
                self.mmg(ps.d[0], [(ps.t[:, 0:32], self.XB.t[:, k, ti * 128:(ti + 1) * 128], wdt.t[:, k, 0:32], k == 0, k == 15) for k in range(16)],
                         R=wdt.d + self.XB.d)
                d = DT.t[:, ti, :]
                self.op(DVE, lambda: DVE.h.tensor_tensor(out=d, in0=ps.t[:, 0:32], in1=self.ROW.t[:, R_DTB + e * 32:R_DTB + (e + 1) * 32], op=ALU.add),
                        R=ps.d + self.ROW.d, W=DT.d)
                self.op(ACT, lambda: ACT.h.activation(out=d, in_=d, func=AF.Exp), R=DT.d, W=DT.d)
                self.op(ACT, lambda: ACT.h.activation(out=d, in_=d, func=AF.Ln, bias=1.0), R=DT.d, W=DT.d)
                a = AA.t[:, ti, :]
                self.op(DVE, lambda: DVE.h.tensor_tensor(out=a, in0=d, in1=A_e, op=ALU.mult), R=DT.d + self.AH.d, W=AA.d)
                samp = (ti >= npt)
                tri = CON.t[:, (C_TRIS if samp else C_TRI):(C_TRIS if samp else C_TRI) + 128]
                one = CON.t[:, (C_SAME if samp else C_ONES):(C_SAME if samp else C_ONES) + 128]
                ps1 = self.ps()
                self.mmg(ps1.d[0], [(ps1.t[:, 0:32], tri, a, True, True)], R=AA.d + CON.d)
                ps2 = self.ps()
                self.mmg(ps2.d[0], [(ps2.t[:, 0:32], one, a, True, True)], R=AA.d + CON.d)
                self.op(ACT, lambda: ACT.h.copy(ACS.t[:, ti, :], ps1.t[:, 0:32]), R=ps1.d, W=ACS.d)
                self.op(ACT, lambda: ACT.h.mul(NACS.t[:, ti, :], ps1.t[:, 0:32], -1.0), R=ps1.d, W=NACS.d)
                self.op(ACT, lambda: ACT.h.activation(out=EACS.t[:, ti, :], in_=ps1.t[:, 0:32], func=AF.Exp), R=ps1.d, W=EACS.d)
                self.op(ACT, lambda: ACT.h.activation(out=CDEC.t[:, ti, :], in_=ps2.t[:, 0:32], func=AF.Exp), R=ps2.d, W=CDEC.d)
                self.op(DVE, lambda: DVE.h.tensor_tensor(out=DEND.t[:, ti, :], in0=ps2.t[:, 0:32], in1=ACS.t[:, ti, :], op=ALU.subtract), R=ps2.d + ACS.d, W=DEND.d)
                self.op(ACT, lambda: ACT.h.activation(out=DEND.t[:, ti, :], in_=DEND.t[:, ti, :], func=AF.Exp), R=DEND.d, W=DEND.d)
            it = 0
            for g in range(8):
                wa = self.load_slab(("w_in", e, g), 16, 512)
                wb = None
                if g % 2 == 0:
                    wbc = self.load_slab(("w_in", e, 8 + g // 2), 16, 512)
                    self.wbc = wbc
                wbc = self.wbc
                bo = (g % 2) * 256
                for ti in range(NT):
                    ps = self.ps()
                    self.mmg(ps.d[0], [(ps.t[:, 0:256], self.XB.t[:, k, ti * 128:(ti + 1) * 128], wa.t[:, k, 0:256], k == 0, k == 15) for k in range(16)],
                             R=wa.d + self.XB.d)
                    self.op(ACT, lambda: ACT.h.activation(out=ZG.t[:, ti, :], in_=ps.t[:, 0:256], func=AF.Silu), R=ps.d, W=ZG.d)
                cids = [2 * g, 2 * g + 1, 16 + g, 24 + g]
                for j in range(4):
                    c32 = cids[j]
                    if Tp:
                        self.op(ACT, lambda: ACT.h.copy(XP.t[:, j, 0:3], self.CVH[e].t[:, c32, :]), R=[self.CVH[e].d[c32]], W=[XP.d[j]])
                    if Ts:
                        if j == 0:
                            self.dma(self.PQ, self.SL, CSI.t[:], self.convT_d[e, g], W=CSI.d)
                        self.op(ACT, lambda: ACT.h.copy(XPS.t[:, j, :, 0:3], CSI.t[:, j]), R=CSI.d, W=[XPS.d[j]])
                for j in range(4):
                    wsrc, col = (wa, 256 + j * 128) if j < 2 else (wbc, bo + (j - 2) * 128)
                    for (s0, sl) in self.segs:
                        ps = self.ps()
                        self.mmg(ps.d[0], [(ps.t[:, 0:sl], wsrc.t[:, k, col:col + 128], self.XB.t[:, k, s0:s0 + sl], k == 0, k == 15) for k in range(16)],
                                 R=wsrc.d + self.XB.d)
                        if s0 < Tp:
                            self.op(ACT, lambda: ACT.h.copy(XP.t[:, j, 3 + s0:3 + s0 + sl], ps.t[:, 0:sl]), R=ps.d, W=[XP.d[j]])
                        else:
                            self.op(ACT, lambda: ACT.h.copy(XPS.t[:, j, :, 3:3 + ST], ps.t[:, 0:128].rearrange("p (b t) -> p b t", t=ST)), R=ps.d, W=[XPS.d[j]])
                for j in range(4):
                    c32 = cids[j]
                    wv = [self.VEC.t[:, OFF_CVW + (e * 32 + c32) * 4 + k:OFF_CVW + (e * 32 + c32) * 4 + k + 1] for k in range(4)]
                    bv = self.VEC.t[:, OFF_CVB + e * 32 + c32:OFF_CVB + e * 32 + c32 + 1]
                    for part in range(2):
                        if (part == 0 and not Tp) or (part == 1 and not Ts):
                            continue
                        if part == 0:
                            ins = [XP.t[:, j, k:k + Tp] for k in range(4)]
                            outs_ = (XC.t[:, j, 0:Tp] if j < 3 else None)
                            bview = (BCb.t[:, j - 2, 0:Tp] if j >= 2 else None)
                            tmpv = self.cvtmp.t[:, 0:Tp]
                            rd = [XP.d[j]]
                        else:
                            ins = [XPS.t[:, j, :, k:k + ST] for k in range(4)]
                            outs_ = (XC.t[:, j, Tp:T].rearrange("p (b t) -> p b t", t=ST) if j < 3 else None)
                            bview = (BCb.t[:, j - 2, Tp:T].rearrange("p (b t) -> p b t", t=ST) if j >= 2 else None)
                            tmpv = self.cvtmp.t[:, 0:128].rearrange("p (b t) -> p b t", t=ST)
                            rd = [XPS.d[j]]
                        td = self.cvtmp.d
                        self.op(DVE, lambda: DVE.h.tensor_scalar(out=tmpv, in0=ins[0], scalar1=wv[0], scalar2=None, op0=ALU.mult), R=rd + self.VEC.d, W=td)
                        for k in range(1, 4):
                            self.op(DVE, lambda k=k: DVE.h.scalar_tensor_tensor(out=tmpv, in0=ins[k], scalar=wv[k], in1=tmpv, op0=ALU.mult, op1=ALU.add), R=rd + td, W=td)
                        if outs_ is not None:
                            self.op(ACT, lambda: ACT.h.activation(out=outs_, in_=tmpv, func=AF.Silu, bias=bv), R=td + self.VEC.d, W=[XC.d[j]])
                        if bview is not None:
                            self.op(ACT, lambda: ACT.h.activation(out=bview, in_=tmpv, func=AF.Silu, bias=bv), R=td + self.VEC.d, W=[BCb.d[j - 2]])
                    if Tp:
                        self.op(ACT, lambda: ACT.h.copy(self.CVH[e].t[:, c32, :], XP.t[:, j, Tp:Tp + 3]), R=[XP.d[j]], W=[self.CVH[e].d[c32]])
                        if self.last:
                            self.dma(self.PQ, self.SO, self.cbT_p[e, :, c32, :], self.CVH[e].t[:, c32, :], R=[self.CVH[e].d[c32]])
                    if Ts:
                        self.op(ACT, lambda: ACT.h.copy(CSO.t[:, j], XPS.t[:, j, :, ST:ST + 3]), R=[XPS.d[j]], W=CSO.d)
                        if j == 3:
                            self.dma(self.PQ, self.SO, self.cbT_s[e, g], CSO.t[:], R=CSO.d)
                for ti in range(NT):
                    for j in range(3):
                        pst = self.ps()
                        self.tr(pst, 128, XC.t[:, j, ti * 128:(ti + 1) * 128], R=[XC.d[j]])
                        if j < 2:
                            self.op(ACT, lambda: ACT.h.copy(XTK.t[:, ti, j * 128:(j + 1) * 128], pst.t[:, 0:128]), R=pst.d, W=XTK.d)
                        else:
                            self.op(ACT, lambda: ACT.h.copy(BTK.t[:, ti, :], pst.t[:, 0:128]), R=pst.d, W=BTK.d)
                hsd = [self.HS[e].d[g]]
                hst = self.HS[e].t[:, g, :]
                for ti in range(NT):
                    samp = ti >= npt
                    k2 = it % 2; it += 1
                    cols = slice(ti * 128, (ti + 1) * 128)
                    hsl = slice(4 * g, 4 * g + 4)
                    tri = CON.t[:, (C_TRIS if samp else C_TRI):(C_TRIS if samp else C_TRI) + 128]
                    mneg = CON.t[:, (C_MNEGS if samp else C_MNEG):(C_MNEGS if samp else C_MNEG) + 128]
                    pcb = self.ps()
                    self.mmg(pcb.d[0], [(pcb.t[:, 0:128], BCb.t[:, 0, cols], BCb.t[:, 1, cols], True, True)], R=BCb.d)
                    at = AT[k2]
                    self.op(DVE, lambda: DVE.h.tensor_tensor(out=at.t[:], in0=tri.unsqueeze(1).to_broadcast([128, 4, 128]),
                                                             in1=AA.t[:, ti, hsl].unsqueeze(2).to_broadcast([128, 4, 128]), op=ALU.mult),
                            R=AA.d + CON.d, W=at.d)
                    pd = self.ps()
                    self.mmg(pd.d[0], [(pd.t[:, :], CON.t[:, C_ONES:C_ONES + 128], at.t[:].rearrange("p h l -> p (h l)"), True, False),
                                       (pd.t[:, :].rearrange("p (h l) -> p h l", l=128), CON.t[:, C_ID:C_ID + 128], mneg.unsqueeze(1).to_broadcast([128, 4, 128]), False, True)],
                             R=at.d + CON.d)
                    lt = LT[k2]
                    for h in range(4):
                        self.op(ACT, lambda h=h: ACT.h.activation(out=lt.t[:, h, :], in_=pd.t[:, h * 128:(h + 1) * 128], func=AF.Exp,
                                                                 bias=NACS.t[:, ti, 4 * g + h:4 * g + h + 1]), R=pd.d + NACS.d, W=lt.d)
                    mt = MT[k2]
                    self.op(DVE, lambda: DVE.h.tensor_tensor(out=mt.t[:], in0=lt.t[:], in1=pcb.t[:, 0:128].unsqueeze(1).to_broadcast([128, 4, 128]), op=ALU.mult),
                            R=lt.d + pcb.d, W=mt.d)
                    xdt = XDT[k2]; xdd = XDD[k2]; y1 = Y1[k2]; y2 = Y2[k2]
                    xs3 = XTK.t[:, ti, :].rearrange("p (h q) -> p h q", q=64)
                    self.op(DVE, lambda: DVE.h.tensor_tensor(out=y1.t[:].rearrange("p (h q) -> p h q", q=64), in0=xs3,
                                                             in1=DT.t[:, ti, hsl].unsqueeze(2).to_broadcast([128, 4, 64]), op=ALU.mult), R=XTK.d + DT.d, W=y1.d)
                    self.op(ACT, lambda: ACT.h.copy(xdt.t[:], y1.t[:]), R=y1.d, W=xdt.d)
                    self.op(DVE, lambda: DVE.h.tensor_tensor(out=xdd.t[:].rearrange("p (h q) -> p h q", q=64), in0=y1.t[:].rearrange("p (h q) -> p h q", q=64),
                                                             in1=DEND.t[:, ti, hsl].unsqueeze(2).to_broadcast([128, 4, 64]), op=ALU.mult), R=y1.d + DEND.d, W=xdd.d)
                    py = self.ps()
                    self.mmg(py.d[0], [(py.t[:, h * 64:(h + 1) * 64], mt.t[:, h, :], xdt.t[:, h * 64:(h + 1) * 64], True, True) for h in range(4)], R=mt.d + xdt.d)
                    po = self.ps()
                    if not samp:
                        self.op(ACT, lambda: ACT.h.copy(HB.t[:], hst), R=hsd, W=HB.d)
                        self.mmg(po.d[0], [(po.t[:, 0:256], BCb.t[:, 1, cols], HB.t[:], True, True)], R=BCb.d + HB.d)
                    else:
                        h0 = self.H0T[0]
                        self.dma(PQ, self.SW, h0.t[:], self.ssdT_d[e, g], W=h0.d)
                        for sh in range(2):
                            b0 = sh * NH
                            self.op(DVE, lambda: DVE.h.tensor_tensor(out=CM.t[:], in0=BCb.t[:, 1, cols].unsqueeze(1).to_broadcast([128, NH, 128]),
                                                                     in1=self.SMTB.t[:, b0:b0 + NH, :], op=ALU.mult), R=BCb.d + self.SMTB.d, W=CM.d)
                            self.mmg(po.d[0], [(po.t[:, 0:256], CM.t[:, b, :], h0.t[:, b0 + b, :], (sh == 0 and b == 0), (sh == 1 and b == NH - 1)) for b in range(NH)],
                                     R=CM.d + h0.d)
                    self.op(DVE, lambda: DVE.h.tensor_tensor(out=y1.t[:].rearrange("p (h q) -> p h q", q=64), in0=po.t[:, 0:256].rearrange("p (h q) -> p h q", q=64),
                                                             in1=EACS.t[:, ti, hsl].unsqueeze(2).to_broadcast([128, 4, 64]), op=ALU.mult), R=po.d + EACS.d + xdt.d, W=y1.d)
                    self.op(DVE, lambda: DVE.h.tensor_tensor(out=y1.t[:], in0=y1.t[:], in1=py.t[:, 0:256], op=ALU.add), R=y1.d + py.d, W=y1.d)
                    self.op(DVE, lambda: DVE.h.tensor_tensor(out=y2.t[:].rearrange("p (h q) -> p h q", q=64), in0=xs3,
                                                             in1=self.ROW.t[:, R_DSK + e * 32 + 4 * g:R_DSK + e * 32 + 4 * g + 4].unsqueeze(2).to_broadcast([128, 4, 64]), op=ALU.mult),
                            R=XTK.d + self.ROW.d, W=y2.d)
                    self.op(DVE, lambda: DVE.h.tensor_tensor(out=y1.t[:], in0=y1.t[:], in1=y2.t[:], op=ALU.add), R=y1.d + y2.d, W=y1.d)
                    self.op(DVE, lambda: DVE.h.tensor_tensor(out=y1.t[:], in0=y1.t[:], in1=ZG.t[:, ti, :], op=ALU.mult), R=y1.d + ZG.d, W=y1.d)
                    ss = SS[k2]
                    self.op(ACT, lambda: ACT.h.activation(out=y2.t[:], in_=y1.t[:], func=AF.Square, accum_out=ss.t[:]), R=y1.d, W=y2.d + ss.d)
                    self.op(ACT, lambda: ACT.h.activation(out=ss.t[:], in_=ss.t[:], func=AF.Ln, bias=RMS_EPS, scale=1.0 / 256), R=ss.d, W=ss.d)
                    self.op(ACT, lambda: ACT.h.activation(out=ss.t[:], in_=ss.t[:], func=AF.Exp, scale=-0.5), R=ss.d, W=ss.d)
                    self.op(DVE, lambda: DVE.h.tensor_scalar(out=y1.t[:], in0=y1.t[:], scalar1=ss.t[:], scalar2=None, op0=ALU.mult), R=y1.d + ss.d, W=y1.d)
                    for j in range(2):
                        pst = self.ps()
                        self.tr(pst, 128, y1.t[:, j * 128:(j + 1) * 128], R=y1.d)
                        c = 2 * g + j
                        ng = self.VEC.t[:, OFF_NRG + e * 16 + c:OFF_NRG + e * 16 + c + 1]
                        self.op(ACT, lambda: ACT.h.activation(out=YB.t[:, c, cols], in_=pst.t[:, 0:128], func=AF.Copy, scale=ng), R=pst.d + self.VEC.d, W=[YB.d[c]])
                    if not samp:
                        pss = self.ps()
                        self.mmg(pss.d[0], [(pss.t[:, 0:256], BTK.t[:, ti, :], xdd.t[:], True, True)], R=BTK.d + xdd.d)
                        self.op(DVE, lambda: DVE.h.tensor_tensor(out=HTMP.t[:].rearrange("p (h q) -> p h q", q=64), in0=hst.rearrange("p (h q) -> p h q", q=64),
                                                                 in1=CDEC.t[:, ti, hsl].unsqueeze(2).to_broadcast([128, 4, 64]), op=ALU.mult), R=hsd + CDEC.d, W=HTMP.d)
                        self.op(DVE, lambda: DVE.h.tensor_tensor(out=hst, in0=HTMP.t[:], in1=pss.t[:, 0:256], op=ALU.add), R=HTMP.d + pss.d + HB.d, W=hsd)
                    else:
                        self.op(DVE, lambda: DVE.h.tensor_copy(AEX.t[:].rearrange("p (h q) -> p h q", q=64), AA.t[:, ti, hsl].unsqueeze(2).to_broadcast([128, 4, 64])),
                                R=AA.d, W=AEX.d)
                        for hf in range(2):
                            pc = self.ps()
                            self.mmg(pc.d[0], [(pc.t[:, 0:NSEQ], AEX.t[:, hf * 128:(hf + 1) * 128], CON.t[:, C_SIND:C_SIND + NSEQ], True, True)], R=AEX.d + CON.d)
                            self.op(ACT, lambda: ACT.h.activation(out=CDC.t[:, hf, :], in_=pc.t[:, 0:NSEQ], func=AF.Exp), R=pc.d, W=CDC.d)
                        for sh in range(2):
                            b0 = sh * NH
                            self.op(DVE, lambda: DVE.h.tensor_tensor(out=BM.t[:], in0=BTK.t[:, ti, :].unsqueeze(1).to_broadcast([128, NH, 128]),
                                                                     in1=CON.t[:, C_SIND + b0:C_SIND + b0 + NH].unsqueeze(2).to_broadcast([128, NH, 128]), op=ALU.mult),
                                    R=BTK.d + CON.d, W=BM.d)
                            for hf in range(2):
                                h0n = H0N; hno = HNO
                                self.dma(self.PQ, self.SL, h0n.t[:], self.ssdN_d[e, g, hf][:, b0:b0 + NH, :], W=h0n.d)
                                for b in range(NH):
                                    pn = self.ps()
                                    self.mmg(pn.d[0], [(pn.t[:, 0:128], xdd.t[:, hf * 128:(hf + 1) * 128], BM.t[:, b, :], True, True)], R=xdd.d + BM.d)
                                    self.op(DVE, lambda: DVE.h.scalar_tensor_tensor(out=hno.t[:, b, :], in0=h0n.t[:, b, :], scalar=CDC.t[:, hf, b0 + b:b0 + b + 1], in1=pn.t[:, 0:128],
                                                                                    op0=ALU.mult, op1=ALU.add), R=h0n.d + CDC.d + pn.d, W=hno.d)
                                self.dma(self.PQ, self.SO, self.hN_s[e, g, hf][:, b0:b0 + NH, :], hno.t[:], R=hno.d)
                if self.last and Tp:
                    self.dma(self.PQ, self.SO, self.hT_p[e, :, g, :], hst, R=hsd)
            for s in range(4):
                self.linear(("w_out", e, s), 16, YB, self.acc_epi(s * 4, False))
            self.barrier()


def _slabs(W, ncols=512):
    K, N = W.shape
    kc = K // 128
    ns = N // ncols
    return np.ascontiguousarray(W.reshape(kc, 128, ns, ncols).transpose(2, 1, 0, 3))


def _fm(v, nch):
    sh = v.shape[:-1]
    a = v.reshape(*sh, nch, 128)
    return np.moveaxis(a, -1, 0)


def _consts():
    c = np.zeros((128, NCON), np.float32)
    idx = np.arange(128)
    c[:, C_ID:C_ID + 128] = np.eye(128)
    tri = (idx[:, None] <= idx[None, :]).astype(np.float32)
    same = (idx[:, None] // ST == idx[None, :] // ST).astype(np.float32)
    c[:, C_TRI:C_TRI + 128] = tri
    c[:, C_TRIS:C_TRIS + 128] = tri * same
    c[:, C_MNEG:C_MNEG + 128] = np.where(tri > 0, 0.0, -30000.0)
    c[:, C_MNEGS:C_MNEGS + 128] = np.where(tri * same > 0, 0.0, -30000.0)
    c[:, C_ONES:C_ONES + 128] = 1.0
    c[:, C_SAME:C_SAME + 128] = same
    sind = (idx[:, None] // ST == np.arange(NSEQ)[None, :]).astype(np.float32)
    c[:, C_SIND:C_SIND + NSEQ] = sind
    for gi in range(4):
        w = 2 << gi
        c[:, C_ICNT + gi * 16:C_ICNT + gi * 16 + 16] = 1.0 / np.minimum(np.arange(16) + 1, w)
    c[:, C_SMT:C_SMT + 16 * 128] = np.broadcast_to(sind.T.reshape(1, 16 * 128), (128, 16 * 128))
    return c


def prepare_shared(inp):
    sh = {}
    w_in = inp["w_in_even"]
    slabs = []
    for e in range(2):
        W = w_in[e]
        z = W[:, 0:2048]; xbc = W[:, 2048:6144]; gb = W[:, 6176:8224]; gc = W[:, 8224:10272]; hs_ = W[:, 10272:12320]
        cols = []
        for g in range(8):
            cols.append(np.concatenate([z[:, g * 256:(g + 1) * 256], xbc[:, g * 256:(g + 1) * 256]], 1))
        for j in range(4):
            g0, g1 = 2 * j, 2 * j + 1
            cols.append(np.concatenate([xbc[:, 2048 + g0 * 128:2048 + (g0 + 1) * 128], xbc[:, 3072 + g0 * 128:3072 + (g0 + 1) * 128],
                                        xbc[:, 2048 + g1 * 128:2048 + (g1 + 1) * 128], xbc[:, 3072 + g1 * 128:3072 + (g1 + 1) * 128]], 1))
        for j in range(4):
            cols.append(hs_[:, j * 512:(j + 1) * 512]); cols.append(gc[:, j * 512:(j + 1) * 512]); cols.append(gb[:, j * 512:(j + 1) * 512])
        slabs.append(np.stack([_slabs(cm)[0] for cm in cols]))
    sh["w_in_s"] = np.stack(slabs)
    sh["w_dt_s"] = np.stack([_slabs(w_in[e][:, 6144:6176], 32)[0] for e in range(2)])
    wo = inp["w_out_even"]
    sh["w_out_s"] = np.stack([np.concatenate([_slabs(wo[e][0:2048]), _slabs(wo[e][2048:4096])]) for e in range(2)])
    sh["w_pool_s"] = np.stack([np.stack([_slabs(inp["w_pool"][o, g])[0] for g in range(4)]) for o in range(2)])
    for nm, key in (("wq_s", "wq_x"), ("wk_s", "wk_x"), ("wv_s", "wv_x"), ("wo_s", "wo_x")):
        sh[nm] = np.stack([_slabs(inp[key][l]) for l in range(4)])
    sh["w_up_s"] = np.stack([_slabs(inp["w_up"][l]) for l in range(4)])
    sh["w_down_s"] = np.stack([np.concatenate([_slabs(inp["w_down"][l][q * 2048:(q + 1) * 2048]) for q in range(4)]) for l in range(4)])
    vec = np.zeros((128, NVEC), np.float32)
    vec[:, OFF_LNG:OFF_LNG + 192] = _fm(inp["ln_g"], 16).reshape(128, 192)
    vec[:, OFF_LNB:OFF_LNB + 192] = _fm(inp["ln_b"], 16).reshape(128, 192)
    vec[:, OFF_PSC:OFF_PSC + 32] = _fm(inp["pool_scale"], 16).reshape(128, 32)
    vec[:, OFF_SCW:OFF_SCW + 96] = _fm(inp["sc_conv_w"], 16).transpose(0, 1, 3, 2).reshape(128, 96)
    vec[:, OFF_CVW:OFF_CVW + 256] = _fm(inp["ssd_conv_w"], 32).transpose(0, 1, 3, 2).reshape(128, 256)
    vec[:, OFF_CVB:OFF_CVB + 64] = _fm(inp["ssd_conv_b"], 32).reshape(128, 64)
    vec[:, OFF_NRG:OFF_NRG + 32] = _fm(inp["ssd_norm_g"], 16).reshape(128, 32)
    sh["vecs"] = vec
    rows = np.zeros((128, NROW), np.float32)
    rows[:, R_DTB:R_DTB + 64] = inp["ssd_dt_bias"].reshape(1, 64)
    rows[:, R_ALOG:R_ALOG + 64] = inp["ssd_a_log"].reshape(1, 64)
    rows[:, R_DSK:R_DSK + 64] = inp["ssd_d"].reshape(1, 64)
    sh["rows"] = rows
    sh["consts"] = _consts()
    return sh


def prepare_core(inp, seq, sb0, TPp):
    m = {}
    xp = inp["x_prompt"][seq, :TPp]
    xs = inp["x_sample"][sb0:sb0 + NSEQ].reshape(NSEQ * ST, D)
    xa = np.concatenate([xp, xs], 0)
    m["xT"] = np.ascontiguousarray(xa.T.reshape(16, 128, -1).transpose(1, 0, 2))
    m["memT"] = np.ascontiguousarray(inp["mem_prompt"][seq].T.reshape(16, 128, 256).transpose(1, 0, 2))
    ck = inp["cache_mem_k"][:, sb0:sb0 + NSEQ]
    m["kT_s"] = np.ascontiguousarray(ck.reshape(4, NSEQ, 256, 4, 4, 128).transpose(0, 1, 3, 5, 4, 2))
    cv = inp["cache_mem_v"][:, sb0:sb0 + NSEQ]
    m["v_s"] = np.ascontiguousarray(cv.reshape(4, NSEQ, 2, 128, 4, 512).transpose(0, 1, 4, 3, 2, 5))
    s = inp["state_ssd"][:, sb0:sb0 + NSEQ]
    m["ssdT_s"] = np.ascontiguousarray(s.reshape(2, NSEQ, 8, 256, 128).transpose(0, 2, 4, 1, 3))
    m["ssdN_s"] = np.ascontiguousarray(s.reshape(2, NSEQ, 8, 2, 128, 128).transpose(0, 2, 3, 4, 1, 5))
    cs = inp["state_ssd_conv"][:, sb0:sb0 + NSEQ]
    cT = cs.reshape(2, NSEQ, 3, 32, 128).transpose(0, 4, 3, 1, 2)
    m["convT_s"] = np.ascontiguousarray(np.stack([cT[:, :, _cids(g)] for g in range(8)], 1))
    ss = inp["state_short_conv"][:, sb0:sb0 + NSEQ]
    m["scT_s"] = np.ascontiguousarray(ss.reshape(2, NSEQ, 2, 16, 128).transpose(0, 4, 3, 1, 2))
    sp = inp["state_pool"][:, sb0:sb0 + NSEQ]
    m["poolT_s"] = np.ascontiguousarray(sp.reshape(2, NSEQ, 15, 16, 128).transpose(0, 4, 3, 1, 2))
    return m


def _cids(g):
    return [2 * g, 2 * g + 1, 16 + g, 24 + g]


def _cb_s(a):
    full = np.zeros((2, 128, 32, NSEQ, 3), np.float32)
    for g in range(8):
        full[:, :, _cids(g)] = a[:, g]
    return full.transpose(0, 3, 4, 2, 1).reshape(2, NSEQ, 3, 4096)


def unT(a):
    return a.transpose(2, 1, 0).reshape(a.shape[2], -1)


BLOCKS = [(4, False), (3, False), (3, False), (3, False), (3, True)]
_prog = {}


def kernel(**inp):
    inp = {k: np.asarray(v) for k, v in inp.items()}
    key = "full"
    if key not in _prog:
        b = Builder(BLOCKS, 4)
        b.cvtmp = None
        _prog[key] = (b, b.build())
    b, nc = _prog[key]
    sh = prepare_shared(inp)
    in_maps = []
    for core in range(8):
        m = dict(sh)
        m.update(prepare_core(inp, core % 4, core * NSEQ, 2048))
        in_maps.append(m)
    res = run_bass_kernel_spmd(nc, in_maps, core_ids=list(range(8))).results
    return assemble(res, 2048)


def assemble(res, TPp):
    nb = len(res)
    npr = min(4, nb)
    y_prompt = np.stack([unT(res[c]["yT"][:, :, :TPp]) for c in range(npr)])
    if res[0]["yT"].shape[2] > TPp:
        y_sample = np.concatenate([unT(res[c]["yT"][:, :, TPp:]).reshape(NSEQ, ST, D) for c in range(nb)])
    else:
        y_sample = np.zeros((nb * NSEQ, ST, D), np.float32)
    h_p = np.stack([res[c]["hT_p"].reshape(2, 128, 32, 64).transpose(0, 2, 3, 1) for c in range(npr)], 1)
    h_s = np.concatenate([res[c]["hN_s"].transpose(0, 4, 1, 2, 3, 5).reshape(2, NSEQ, 32, 64, 128) for c in range(nb)], 1)
    cb_p = np.stack([res[c]["cbT_p"].transpose(0, 3, 2, 1).reshape(2, 3, 4096) for c in range(npr)], 1)
    cb_s = np.concatenate([_cb_s(res[c]["cbT_s"]) for c in range(nb)], 1)
    sb_p = np.stack([res[c]["sbT_p"].transpose(0, 3, 2, 1).reshape(2, 2, 2048) for c in range(npr)], 1)
    sb_s = np.concatenate([res[c]["sbT_s"].transpose(0, 3, 4, 2, 1).reshape(2, NSEQ, 2, 2048) for c in range(nb)], 1)
    pb_p = np.stack([res[c]["pbT_p"].transpose(0, 3, 2, 1).reshape(2, 15, 2048) for c in range(npr)], 1)
    pb_s = np.concatenate([res[c]["pbT_s"].transpose(0, 3, 4, 2, 1).reshape(2, NSEQ, 15, 2048) for c in range(nb)], 1)
    mk_p = np.stack([res[c]["mk_p"].reshape(4, 256, 4, 512) for c in range(npr)], 1)
    mv_p = np.stack([res[c]["mv_p"].reshape(4, 256, 4, 512) for c in range(npr)], 1)
    outs = (y_prompt, y_sample, h_p, h_s, cb_p, cb_s, sb_p, sb_s, pb_p, pb_s, mk_p, mv_p)
    return tuple(np.ascontiguousarray(o, dtype=np.float32) for o in outs)
```

```python
import numpy as np
import ml_dtypes
from contextlib import ExitStack
import concourse.bass as bass
import concourse.mybir as mybir
from concourse.bass_utils import run_bass_kernel_spmd

F32 = mybir.dt.float32
BF16 = mybir.dt.bfloat16
AF = mybir.ActivationFunctionType
ALU = mybir.AluOpType
AX = mybir.AxisListType

D = 2048
DEPTH = 4
ALPHA = (2.0 * DEPTH) ** 0.25
LN_EPS = 1e-5
RMS_EPS = 1e-5
NSEQ = 16
ST = 8
XSCALE = 512 ** -0.5

OFF_LNG = 0
OFF_LNB = OFF_LNG + 4 * 3 * 16
OFF_PSC = OFF_LNB + 4 * 3 * 16
OFF_SCW = OFF_PSC + 2 * 16
OFF_CVW = OFF_SCW + 2 * 16 * 3
OFF_CVB = OFF_CVW + 2 * 32 * 4
OFF_NRG = OFF_CVB + 2 * 32
NVEC = OFF_NRG + 2 * 16
C_ID = 0; C_TRI = 128; C_TRIS = 256; C_MNEG = 384; C_MNEGS = 512; C_ONES = 640; C_SAME = 768
C_SIND = 896; C_ICNT = 912; C_SMT = 976; NCON = C_SMT + 16 * 128
R_DTB = 0; R_ALOG = 64; R_DSK = 128; NROW = 192


class Dep:
    __slots__ = ("w", "r", "ps")

    def __init__(self):
        self.w = None
        self.r = {}
        self.ps = False


class Eng:
    def __init__(self, h, sem, is_pe=False):
        self.h = h; self.sem = sem; self.cnt = 0; self.seen = {}; self.is_pe = is_pe


class Stream:
    def __init__(self, sems):
        self.sems = sems; self.cnts = [0] * len(sems); self.idx = 0


class Tl:
    def __init__(self, t, n):
        self.t = t
        self.d = [Dep() for _ in range(n)]


class Builder:
    def __init__(self, blocks, n_layers, first_seq_block=True):
        self.blocks = blocks
        self.n_layers = n_layers
        self.TP_total = sum(b[0] for b in blocks) * 128
        assert self.TP_total in (2048,) or len(blocks) <= 2, "prompt tiles must cover the full sequence"
        self.has_sample = any(b[1] for b in blocks)
        self.TC = self.TP_total + (128 if self.has_sample else 0)
        self.TMAX = max(b[0] * 128 + (128 if b[1] else 0) for b in blocks)
        self.nc = bass.Bass("TRN2", target_bir_lowering=False)
        self.es = ExitStack()
        self.ins = {}
        self.outs = {}
        self.nsem = 0

    def sem(self):
        self.nsem += 1
        return self.es.enter_context(self.nc.semaphore(f"s{self.nsem}"))

    def din(self, name, shape):
        self.ins[name] = shape
        return self.nc.dram_tensor(name, list(shape), F32, kind="ExternalInput").ap()

    def dout(self, name, shape):
        self.outs[name] = shape
        return self.nc.dram_tensor(name, list(shape), F32, kind="ExternalOutput").ap()

    def tile(self, name, shape, dt, n=1, stack=None):
        self.ntile = getattr(self, "ntile", 0) + 1
        t = (stack or self.es).enter_context(self.nc.sbuf_tensor(f"{name}_{self.ntile}", list(shape), dt))
        return Tl(t, n)

    def wait_for(self, E, pairs):
        best = {}
        for sem, v in pairs:
            k = sem.num
            if k not in best or best[k][1] < v:
                best[k] = (sem, v)
        for k, (sem, v) in best.items():
            if E.is_pe and sem is E.sem:
                continue
            if E.seen.get(k, 0) >= v:
                continue
            E.h.wait_ge(sem, v)
            E.seen[k] = v

    @staticmethod
    def pairs(R, W):
        for d in R:
            if d.w:
                yield d.w
            if d.ps:
                yield from d.r.values()
        for d in W:
            if d.w:
                yield d.w
            yield from d.r.values()

    def op(self, E, fn, R=(), W=()):
        self.wait_for(E, self.pairs(R, W))
        ins = fn()
        E.cnt += 1
        ins.then_inc(E.sem, 1)
        m = (E.sem, E.cnt)
        for d in W:
            d.w = m; d.r = {}
        for d in R:
            d.r[E.sem.num] = m

    def dma(self, Q, S, out, in_, R=(), W=()):
        j = S.idx % len(S.sems)
        S.idx += 1
        pr = list(self.pairs(R, W))
        if S.cnts[j]:
            pr.append((S.sems[j], S.cnts[j]))
        self.wait_for(Q, pr)
        Q.h.dma_start(out=out, in_=in_).then_inc(S.sems[j], 16)
        S.cnts[j] += 16
        m = (S.sems[j], S.cnts[j])
        for d in W:
            d.w = m; d.r = {}
        for d in R:
            d.r[S.sems[j].num] = m

    def mmg(self, psd, mms, R):
        PE = self.PE
        self.wait_for(PE, self.pairs(R, [psd]))
        ins = None
        for (o, l, r, st, sp) in mms:
            ins = PE.h.matmul(o, l, r, start=st, stop=sp)
        PE.cnt += 1
        ins.then_inc(PE.sem, 1)
        m = (PE.sem, PE.cnt)
        psd.w = m; psd.r = {}
        for d in R:
            d.r[PE.sem.num] = m

    def tr(self, ps, n_out_part, in_ap, R, ncols=128):
        PE = self.PE
        self.wait_for(PE, self.pairs(list(R) + [self.CON.d[0]], [ps.d[0]]))
        ins = PE.h.transpose(ps.t[:, 0:128], in_ap, self.CON.t[:, C_ID:C_ID + 128])
        PE.cnt += 1
        ins.then_inc(PE.sem, 1)
        m = (PE.sem, PE.cnt)
        ps.d[0].w = m; ps.d[0].r = {}
        for d in R:
            d.r[PE.sem.num] = m

    def ps(self):
        p = self.PS[self.psi % 8]
        self.psi += 1
        return p

    def ws(self):
        w = self.WS[self.wsi % len(self.WS)]
        self.wsi += 1
        return w

    def barrier(self):
        engs = [self.PE, self.ACT, self.DVE, self.PQ]
        targets = [(e.sem, e.cnt) for e in (self.PE, self.ACT, self.DVE) if e.cnt > 0]
        for s in (self.SL, self.SO):
            targets += [(sm, c) for sm, c in zip(s.sems, s.cnts) if c > 0]
        for e in engs:
            self.wait_for(e, targets)

    def build(self):
        nc = self.nc
        TC, TMAX = self.TC, self.TMAX
        NL = self.n_layers
        self.xT = self.din("xT", (128, 16, TC))
        self.vecs_d = self.din("vecs", (128, NVEC))
        self.rows_d = self.din("rows", (128, NROW))
        self.con_d = self.din("consts", (128, NCON))
        self.memT_d = self.din("memT", (128, 16, 256))
        self.w_in_d = self.din("w_in_s", (2, 24, 128, 16, 512))
        self.w_dt_d = self.din("w_dt_s", (2, 128, 16, 32))
        self.w_out_d = self.din("w_out_s", (2, 8, 128, 16, 512))
        self.w_pool_d = self.din("w_pool_s", (2, 4, 128, 4, 512))
        self.wq_d = self.din("wq_s", (4, 4, 128, 16, 512))
        self.wk_d = self.din("wk_s", (4, 4, 128, 16, 512))
        self.wv_d = self.din("wv_s", (4, 4, 128, 16, 512))
        self.wo_d = self.din("wo_s", (4, 4, 128, 16, 512))
        self.wup_d = self.din("w_up_s", (4, 16, 128, 16, 512))
        self.wdn_d = self.din("w_down_s", (4, 16, 128, 16, 512))
        self.kT_d = self.din("kT_s", (4, NSEQ, 4, 128, 4, 256))
        self.v_d = self.din("v_s", (4, NSEQ, 4, 128, 2, 512))
        self.ssdT_d = self.din("ssdT_s", (2, 8, 128, NSEQ, 256))
        self.ssdN_d = self.din("ssdN_s", (2, 8, 2, 128, NSEQ, 128))
        self.convT_d = self.din("convT_s", (2, 8, 128, 4, NSEQ, 3))
        self.scT_d = self.din("scT_s", (2, 128, 16, NSEQ, 2))
        self.poolT_d = self.din("poolT_s", (2, 128, 16, NSEQ, 15))
        self.yT = self.dout("yT", (128, 16, TC))
        self.hT_p = self.dout("hT_p", (2, 128, 8, 256))
        self.hN_s = self.dout("hN_s", (2, 8, 2, 128, NSEQ, 128))
        self.cbT_p = self.dout("cbT_p", (2, 128, 32, 3))
        self.cbT_s = self.dout("cbT_s", (2, 8, 128, 4, NSEQ, 3))
        self.sbT_p = self.dout("sbT_p", (2, 128, 16, 2))
        self.sbT_s = self.dout("sbT_s", (2, 128, 16, NSEQ, 2))
        self.pbT_p = self.dout("pbT_p", (2, 128, 16, 15))
        self.pbT_s = self.dout("pbT_s", (2, 128, 16, NSEQ, 15))
        self.mk_p = self.dout("mk_p", (4, 256, 2048))
        self.mv_p = self.dout("mv_p", (4, 256, 2048))
        self.PE = Eng(nc.tensor, self.sem(), True)
        self.ACT = Eng(nc.scalar, self.sem())
        self.DVE = Eng(nc.vector, self.sem())
        self.SP = Eng(nc.sync, self.sem())
        self.PQ = Eng(nc.gpsimd, self.sem())
        self.SL = Stream([self.sem() for _ in range(8)])
        self.SO = Stream([self.sem() for _ in range(8)])
        self.SW = Stream([self.sem() for _ in range(6)])
        self.SWB = Stream([self.sem() for _ in range(4)])
        self.SCS = Stream([self.sem() for _ in range(4)])
        self.wd = {"w_in": self.w_in_d, "w_dt": self.w_dt_d, "w_out": self.w_out_d, "w_pool": self.w_pool_d, "wq": self.wq_d, "wk": self.wk_d,
                   "wv": self.wv_d, "wo": self.wo_d, "w_up": self.wup_d, "w_dn": self.wdn_d}
        self.wc = {k: nc.dram_tensor("c_" + k, list(v.shape), BF16, kind="Internal").ap() for k, v in self.wd.items()}
        self.cdeps = {}
        T = TMAX
        self.XR = self.tile("XR", (128, 16, T), F32, 16)
        self.XB = self.tile("XB", (128, 16, T), BF16, 16)
        self.WS = [self.tile(f"WS{i}", (128, 16, 512), BF16) for i in range(2)]
        self.wsi = 0
        self.CON = self.tile("CON", (128, C_SMT), F32)
        self.VEC = self.tile("VEC", (128, NVEC), F32)
        self.ROW = self.tile("ROW", (128, NROW), F32)
        self.ONESB = self.tile("ONESB", (128, 128), BF16)
        self.SMTB = self.tile("SMTB", (128, NSEQ, 128), BF16)
        self.MEMT = self.tile("MEMT", (128, 16, 256), BF16)
        self.HS = [self.tile(f"HS{e}", (128, 8, 256), F32, 8) for e in range(2)]
        self.H0T = [self.tile(f"H0T{i}", (128, NSEQ, 256), BF16) for i in range(1)]
        self.CVH = [self.tile(f"CVH{e}", (128, 32, 3), F32, 32) for e in range(2)]
        self.SCH = [self.tile(f"SCH{e}", (128, 16, 2), F32, 16) for e in range(2)]
        self.PH = [self.tile(f"PH{o}", (128, 16, 15), F32, 16) for o in range(2)]
        self.AH = self.tile("AH", (128, 64), F32)
        self.PS = []
        for i in range(8):
            t = self.es.enter_context(nc.psum_tensor(f"PS{i}", [128, 512], F32))
            self.PS.append(Tl(t, 1))
            self.PS[-1].d[0].ps = True
        self.psi = 0
        SP, ACT, DVE, PQ = self.SP, self.ACT, self.DVE, self.PQ
        self.dma(self.PQ, self.SL, self.CON.t[:], self.con_d[:, 0:C_SMT], W=self.CON.d)
        self.dma(self.PQ, self.SL, self.VEC.t[:], self.vecs_d, W=self.VEC.d)
        self.dma(self.PQ, self.SL, self.ROW.t[:], self.rows_d, W=self.ROW.d)
        self.dma(PQ, self.SW, self.MEMT.t[:], self.memT_d, W=self.MEMT.d)
        self.op(DVE, lambda: DVE.h.tensor_scalar(out=self.ONESB.t[:], in0=self.CON.t[:, C_ONES:C_ONES + 128], scalar1=1.0 / D,
                                                 scalar2=None, op0=ALU.mult), R=self.CON.d, W=self.ONESB.d)
        with ExitStack() as st0:
            smt = self.tile("SMTF", (128, 16 * 128), F32, stack=st0)
            self.dma(self.PQ, self.SL, smt.t[:], self.con_d[:, C_SMT:C_SMT + 16 * 128], W=smt.d)
            self.op(DVE, lambda: DVE.h.tensor_copy(self.SMTB.t[:], smt.t[:].rearrange("p (b l) -> p b l", l=128)), R=smt.d, W=self.SMTB.d)
            self.DVE.h.wait_ge(self.DVE.sem, self.DVE.cnt)
            self.barrier()
        self.op(ACT, lambda: ACT.h.activation(out=self.AH.t[:], in_=self.ROW.t[:, R_ALOG:R_ALOG + 64], func=AF.Exp), R=self.ROW.d, W=self.AH.d)
        self.op(ACT, lambda: ACT.h.mul(self.AH.t[:], self.AH.t[:], -1.0), R=self.AH.d, W=self.AH.d)
        for tl in self.HS + self.CVH + self.SCH + self.PH:
            self.op(DVE, lambda tl=tl: DVE.h.memset(tl.t[:], 0.0), W=tl.d)
        pcol = 0
        for bi, (npt, hs) in enumerate(self.blocks):
            self.run_block(bi, npt, hs, pcol, first=(bi == 0), last=(bi == len(self.blocks) - 1))
            pcol += npt * 128
        self.barrier()
        for sm, c in zip(self.SO.sems, self.SO.cnts):
            if c:
                self.PQ.h.wait_ge(sm, c)
        return nc

    def run_block(self, bi, npt, hs, pcol, first, last):
        SP, ACT, DVE = self.SP, self.ACT, self.DVE
        Tp = npt * 128
        Ts = 128 if hs else 0
        T = Tp + Ts
        self.Tp, self.Ts, self.T, self.npt, self.hs = Tp, Ts, T, npt, hs
        self.first, self.last = first, last
        segs = []
        o = 0
        while o < Tp:
            l = min(512, Tp - o); segs.append((o, l)); o += l
        if hs:
            segs.append((Tp, 128))
        self.segs = segs
        self.msegs = []
        o = 0
        while o < T:
            l = min(512, T - o); self.msegs.append((o, l)); o += l
        for c in range(16):
            self.dma(self.PQ, self.SL, self.XR.t[:, c, 0:Tp], self.xT[:, c, pcol:pcol + Tp], W=[self.XR.d[c]])
            if hs:
                self.dma(self.PQ, self.SL, self.XR.t[:, c, Tp:T], self.xT[:, c, self.TP_total:self.TP_total + 128], W=[self.XR.d[c]])
            self.op(ACT, lambda c=c: ACT.h.copy(self.XB.t[:, c, 0:T], self.XR.t[:, c, 0:T]), R=[self.XR.d[c]], W=[self.XB.d[c]])
        import os
        stop = int(os.environ.get("KSTOP", "1000"))
        ph = [0]

        def run(f, *a):
            if ph[0] < stop:
                f(*a)
            ph[0] += 1
        for l in range(self.n_layers):
            if l % 2 == 0:
                run(self.even_sc, l // 2)
                run(self.even_ssd, l // 2)
            else:
                run(self.pool, l // 2)
            run(self.ln, l, 0)
            run(self.xattn, l)
            run(self.ln, l, 1)
            run(self.mlp, l)
            run(self.ln, l, 2)
        for c in range(16):
            self.dma(self.PQ, self.SO, self.yT[:, c, pcol:pcol + Tp], self.XR.t[:, c, 0:Tp], R=[self.XR.d[c]])
            if hs:
                self.dma(self.PQ, self.SO, self.yT[:, c, self.TP_total:self.TP_total + 128], self.XR.t[:, c, Tp:T], R=[self.XR.d[c]])
        self.flush_store()
        self.barrier()

    def flush_store(self):
        p = getattr(self, "pending", None)
        if p is not None:
            cch, w, kc, ncols, dep = p
            self.dma(self.PQ, self.SCS, cch, w.t[:, 0:kc, 0:ncols], R=w.d, W=[dep])
            self.pending = None

    def load_slab(self, key, kc, ncols):
        w = self.ws()
        name, idx = key[0], tuple(key[1:])
        src = self.wd[name][idx]
        cch = self.wc[name][idx]
        dep = self.cdeps.setdefault(key, Dep())
        if self.first:
            self.dma(self.PQ, self.SW, w.t[:, 0:kc, 0:ncols], src, W=w.d)
            self.flush_store()
            if len(self.blocks) > 1:
                self.pending = (cch, w, kc, ncols, dep)
        else:
            self.dma(self.SP, self.SWB, w.t[:, 0:kc, 0:ncols], cch, R=[dep], W=w.d)
        return w

    def linear(self, src, kc, X, epi, ncols=512, w=None, segs=None):
        if w is None:
            w = self.load_slab(src, kc, ncols)
        for nci in range(ncols // 128):
            for (s0, sl) in (segs or self.msegs):
                ps = self.ps()
                mms = [(ps.t[:, 0:sl], w.t[:, k, nci * 128:(nci + 1) * 128], X.t[:, k, s0:s0 + sl], k == 0, k == kc - 1)
                       for k in range(kc)]
                self.mmg(ps.d[0], mms, R=w.d + X.d[0:kc])
                epi(nci, s0, sl, ps)
        return w

    def acc_epi(self, cbase, firstacc):
        DVE = self.DVE

        def epi(nci, s0, sl, ps):
            c = cbase + nci
            xr = self.XR.t[:, c, s0:s0 + sl]
            if firstacc:
                self.op(DVE, lambda: DVE.h.scalar_tensor_tensor(out=xr, in0=xr, scalar=ALPHA, in1=ps.t[:, 0:sl], op0=ALU.mult, op1=ALU.add),
                        R=[ps.d[0], self.XR.d[c]], W=[self.XR.d[c]])
            else:
                self.op(DVE, lambda: DVE.h.tensor_tensor(out=xr, in0=xr, in1=ps.t[:, 0:sl], op=ALU.add),
                        R=[ps.d[0], self.XR.d[c]], W=[self.XR.d[c]])
        return epi

    def ln(self, l, i):
        ACT, DVE = self.ACT, self.DVE
        T = self.T
        with ExitStack() as st:
            tA = [self.tile(f"lnA{j}", (128, T), BF16, stack=st) for j in range(2)]
            tB = [self.tile(f"lnB{j}", (128, T), BF16, stack=st) for j in range(2)]
            mean = self.tile("lnmean", (128, T), F32, stack=st)
            rstd = self.tile("lnrstd", (128, T), F32, stack=st)
            nmr = self.tile("lnnmr", (128, T), F32, stack=st)
            tmp = [self.tile(f"lntmp{j}", (128, T), F32, stack=st) for j in range(2)]
            pm = [self.ps() for _ in self.msegs]
            pq = [self.ps() for _ in self.msegs]
            for c in range(16):
                a = tA[c % 2]; b = tB[c % 2]
                self.op(DVE, lambda: DVE.h.tensor_copy(a.t[:], self.XR.t[:, c, 0:T]), R=[self.XR.d[c]], W=a.d)
                self.op(ACT, lambda: ACT.h.activation(out=b.t[:], in_=self.XR.t[:, c, 0:T], func=AF.Square), R=[self.XR.d[c]], W=b.d)
                for si, (s0, sl) in enumerate(self.msegs):
                    self.mmg(pm[si].d[0], [(pm[si].t[:, 0:sl], self.ONESB.t[:], a.t[:, s0:s0 + sl], c == 0, c == 15)], R=a.d + self.ONESB.d)
                    self.mmg(pq[si].d[0], [(pq[si].t[:, 0:sl], self.ONESB.t[:], b.t[:, s0:s0 + sl], c == 0, c == 15)], R=b.d + self.ONESB.d)
            for si, (s0, sl) in enumerate(self.msegs):
                mn = mean.t[:, s0:s0 + sl]; rs = rstd.t[:, s0:s0 + sl]; nm = nmr.t[:, s0:s0 + sl]
                self.op(ACT, lambda: ACT.h.copy(mn, pm[si].t[:, 0:sl]), R=pm[si].d, W=mean.d)
                self.op(DVE, lambda: DVE.h.tensor_tensor(out=nm, in0=mn, in1=mn, op=ALU.mult), R=mean.d, W=nmr.d)
                self.op(DVE, lambda: DVE.h.tensor_tensor(out=rs, in0=pq[si].t[:, 0:sl], in1=nm, op=ALU.subtract), R=pq[si].d + nmr.d, W=rstd.d)
                self.op(ACT, lambda: ACT.h.activation(out=rs, in_=rs, func=AF.Ln, bias=LN_EPS), R=rstd.d, W=rstd.d)
                self.op(ACT, lambda: ACT.h.activation(out=rs, in_=rs, func=AF.Exp, scale=-0.5), R=rstd.d, W=rstd.d)
                self.op(DVE, lambda: DVE.h.scalar_tensor_tensor(out=nm, in0=mn, scalar=-1.0, in1=rs, op0=ALU.mult, op1=ALU.mult),
                        R=mean.d + rstd.d, W=nmr.d)
            for c in range(16):
                t = tmp[c % 2]
                g = self.VEC.t[:, OFF_LNG + (l * 3 + i) * 16 + c:OFF_LNG + (l * 3 + i) * 16 + c + 1]
                bb = self.VEC.t[:, OFF_LNB + (l * 3 + i) * 16 + c:OFF_LNB + (l * 3 + i) * 16 + c + 1]
                self.op(DVE, lambda: DVE.h.tensor_tensor(out=t.t[:], in0=self.XR.t[:, c, 0:T], in1=rstd.t[:], op=ALU.mult),
                        R=[self.XR.d[c]] + rstd.d, W=t.d)
                self.op(DVE, lambda: DVE.h.tensor_tensor(out=t.t[:], in0=t.t[:], in1=nmr.t[:], op=ALU.add), R=t.d + nmr.d, W=t.d)
                self.op(ACT, lambda: ACT.h.activation(out=self.XR.t[:, c, 0:T], in_=t.t[:], func=AF.Identity, bias=bb, scale=g),
                        R=t.d + self.VEC.d, W=[self.XR.d[c]])
                self.op(ACT, lambda: ACT.h.activation(out=self.XB.t[:, c, 0:T], in_=t.t[:], func=AF.Identity, bias=bb, scale=g),
                        R=t.d + self.VEC.d, W=[self.XB.d[c]])
            self.barrier()

    def mlp(self, l):
        ACT, DVE = self.ACT, self.DVE
        T = self.T
        with ExitStack() as st:
            YB = self.tile("mlpY", (128, 16, T), BF16, 16, stack=st)
            rt = [self.tile(f"mlpr{j}", (128, 512), F32, stack=st) for j in range(3)]
            cnt = [0]
            for q in range(4):
                for s in range(4):
                    def epi(nci, s0, sl, ps, s=s):
                        c = s * 4 + nci
                        r = rt[cnt[0] % 3]; cnt[0] += 1
                        self.op(ACT, lambda: ACT.h.activation(out=r.t[:, 0:sl], in_=ps.t[:, 0:sl], func=AF.Relu), R=ps.d, W=r.d)
                        self.op(DVE, lambda: DVE.h.tensor_tensor(out=YB.t[:, c, s0:s0 + sl], in0=r.t[:, 0:sl], in1=r.t[:, 0:sl], op=ALU.mult),
                                R=r.d, W=[YB.d[c]])
                    self.linear(("w_up", l, q * 4 + s), 16, self.XB, epi)
                for s in range(4):
                    self.linear(("w_dn", l, q * 4 + s), 16, YB, self.acc_epi(s * 4, q == 0))
            self.barrier()

    def softmax_pt(self, ps_s, PT, col0, st_tiles, idx):
        ACT, DVE = self.ACT, self.DVE
        mx, nb, p, rs = st_tiles
        k = idx % 2
        self.op(DVE, lambda: DVE.h.reduce_max(out=mx[k].t[:], in_=ps_s.t[:, 0:256], axis=AX.X), R=ps_s.d, W=mx[k].d)
        self.op(DVE, lambda: DVE.h.tensor_scalar(out=nb[k].t[:], in0=mx[k].t[:], scalar1=-XSCALE, scalar2=None, op0=ALU.mult), R=mx[k].d, W=nb[k].d)
        self.op(ACT, lambda: ACT.h.activation(out=p[k].t[:], in_=ps_s.t[:, 0:256], func=AF.Exp, bias=nb[k].t[:], scale=XSCALE, accum_out=rs[k].t[:]),
                R=ps_s.d + nb[k].d, W=p[k].d + rs[k].d)
        self.op(DVE, lambda: DVE.h.reciprocal(out=rs[k].t[:], in_=rs[k].t[:]), R=rs[k].d, W=rs[k].d)
        self.op(DVE, lambda: DVE.h.tensor_scalar(out=p[k].t[:], in0=p[k].t[:], scalar1=rs[k].t[:], scalar2=None, op0=ALU.mult),
                R=p[k].d + rs[k].d, W=p[k].d)
        for mt in range(2):
            pst = self.ps()
            self.tr(pst, 128, p[k].t[:, mt * 128:(mt + 1) * 128], R=p[k].d)
            self.op(ACT, lambda: ACT.h.copy(PT.t[:, mt, col0:col0 + 128], pst.t[:, 0:128]), R=pst.d, W=[PT.d[mt]])

    def xattn(self, l):
        ACT, DVE, SP, PQ = self.ACT, self.DVE, self.SP, self.PQ
        T, Tp, Ts = self.T, self.Tp, self.Ts
        with ExitStack() as st:
            YB = self.tile("xaY", (128, 16, T), BF16, 16, stack=st)
            QT = self.tile("xaQ", (128, 4, T), BF16, 4, stack=st)
            KT = self.tile("xaK", (128, 4, 256), BF16, 4, stack=st)
            VH = self.tile("xaV", (128, 2, 512), BF16, 2, stack=st)
            PT = self.tile("xaPT", (128, 2, T), BF16, 2, stack=st)
            QM = [self.tile(f"xaQM{j}", (128, 4, 128), BF16, stack=st) for j in range(2)]
            GB = 4
            self.KTB = [self.tile(f"KTB{i}", (128, GB, 4, 256) if Ts else (128, 1, 1, 1), BF16, stack=st) for i in range(2)]
            self.VB = [self.tile(f"VB{i}", (128, GB, 2, 512) if Ts else (128, 1, 1, 1), BF16, stack=st) for i in range(2)]
            stg = [self.tile(f"xaS{j}", (128, 512), F32, stack=st) for j in range(2)]
            mx = [self.tile(f"xamx{j}", (128, 1), F32, stack=st) for j in range(2)]
            nb = [self.tile(f"xanb{j}", (128, 1), F32, stack=st) for j in range(2)]
            p = [self.tile(f"xap{j}", (128, 256), F32, stack=st) for j in range(2)]
            rs = [self.tile(f"xars{j}", (128, 1), F32, stack=st) for j in range(2)]
            sm = (mx, nb, p, rs)
            if Ts:
                for q in QM:
                    self.op(DVE, lambda q=q: DVE.h.memset(q.t[:], 0.0), W=q.d)
            sidx = 0
            stgi = 0
            import os
            XS = int(os.environ.get("XSTOP", "100"))
            for hd in range(4):
                def qepi(nci, s0, sl, ps):
                    self.op(ACT, lambda: ACT.h.copy(QT.t[:, nci, s0:s0 + sl], ps.t[:, 0:sl]), R=ps.d, W=[QT.d[nci]])
                self.linear(("wq", l, hd), 16, self.XB, qepi)
                if Tp and XS >= 2:
                    wk = self.load_slab(("wk", l, hd), 16, 512)
                    for nci in range(4):
                        ps = self.ps()
                        self.mmg(ps.d[0], [(ps.t[:, 0:256], wk.t[:, k, nci * 128:(nci + 1) * 128], self.MEMT.t[:, k, :], k == 0, k == 15) for k in range(16)],
                                 R=wk.d + self.MEMT.d)
                        self.op(ACT, lambda: ACT.h.copy(KT.t[:, nci, :], ps.t[:, 0:256]), R=ps.d, W=[KT.d[nci]])
                    if self.first and XS >= 3:
                        for mt in range(2):
                            ps = self.ps()
                            self.mmg(ps.d[0], [(ps.t[:, :], self.MEMT.t[:, k, mt * 128:(mt + 1) * 128], wk.t[:, k, :], k == 0, k == 15) for k in range(16)],
                                     R=wk.d + self.MEMT.d)
                            sg = stg[stgi % 2]; stgi += 1
                            self.op(ACT, lambda: ACT.h.copy(sg.t[:], ps.t[:, :]), R=ps.d, W=sg.d)
                            self.dma(self.PQ, self.SO, self.mk_p[l, mt * 128:(mt + 1) * 128, hd * 512:(hd + 1) * 512], sg.t[:], R=sg.d)
                    wv = self.load_slab(("wv", l, hd), 16, 512)
                    for mt in range(2 if XS >= 4 else 0):
                        ps = self.ps()
                        self.mmg(ps.d[0], [(ps.t[:, :], self.MEMT.t[:, k, mt * 128:(mt + 1) * 128], wv.t[:, k, :], k == 0, k == 15) for k in range(16)],
                                 R=wv.d + self.MEMT.d)
                        self.op(ACT, lambda: ACT.h.copy(VH.t[:, mt, :], ps.t[:, :]), R=ps.d, W=[VH.d[mt]])
                        if self.first:
                            sg = stg[stgi % 2]; stgi += 1
                            self.op(ACT, lambda: ACT.h.copy(sg.t[:], ps.t[:, :]), R=ps.d, W=sg.d)
                            self.dma(self.PQ, self.SO, self.mv_p[l, mt * 128:(mt + 1) * 128, hd * 512:(hd + 1) * 512], sg.t[:], R=sg.d)
                    for ti in range(self.npt if XS >= 5 else 0):
                        ps = self.ps()
                        self.mmg(ps.d[0], [(ps.t[:, 0:256], QT.t[:, dc, ti * 128:(ti + 1) * 128], KT.t[:, dc, :], dc == 0, dc == 3) for dc in range(4)],
                                 R=QT.d + KT.d)
                        self.softmax_pt(ps, PT, ti * 128, sm, sidx); sidx += 1
                    for dc in range(4 if XS >= 6 else 0):
                        for (s0, sl) in self.segs:
                            if s0 >= Tp:
                                continue
                            ps = self.ps()
                            self.mmg(ps.d[0], [(ps.t[:, 0:sl], VH.t[:, mt, dc * 128:(dc + 1) * 128], PT.t[:, mt, s0:s0 + sl], mt == 0, mt == 1) for mt in range(2)],
                                     R=VH.d + PT.d)
                            c = hd * 4 + dc
                            self.op(ACT, lambda: ACT.h.copy(YB.t[:, c, s0:s0 + sl], ps.t[:, 0:sl]), R=ps.d, W=[YB.d[c]])
                if Ts:
                    pss = self.ps()
                    for b in range(NSEQ):
                        kbt = self.KTB[(b // GB) % 2]
                        if b % GB == 0:
                            self.dma(PQ, self.SW, kbt.t[:], self.kT_d[l, b:b + GB, hd].rearrange("b p d m -> p b d m"), W=kbt.d)
                        kb = Tl(kbt.t[:, b % GB], 0); kb.d = kbt.d
                        qm = QM[b % 2]
                        cs = slice(b * ST, (b + 1) * ST)
                        self.op(DVE, lambda: DVE.h.tensor_copy(qm.t[:, :, cs], QT.t[:, :, Tp + b * ST:Tp + (b + 1) * ST]), R=QT.d, W=qm.d)
                        self.mmg(pss.d[0], [(pss.t[:, 0:256], qm.t[:, dc, :], kb.t[:, dc, :], (b == 0 and dc == 0), (b == NSEQ - 1 and dc == 3)) for dc in range(4)],
                                 R=qm.d + kb.d)
                        self.op(DVE, lambda: DVE.h.memset(qm.t[:, :, cs], 0.0), W=qm.d)
                    self.softmax_pt(pss, PT, Tp, sm, sidx); sidx += 1
                    pso = [self.ps() for _ in range(4)]
                    for b in range(NSEQ):
                        vbt = self.VB[(b // GB) % 2]
                        if b % GB == 0:
                            self.dma(PQ, self.SW, vbt.t[:], self.v_d[l, b:b + GB, hd].rearrange("b p t d -> p b t d"), W=vbt.d)
                        vb = Tl(vbt.t[:, b % GB], 0); vb.d = vbt.d
                        for dc in range(4):
                            self.mmg(pso[dc].d[0], [(pso[dc].t[:, b * ST:(b + 1) * ST], vb.t[:, mt, dc * 128:(dc + 1) * 128],
                                                     PT.t[:, mt, Tp + b * ST:Tp + (b + 1) * ST], mt == 0, mt == 1) for mt in range(2)],
                                     R=vb.d + PT.d)
                    for dc in range(4):
                        c = hd * 4 + dc
                        self.op(ACT, lambda: ACT.h.copy(YB.t[:, c, Tp:T], pso[dc].t[:, 0:128]), R=pso[dc].d, W=[YB.d[c]])
            for s in range(4 if XS >= 7 else 0):
                self.linear(("wo", l, s), 16, YB, self.acc_epi(s * 4, True))
            self.barrier()

    def pool(self, o):
        ACT, DVE, SP = self.ACT, self.DVE, self.SP
        T, Tp, Ts = self.T, self.Tp, self.Ts
        with ExitStack() as st:
            YB = self.tile("plY", (128, 16, T), BF16, 16, stack=st)
            FP = [self.tile(f"plF{j}", (128, 15 + max(Tp, 1)), F32, stack=st) for j in range(2)]
            SA = [self.tile(f"plA{j}", (128, 15 + max(Tp, 1)), F32, stack=st) for j in range(2)]
            SB_ = [self.tile(f"plB{j}", (128, 15 + max(Tp, 1)), F32, stack=st) for j in range(2)]
            FS = [self.tile(f"plFS{j}", (128, NSEQ, 23), F32, stack=st) for j in range(2)]
            SAs = [self.tile(f"plAs{j}", (128, NSEQ, 23), F32, stack=st) for j in range(2)]
            SBs = [self.tile(f"plBs{j}", (128, NSEQ, 23), F32, stack=st) for j in range(2)]
            tm = [self.tile(f"pltm{j}", (128, 512), F32, stack=st) for j in range(2)]
            PSI = [self.tile(f"plSI{j}", (128, NSEQ, 15), F32, stack=st) for j in range(2)]
            PSO_ = [self.tile(f"plSO{j}", (128, NSEQ, 15), F32, stack=st) for j in range(2)]
            for c in range(16):
                gi = c // 4
                w = 2 << gi
                nst = gi + 1
                k = c % 2
                if Tp:
                    f = FP[k]; a = SA[k]; b = SB_[k]
                    self.op(ACT, lambda: ACT.h.copy(f.t[:, 0:15], self.PH[o].t[:, c, :]), R=[self.PH[o].d[c]], W=f.d)
                    self.op(ACT, lambda: ACT.h.copy(f.t[:, 15:15 + Tp], self.XR.t[:, c, 0:Tp]), R=[self.XR.d[c]], W=f.d)
                    src = f; lo = 0
                    for s_ in range(nst):
                        sh = 1 << s_
                        dst = a if s_ % 2 == 0 else b
                        nlo = lo + sh
                        self.op(DVE, lambda: DVE.h.tensor_tensor(out=dst.t[:, nlo:15 + Tp], in0=src.t[:, nlo:15 + Tp], in1=src.t[:, nlo - sh:15 + Tp - sh], op=ALU.add),
                                R=src.d, W=dst.d)
                        src = dst; lo = nlo
                    self.op(DVE, lambda: DVE.h.scalar_tensor_tensor(out=YB.t[:, c, 0:Tp], in0=src.t[:, 15:15 + Tp], scalar=1.0 / w, in1=f.t[:, 15:15 + Tp],
                                                                    op0=ALU.mult, op1=ALU.subtract), R=src.d + f.d, W=[YB.d[c]])
                    if self.first:
                        t_ = tm[k]
                        self.op(DVE, lambda: DVE.h.tensor_tensor(out=t_.t[:, 0:16], in0=src.t[:, 15:31], in1=self.CON.t[:, C_ICNT + gi * 16:C_ICNT + gi * 16 + 16], op=ALU.mult),
                                R=src.d + self.CON.d, W=t_.d)
                        self.op(DVE, lambda: DVE.h.tensor_tensor(out=YB.t[:, c, 0:16], in0=t_.t[:, 0:16], in1=f.t[:, 15:31], op=ALU.subtract),
                                R=t_.d + f.d, W=[YB.d[c]])
                    self.op(ACT, lambda: ACT.h.copy(self.PH[o].t[:, c, :], f.t[:, Tp:Tp + 15]), R=f.d, W=[self.PH[o].d[c]])
                    if self.last:
                        self.dma(self.PQ, self.SO, self.pbT_p[o, :, c, :], self.PH[o].t[:, c, :], R=[self.PH[o].d[c]])
                if Ts:
                    f = FS[k]; a = SAs[k]; b = SBs[k]
                    self.dma(self.PQ, self.SL, PSI[k].t[:], self.poolT_d[o, :, c, :, :], W=PSI[k].d)
                    self.op(ACT, lambda: ACT.h.copy(f.t[:, :, 0:15], PSI[k].t[:]), R=PSI[k].d, W=f.d)
                    self.op(ACT, lambda: ACT.h.copy(f.t[:, :, 15:23], self.XR.t[:, c, Tp:T].rearrange("p (b t) -> p b t", t=ST)), R=[self.XR.d[c]], W=f.d)
                    src = f; lo = 0
                    for s_ in range(nst):
                        sh = 1 << s_
                        dst = a if s_ % 2 == 0 else b
                        nlo = lo + sh
                        self.op(DVE, lambda: DVE.h.tensor_tensor(out=dst.t[:, :, nlo:23], in0=src.t[:, :, nlo:23], in1=src.t[:, :, nlo - sh:23 - sh], op=ALU.add),
                                R=src.d, W=dst.d)
                        src = dst; lo = nlo
                    self.op(DVE, lambda: DVE.h.scalar_tensor_tensor(out=YB.t[:, c, Tp:T].rearrange("p (b t) -> p b t", t=ST), in0=src.t[:, :, 15:23], scalar=1.0 / w,
                                                                    in1=f.t[:, :, 15:23], op0=ALU.mult, op1=ALU.subtract), R=src.d + f.d, W=[YB.d[c]])
                    self.op(ACT, lambda: ACT.h.copy(PSO_[k].t[:], f.t[:, :, 8:23]), R=f.d, W=PSO_[k].d)
                    self.dma(self.PQ, self.SO, self.pbT_s[o, :, c, :, :], PSO_[k].t[:], R=PSO_[k].d)
            for gi in range(4):
                def epi(nci, s0, sl, ps, gi=gi):
                    c = gi * 4 + nci
                    t_ = tm[nci % 2]
                    sc = self.VEC.t[:, OFF_PSC + o * 16 + c:OFF_PSC + o * 16 + c + 1]
                    self.op(ACT, lambda: ACT.h.activation(out=t_.t[:, 0:sl], in_=ps.t[:, 0:sl], func=AF.Copy, scale=sc), R=ps.d + self.VEC.d, W=t_.d)
                    xr = self.XR.t[:, c, s0:s0 + sl]
                    self.op(DVE, lambda: DVE.h.scalar_tensor_tensor(out=xr, in0=xr, scalar=ALPHA, in1=t_.t[:, 0:sl], op0=ALU.mult, op1=ALU.add),
                            R=t_.d + [self.XR.d[c]], W=[self.XR.d[c]])
                Xv = Tl(YB.t[:, gi * 4:(gi + 1) * 4, :], 0)
                Xv.d = YB.d[gi * 4:(gi + 1) * 4]
                self.linear(("w_pool", o, gi), 4, Xv, epi)
            self.barrier()

    def even_sc(self, e):
        ACT, DVE, SP = self.ACT, self.DVE, self.SP
        T, Tp, Ts = self.T, self.Tp, self.Ts
        with ExitStack() as st:
            YB = self.tile("scY", (128, 16, T), BF16, 16, stack=st)
            U = self.tile("scU", (128, 4, 2 + max(Tp, 1)), F32, 4, stack=st)
            US = self.tile("scUS", (128, 4, NSEQ, 2 + ST), F32, 4, stack=st)
            V = self.tile("scV", (128, 512), F32, stack=st)
            VS = self.tile("scVS", (128, NSEQ, ST), F32, stack=st)
            SSI = self.tile("scSI", (128, 4, NSEQ, 2), F32, stack=st)
            SSO = self.tile("scSO", (128, 4, NSEQ, 2), F32, stack=st)
            for j in range(4):
                def uview(nci, s0, sl):
                    if s0 < Tp:
                        return U.t[:, nci, 2 + s0:2 + s0 + sl]
                    return US.t[:, nci, :, 2:2 + ST]

                def pview(ps, s0, sl):
                    if s0 < Tp:
                        return ps.t[:, 0:sl]
                    return ps.t[:, 0:128].rearrange("p (b t) -> p b t", t=ST)

                def epi_h(nci, s0, sl, ps):
                    self.op(ACT, lambda: ACT.h.copy(uview(nci, s0, sl), pview(ps, s0, sl)), R=ps.d, W=[U.d[nci]])

                def epi_g(nci, s0, sl, ps):
                    uv = uview(nci, s0, sl)
                    self.op(DVE, lambda: DVE.h.tensor_tensor(out=uv, in0=uv, in1=pview(ps, s0, sl), op=ALU.mult), R=ps.d + [U.d[nci]], W=[U.d[nci]])

                def epi_b(nci, s0, sl, ps, j=j):
                    c = j * 4 + nci
                    wv = [self.VEC.t[:, OFF_SCW + (e * 16 + c) * 3 + k:OFF_SCW + (e * 16 + c) * 3 + k + 1] for k in range(3)]
                    if s0 < Tp:
                        v = V.t[:, 0:sl]
                        ins = [U.t[:, nci, s0 + k:s0 + k + sl] for k in range(3)]
                        yv = YB.t[:, c, s0:s0 + sl]
                        vd = V.d
                    else:
                        v = VS.t[:]
                        ins = [US.t[:, nci, :, k:k + ST] for k in range(3)]
                        yv = YB.t[:, c, Tp:T].rearrange("p (b t) -> p b t", t=ST)
                        vd = VS.d
                    self.op(DVE, lambda: DVE.h.tensor_scalar(out=v, in0=ins[0], scalar1=wv[0], scalar2=None, op0=ALU.mult), R=[U.d[nci]] + self.VEC.d, W=vd)
                    self.op(DVE, lambda: DVE.h.scalar_tensor_tensor(out=v, in0=ins[1], scalar=wv[1], in1=v, op0=ALU.mult, op1=ALU.add), R=[U.d[nci]] + vd, W=vd)
                    self.op(DVE, lambda: DVE.h.scalar_tensor_tensor(out=v, in0=ins[2], scalar=wv[2], in1=v, op0=ALU.mult, op1=ALU.add), R=[U.d[nci]] + vd, W=vd)
                    self.op(DVE, lambda: DVE.h.tensor_tensor(out=yv, in0=v, in1=pview(ps, s0, sl), op=ALU.mult), R=ps.d + vd, W=[YB.d[c]])
                for nci in range(4):
                    c = j * 4 + nci
                    if Tp:
                        self.op(ACT, lambda: ACT.h.copy(U.t[:, nci, 0:2], self.SCH[e].t[:, c, :]), R=[self.SCH[e].d[c]], W=[U.d[nci]])
                if Ts:
                    self.dma(self.PQ, self.SL, SSI.t[:], self.scT_d[e, :, j * 4:(j + 1) * 4, :, :], W=SSI.d)
                    for nci in range(4):
                        self.op(ACT, lambda: ACT.h.copy(US.t[:, nci, :, 0:2], SSI.t[:, nci]), R=SSI.d, W=[U.d[nci]])
                self.linear(("w_in", e, 12 + j * 3 + 0), 16, self.XB, epi_h, segs=self.segs)
                self.linear(("w_in", e, 12 + j * 3 + 1), 16, self.XB, epi_g, segs=self.segs)
                for nci in range(4):
                    c = j * 4 + nci
                    if Tp:
                        self.op(ACT, lambda: ACT.h.copy(self.SCH[e].t[:, c, :], U.t[:, nci, Tp:Tp + 2]), R=[U.d[nci]], W=[self.SCH[e].d[c]])
                        if self.last:
                            self.dma(self.PQ, self.SO, self.sbT_p[e, :, c, :], self.SCH[e].t[:, c, :], R=[self.SCH[e].d[c]])
                    if Ts:
                        self.op(ACT, lambda: ACT.h.copy(SSO.t[:, nci], US.t[:, nci, :, ST:ST + 2]), R=[U.d[nci]], W=SSO.d)
                if Ts:
                    self.dma(self.PQ, self.SO, self.sbT_s[e, :, j * 4:(j + 1) * 4, :, :], SSO.t[:], R=SSO.d)
                self.linear(("w_in", e, 12 + j * 3 + 2), 16, self.XB, epi_b, segs=self.segs)
            for s in range(4):
                self.linear(("w_out", e, 4 + s), 16, YB, self.acc_epi(s * 4, True))
            self.barrier()

    def even_ssd(self, e):
        ACT, DVE, SP, PQ = self.ACT, self.DVE, self.SP, self.PQ
        T, Tp, Ts, npt = self.T, self.Tp, self.Ts, self.npt
        NT = npt + (1 if Ts else 0)
        CON = self.CON
        with ExitStack() as st:
            tl = lambda name, shape, dt, n=1: self.tile(name, shape, dt, n, stack=st)
            YB = tl("sdY", (128, 16, T), BF16, 16)
            DT = tl("sdDT", (128, NT, 32), F32)
            AA = tl("sdA", (128, NT, 32), F32)
            ACS = tl("sdACS", (128, NT, 32), F32)
            NACS = tl("sdNACS", (128, NT, 32), F32)
            EACS = tl("sdEACS", (128, NT, 32), F32)
            DEND = tl("sdDEND", (128, NT, 32), F32)
            CDEC = tl("sdCDEC", (128, NT, 32), F32)
            ZG = tl("sdZG", (128, NT, 256), F32)
            XP = tl("sdXP", (128, 4, 3 + max(Tp, 1)), F32, 4)
            NQ = NSEQ if Ts else 1
            NH = NSEQ // 2 if Ts else 1
            XPS = tl("sdXPS", (128, 4, NQ, 3 + ST), F32, 4)
            XC = tl("sdXC", (128, 3, T), F32, 3)
            BCb = tl("sdBCb", (128, 2, T), BF16, 2)
            XTK = tl("sdXTK", (128, NT, 256), F32)
            BTK = tl("sdBTK", (128, NT, 128), BF16)
            AT = [tl(f"sdAT{j}", (128, 4, 128), F32) for j in range(2)]
            LT = [tl(f"sdLT{j}", (128, 4, 128), F32) for j in range(2)]
            MT = [tl(f"sdMT{j}", (128, 4, 128), BF16) for j in range(2)]
            XDT = [tl(f"sdXDT{j}", (128, 256), BF16) for j in range(2)]
            XDD = [tl(f"sdXDD{j}", (128, 256), BF16) for j in range(2)]
            Y1 = [tl(f"sdY1{j}", (128, 256), F32) for j in range(2)]
            Y2 = [tl(f"sdY2{j}", (128, 256), F32) for j in range(2)]
            SS = [tl(f"sdSS{j}", (128, 1), F32) for j in range(2)]
            HB = tl("sdHB", (128, 256), BF16)
            HTMP = tl("sdHTMP", (128, 256), F32)
            CM = tl("sdCM", (128, NH, 128), BF16)
            BM = tl("sdBM", (128, NH, 128), BF16)
            AEX = tl("sdAEX", (128, 256), F32)
            CDC = tl("sdCDC", (128, 2, NSEQ), F32)
            H0N = tl("sdH0N", (128, NH, 128), F32)
            HNO = tl("sdHNO", (128, NH, 128), F32)
            A_e = self.AH.t[:, e * 32:(e + 1) * 32]
            CSI = tl("sdCSI", (128, 4, NQ, 3), F32)
            CSO = tl("sdCSO", (128, 4, NQ, 3), F32)
            self.cvtmp = tl("sdCVT", (128, max(Tp, 128)), F32)
            wdt = self.load_slab(("w_dt", e), 16, 32)
            for ti in range(NT):
                ps = self.ps()
                self.mmg(ps.d[0], [(ps.t[:, 0:32], self.XB.t[:, k, ti * 128:(ti + 1) * 128], wdt.t[:, k, 0:32], k == 0, k == 15) for k in range(16)],
                         R=wdt.d + self.XB.d)
                d = DT.t[:, ti, :]
                self.op(DVE, lambda: DVE.h.tensor_tensor(out=d, in0=ps.t[:, 0:32], in1=self.ROW.t[:, R_DTB + e * 32:R_DTB + (e + 1) * 32], op=ALU.add),
                        R=ps.d + self.ROW.d, W=DT.d)
                self.op(ACT, lambda: ACT.h.activation(out=d, in_=d, func=AF.Exp), R=DT.d, W=DT.d)
                self.op(ACT, lambda: ACT.h.activation(out=d, in_=d, func=AF.Ln, bias=1.0), R=DT.d, W=DT.d)
                a = AA.t[:, ti, :]
                self.op(DVE, lambda: DVE.h.tensor_tensor(out=a, in0=d, in1=A_e, op=ALU.mult), R=DT.d + self.AH.d, W=AA.d)
                samp = (ti >= npt)
                tri = CON.t[:, (C_TRIS if samp else C_TRI):(C_TRIS if samp else C_TRI) + 128]
                one = CON.t[:, (C_SAME if samp else C_ONES):(C_SAME if samp else C_ONES) + 128]
                ps1 = self.ps()
                self.mmg(ps1.d[0], [(ps1.t[:, 0:32], tri, a, True, True)], R=AA.d + CON.d)
                ps2 = self.ps()
                self.mmg(ps2.d[0], [(ps2.t[:, 0:32], one, a, True, True)], R=AA.d + CON.d)
                self.op(ACT, lambda: ACT.h.copy(ACS.t[:, ti, :], ps1.t[:, 0:32]), R=ps1.d, W=ACS.d)
                self.op(ACT, lambda: ACT.h.mul(NACS.t[:, ti, :], ps1.t[:, 0:32], -1.0), R=ps1.d, W=NACS.d)
                self.op(ACT, lambda: ACT.h.activation(out=EACS.t[:, ti, :], in_=ps1.t[:, 0:32], func=AF.Exp), R=ps1.d, W=EACS.d)
                self.op(ACT, lambda: ACT.h.activation(out=CDEC.t[:, ti, :], in_=ps2.t[:, 0:32], func=AF.Exp), R=ps2.d, W=CDEC.d)
                self.op(DVE, lambda: DVE.h.tensor_tensor(out=DEND.t[:, ti, :], in0=ps2.t[:, 0:32], in1=ACS.t[:, ti, :], op=ALU.subtract), R=ps2.d + ACS.d, W=DEND.d)
                self.op(ACT, lambda: ACT.h.activation(out=DEND.t[:, ti, :], in_=DEND.t[:, ti, :], func=AF.Exp), R=DEND.d, W=DEND.d)
            it = 0
            for g in range(8):
                wa = self.load_slab(("w_in", e, g), 16, 512)
                wb = None
                if g % 2 == 0:
                    wbc = self.load_slab(("w_in", e, 8 + g // 2), 16, 512)
                    self.wbc = wbc
                wbc = self.wbc
                bo = (g % 2) * 256
                for ti in range(NT):
                    ps = self.ps()
                    self.mmg(ps.d[0], [(ps.t[:, 0:256], self.XB.t[:, k, ti * 128:(ti + 1) * 128], wa.t[:, k, 0:256], k == 0, k == 15) for k in range(16)],
                             R=wa.d + self.XB.d)
                    self.op(ACT, lambda: ACT.h.activation(out=ZG.t[:, ti, :], in_=ps.t[:, 0:256], func=AF.Silu), R=ps.d, W=ZG.d)
                cids = [2 * g, 2 * g + 1, 16 + g, 24 + g]
                for j in range(4):
                    c32 = cids[j]
                    if Tp:
                        self.op(ACT, lambda: ACT.h.copy(XP.t[:, j, 0:3], self.CVH[e].t[:, c32, :]), R=[self.CVH[e].d[c32]], W=[XP.d[j]])
                    if Ts:
                        if j == 0:
                            self.dma(self.PQ, self.SL, CSI.t[:], self.convT_d[e, g], W=CSI.d)
                        self.op(ACT, lambda: ACT.h.copy(XPS.t[:, j, :, 0:3], CSI.t[:, j]), R=CSI.d, W=[XPS.d[j]])
                for j in range(4):
                    wsrc, col = (wa, 256 + j * 128) if j < 2 else (wbc, bo + (j - 2) * 128)
                    for (s0, sl) in self.segs:
                        ps = self.ps()
                        self.mmg(ps.d[0], [(ps.t[:, 0:sl], wsrc.t[:, k, col:col + 128], self.XB.t[:, k, s0:s0 + sl], k == 0, k == 15) for k in range(16)],
                                 R=wsrc.d + self.XB.d)
                        if s0 < Tp:
                            self.op(ACT, lambda: ACT.h.copy(XP.t[:, j, 3 + s0:3 + s0 + sl], ps.t[:, 0:sl]), R=ps.d, W=[XP.d[j]])
                        else:
                            self.op(ACT, lambda: ACT.h.copy(XPS.t[:, j, :, 3:3 + ST], ps.t[:, 0:128].rearrange("p (b t) -> p b t", t=ST)), R=ps.d, W=[XPS.d[j]])
                for j in range(4):
                    c32 = cids[j]
                    wv = [self.VEC.t[:, OFF_CVW + (e * 32 + c32) * 4 + k:OFF_CVW + (e * 32 + c32) * 4 + k + 1] for k in range(4)]
                    bv = self.VEC.t[:, OFF_CVB + e * 32 + c32:OFF_CVB + e * 32 + c32 + 1]
                    for part in range(2):
                        if (part == 0 and not Tp) or (part == 1 and not Ts):
                            continue
                        if part == 0:
                            ins = [XP.t[:, j, k:k + Tp] for k in range(4)]
                            outs_ = (XC.t[:, j, 0:Tp] if j < 3 else None)
                            bview = (BCb.t[:, j - 2, 0:Tp] if j >= 2 else None)
                            tmpv = self.cvtmp.t[:, 0:Tp]
                            rd = [XP.d[j]]
                        else:
                            ins = [XPS.t[:, j, :, k:k + ST] for k in range(4)]
                            outs_ = (XC.t[:, j, Tp:T].rearrange("p (b t) -> p b t", t=ST) if j < 3 else None)
                            bview = (BCb.t[:, j - 2, Tp:T].rearrange("p (b t) -> p b t", t=ST) if j >= 2 else None)
                            tmpv = self.cvtmp.t[:, 0:128].rearrange("p (b t) -> p b t", t=ST)
                            rd = [XPS.d[j]]
                        td = self.cvtmp.d
                        self.op(DVE, lambda: DVE.h.tensor_scalar(out=tmpv, in0=ins[0], scalar1=wv[0], scalar2=None, op0=ALU.mult), R=rd + self.VEC.d, W=td)
                        for k in range(1, 4):
                            self.op(DVE, lambda k=k: DVE.h.scalar_tensor_tensor(out=tmpv, in0=ins[k], scalar=wv[k], in1=tmpv, op0=ALU.mult, op1=ALU.add), R=rd + td, W=td)
                        if outs_ is not None:
                            self.op(ACT, lambda: ACT.h.activation(out=outs_, in_=tmpv, func=AF.Silu, bias=bv), R=td + self.VEC.d, W=[XC.d[j]])
                        if bview is not None:
                            self.op(ACT, lambda: ACT.h.activation(out=bview, in_=tmpv, func=AF.Silu, bias=bv), R=td + self.VEC.d, W=[BCb.d[j - 2]])
                    if Tp:
                        self.op(ACT, lambda: ACT.h.copy(self.CVH[e].t[:, c32, :], XP.t[:, j, Tp:Tp + 3]), R=[XP.d[j]], W=[self.CVH[e].d[c32]])
                        if self.last:
                            self.dma(self.PQ, self.SO, self.cbT_p[e, :, c32, :], self.CVH[e].t[:, c32, :], R=[self.CVH[e].d[c32]])
                    if Ts:
                        self.op(ACT, lambda: ACT.h.copy(CSO.t[:, j], XPS.t[:, j, :, ST:ST + 3]), R=[XPS.d[j]], W=CSO.d)
                        if j == 3:
                            self.dma(self.PQ, self.SO, self.cbT_s[e, g], CSO.t[:], R=CSO.d)
                for ti in range(NT):
                    for j in range(3):
                        pst = self.ps()
                        self.tr(pst, 128, XC.t[:, j, ti * 128:(ti + 1) * 128], R=[XC.d[j]])
                        if j < 2:
                            self.op(ACT, lambda: ACT.h.copy(XTK.t[:, ti, j * 128:(j + 1) * 128], pst.t[:, 0:128]), R=pst.d, W=XTK.d)
                        else:
                            self.op(ACT, lambda: ACT.h.copy(BTK.t[:, ti, :], pst.t[:, 0:128]), R=pst.d, W=BTK.d)
                hsd = [self.HS[e].d[g]]
                hst = self.HS[e].t[:, g, :]
                def chunk(ti):
                    nonlocal it
                    samp = ti >= npt
                    k2 = it % 2; it += 1
                    cols = slice(ti * 128, (ti + 1) * 128)
                    hsl = slice(4 * g, 4 * g + 4)
                    tri = CON.t[:, (C_TRIS if samp else C_TRI):(C_TRIS if samp else C_TRI) + 128]
                    mneg = CON.t[:, (C_MNEGS if samp else C_MNEG):(C_MNEGS if samp else C_MNEG) + 128]
                    pcb = self.ps()
                    self.mmg(pcb.d[0], [(pcb.t[:, 0:128], BCb.t[:, 0, cols], BCb.t[:, 1, cols], True, True)], R=BCb.d)
                    at = AT[k2]
                    self.op(DVE, lambda: DVE.h.tensor_tensor(out=at.t[:], in0=tri.unsqueeze(1).to_broadcast([128, 4, 128]),
                                                             in1=AA.t[:, ti, hsl].unsqueeze(2).to_broadcast([128, 4, 128]), op=ALU.mult),
                            R=AA.d + CON.d, W=at.d)
                    pd = self.ps()
                    self.mmg(pd.d[0], [(pd.t[:, :], CON.t[:, C_ONES:C_ONES + 128], at.t[:].rearrange("p h l -> p (h l)"), True, False),
                                       (pd.t[:, :].rearrange("p (h l) -> p h l", l=128), CON.t[:, C_ID:C_ID + 128], mneg.unsqueeze(1).to_broadcast([128, 4, 128]), False, True)],
                             R=at.d + CON.d)
                    lt = LT[k2]
                    for h in range(4):
                        self.op(ACT, lambda h=h: ACT.h.activation(out=lt.t[:, h, :], in_=pd.t[:, h * 128:(h + 1) * 128], func=AF.Exp,
                                                                 bias=NACS.t[:, ti, 4 * g + h:4 * g + h + 1]), R=pd.d + NACS.d, W=lt.d)
                    yield
                    mt = MT[k2]
                    self.op(DVE, lambda: DVE.h.tensor_tensor(out=mt.t[:], in0=lt.t[:], in1=pcb.t[:, 0:128].unsqueeze(1).to_broadcast([128, 4, 128]), op=ALU.mult),
                            R=lt.d + pcb.d, W=mt.d)
                    xdt = XDT[k2]; xdd = XDD[k2]; y1 = Y1[k2]; y2 = Y2[k2]
                    xs3 = XTK.t[:, ti, :].rearrange("p (h q) -> p h q", q=64)
                    self.op(DVE, lambda: DVE.h.tensor_tensor(out=y1.t[:].rearrange("p (h q) -> p h q", q=64), in0=xs3,
                                                             in1=DT.t[:, ti, hsl].unsqueeze(2).to_broadcast([128, 4, 64]), op=ALU.mult), R=XTK.d + DT.d, W=y1.d)
                    self.op(ACT, lambda: ACT.h.copy(xdt.t[:], y1.t[:]), R=y1.d, W=xdt.d)
                    self.op(DVE, lambda: DVE.h.tensor_tensor(out=xdd.t[:].rearrange("p (h q) -> p h q", q=64), in0=y1.t[:].rearrange("p (h q) -> p h q", q=64),
                                                             in1=DEND.t[:, ti, hsl].unsqueeze(2).to_broadcast([128, 4, 64]), op=ALU.mult), R=y1.d + DEND.d, W=xdd.d)
                    py = self.ps()
                    self.mmg(py.d[0], [(py.t[:, h * 64:(h + 1) * 64], mt.t[:, h, :], xdt.t[:, h * 64:(h + 1) * 64], True, True) for h in range(4)], R=mt.d + xdt.d)
                    po = self.ps()
                    if not samp:
                        self.op(ACT, lambda: ACT.h.copy(HB.t[:], hst), R=hsd, W=HB.d)
                        self.mmg(po.d[0], [(po.t[:, 0:256], BCb.t[:, 1, cols], HB.t[:], True, True)], R=BCb.d + HB.d)
                    else:
                        h0 = self.H0T[0]
                        self.dma(PQ, self.SW, h0.t[:], self.ssdT_d[e, g], W=h0.d)
                        for sh in range(2):
                            b0 = sh * NH
                            self.op(DVE, lambda: DVE.h.tensor_tensor(out=CM.t[:], in0=BCb.t[:, 1, cols].unsqueeze(1).to_broadcast([128, NH, 128]),
                                                                     in1=self.SMTB.t[:, b0:b0 + NH, :], op=ALU.mult), R=BCb.d + self.SMTB.d, W=CM.d)
                            self.mmg(po.d[0], [(po.t[:, 0:256], CM.t[:, b, :], h0.t[:, b0 + b, :], (sh == 0 and b == 0), (sh == 1 and b == NH - 1)) for b in range(NH)],
                                     R=CM.d + h0.d)
                    if not samp:
                        pss = self.ps()
                        self.mmg(pss.d[0], [(pss.t[:, 0:256], BTK.t[:, ti, :], xdd.t[:], True, True)], R=BTK.d + xdd.d)
                        self.op(DVE, lambda: DVE.h.tensor_tensor(out=HTMP.t[:].rearrange("p (h q) -> p h q", q=64), in0=hst.rearrange("p (h q) -> p h q", q=64),
                                                                 in1=CDEC.t[:, ti, hsl].unsqueeze(2).to_broadcast([128, 4, 64]), op=ALU.mult), R=hsd + CDEC.d, W=HTMP.d)
                        self.op(DVE, lambda: DVE.h.tensor_tensor(out=hst, in0=HTMP.t[:], in1=pss.t[:, 0:256], op=ALU.add), R=HTMP.d + pss.d + HB.d, W=hsd)
                    yield
                    self.op(DVE, lambda: DVE.h.tensor_tensor(out=y1.t[:].rearrange("p (h q) -> p h q", q=64), in0=po.t[:, 0:256].rearrange("p (h q) -> p h q", q=64),
                                                             in1=EACS.t[:, ti, hsl].unsqueeze(2).to_broadcast([128, 4, 64]), op=ALU.mult), R=po.d + EACS.d + xdt.d, W=y1.d)
                    self.op(DVE, lambda: DVE.h.tensor_tensor(out=y1.t[:], in0=y1.t[:], in1=py.t[:, 0:256], op=ALU.add), R=y1.d + py.d, W=y1.d)
                    self.op(DVE, lambda: DVE.h.tensor_tensor(out=y2.t[:].rearrange("p (h q) -> p h q", q=64), in0=xs3,
                                                             in1=self.ROW.t[:, R_DSK + e * 32 + 4 * g:R_DSK + e * 32 + 4 * g + 4].unsqueeze(2).to_broadcast([128, 4, 64]), op=ALU.mult),
                            R=XTK.d + self.ROW.d, W=y2.d)
                    self.op(DVE, lambda: DVE.h.tensor_tensor(out=y1.t[:], in0=y1.t[:], in1=y2.t[:], op=ALU.add), R=y1.d + y2.d, W=y1.d)
                    self.op(DVE, lambda: DVE.h.tensor_tensor(out=y1.t[:], in0=y1.t[:], in1=ZG.t[:, ti, :], op=ALU.mult), R=y1.d + ZG.d, W=y1.d)
                    ss = SS[k2]
                    self.op(ACT, lambda: ACT.h.activation(out=y2.t[:], in_=y1.t[:], func=AF.Square, accum_out=ss.t[:]), R=y1.d, W=y2.d + ss.d)
                    self.op(ACT, lambda: ACT.h.activation(out=ss.t[:], in_=ss.t[:], func=AF.Ln, bias=RMS_EPS, scale=1.0 / 256), R=ss.d, W=ss.d)
                    self.op(ACT, lambda: ACT.h.activation(out=ss.t[:], in_=ss.t[:], func=AF.Exp, scale=-0.5), R=ss.d, W=ss.d)
                    self.op(DVE, lambda: DVE.h.tensor_scalar(out=y1.t[:], in0=y1.t[:], scalar1=ss.t[:], scalar2=None, op0=ALU.mult), R=y1.d + ss.d, W=y1.d)
                    for j in range(2):
                        pst = self.ps()
                        self.tr(pst, 128, y1.t[:, j * 128:(j + 1) * 128], R=y1.d)
                        c = 2 * g + j
                        ng = self.VEC.t[:, OFF_NRG + e * 16 + c:OFF_NRG + e * 16 + c + 1]
                        self.op(ACT, lambda: ACT.h.activation(out=YB.t[:, c, cols], in_=pst.t[:, 0:128], func=AF.Copy, scale=ng), R=pst.d + self.VEC.d, W=[YB.d[c]])
                    if samp:
                        self.op(DVE, lambda: DVE.h.tensor_copy(AEX.t[:].rearrange("p (h q) -> p h q", q=64), AA.t[:, ti, hsl].unsqueeze(2).to_broadcast([128, 4, 64])),
                                R=AA.d, W=AEX.d)
                        for hf in range(2):
                            pc = self.ps()
                            self.mmg(pc.d[0], [(pc.t[:, 0:NSEQ], AEX.t[:, hf * 128:(hf + 1) * 128], CON.t[:, C_SIND:C_SIND + NSEQ], True, True)], R=AEX.d + CON.d)
                            self.op(ACT, lambda: ACT.h.activation(out=CDC.t[:, hf, :], in_=pc.t[:, 0:NSEQ], func=AF.Exp), R=pc.d, W=CDC.d)
                        for sh in range(2):
                            b0 = sh * NH
                            self.op(DVE, lambda: DVE.h.tensor_tensor(out=BM.t[:], in0=BTK.t[:, ti, :].unsqueeze(1).to_broadcast([128, NH, 128]),
                                                                     in1=CON.t[:, C_SIND + b0:C_SIND + b0 + NH].unsqueeze(2).to_broadcast([128, NH, 128]), op=ALU.mult),
                                    R=BTK.d + CON.d, W=BM.d)
                            for hf in range(2):
                                h0n = H0N; hno = HNO
                                self.dma(self.PQ, self.SL, h0n.t[:], self.ssdN_d[e, g, hf][:, b0:b0 + NH, :], W=h0n.d)
                                for b in range(NH):
                                    pn = self.ps()
                                    self.mmg(pn.d[0], [(pn.t[:, 0:128], xdd.t[:, hf * 128:(hf + 1) * 128], BM.t[:, b, :], True, True)], R=xdd.d + BM.d)
                                    self.op(DVE, lambda: DVE.h.scalar_tensor_tensor(out=hno.t[:, b, :], in0=h0n.t[:, b, :], scalar=CDC.t[:, hf, b0 + b:b0 + b + 1], in1=pn.t[:, 0:128],
                                                                                    op0=ALU.mult, op1=ALU.add), R=h0n.d + CDC.d + pn.d, W=hno.d)
                                self.dma(self.PQ, self.SO, self.hN_s[e, g, hf][:, b0:b0 + NH, :], hno.t[:], R=hno.d)
                gens = [chunk(ti) for ti in range(NT)]
                next(gens[0], None)
                next(gens[0], None)
                for i_ in range(1, NT):
                    next(gens[i_], None)
                    next(gens[i_ - 1], None)
                    next(gens[i_], None)
                next(gens[NT - 1], None)
                if self.last and Tp:
                    self.dma(self.PQ, self.SO, self.hT_p[e, :, g, :], hst, R=hsd)
            for s in range(4):
                self.linear(("w_out", e, s), 16, YB, self.acc_epi(s * 4, False))
            self.barrier()


def _slabs(W, ncols=512):
    K, N = W.shape
    kc = K // 128
    ns = N // ncols
    return np.ascontiguousarray(W.reshape(kc, 128, ns, ncols).transpose(2, 1, 0, 3))


def _fm(v, nch):
    sh = v.shape[:-1]
    a = v.reshape(*sh, nch, 128)
    return np.moveaxis(a, -1, 0)


def _consts():
    c = np.zeros((128, NCON), np.float32)
    idx = np.arange(128)
    c[:, C_ID:C_ID + 128] = np.eye(128)
    tri = (idx[:, None] <= idx[None, :]).astype(np.float32)
    same = (idx[:, None] // ST == idx[None, :] // ST).astype(np.float32)
    c[:, C_TRI:C_TRI + 128] = tri
    c[:, C_TRIS:C_TRIS + 128] = tri * same
    c[:, C_MNEG:C_MNEG + 128] = np.where(tri > 0, 0.0, -30000.0)
    c[:, C_MNEGS:C_MNEGS + 128] = np.where(tri * same > 0, 0.0, -30000.0)
    c[:, C_ONES:C_ONES + 128] = 1.0
    c[:, C_SAME:C_SAME + 128] = same
    sind = (idx[:, None] // ST == np.arange(NSEQ)[None, :]).astype(np.float32)
    c[:, C_SIND:C_SIND + NSEQ] = sind
    for gi in range(4):
        w = 2 << gi
        c[:, C_ICNT + gi * 16:C_ICNT + gi * 16 + 16] = 1.0 / np.minimum(np.arange(16) + 1, w)
    c[:, C_SMT:C_SMT + 16 * 128] = np.broadcast_to(sind.T.reshape(1, 16 * 128), (128, 16 * 128))
    return c


def prepare_shared(inp):
    sh = {}
    w_in = inp["w_in_even"]
    slabs = []
    for e in range(2):
        W = w_in[e]
        z = W[:, 0:2048]; xbc = W[:, 2048:6144]; gb = W[:, 6176:8224]; gc = W[:, 8224:10272]; hs_ = W[:, 10272:12320]
        cols = []
        for g in range(8):
            cols.append(np.concatenate([z[:, g * 256:(g + 1) * 256], xbc[:, g * 256:(g + 1) * 256]], 1))
        for j in range(4):
            g0, g1 = 2 * j, 2 * j + 1
            cols.append(np.concatenate([xbc[:, 2048 + g0 * 128:2048 + (g0 + 1) * 128], xbc[:, 3072 + g0 * 128:3072 + (g0 + 1) * 128],
                                        xbc[:, 2048 + g1 * 128:2048 + (g1 + 1) * 128], xbc[:, 3072 + g1 * 128:3072 + (g1 + 1) * 128]], 1))
        for j in range(4):
            cols.append(hs_[:, j * 512:(j + 1) * 512]); cols.append(gc[:, j * 512:(j + 1) * 512]); cols.append(gb[:, j * 512:(j + 1) * 512])
        slabs.append(np.stack([_slabs(cm)[0] for cm in cols]))
    sh["w_in_s"] = np.stack(slabs)
    sh["w_dt_s"] = np.stack([_slabs(w_in[e][:, 6144:6176], 32)[0] for e in range(2)])
    wo = inp["w_out_even"]
    sh["w_out_s"] = np.stack([np.concatenate([_slabs(wo[e][0:2048]), _slabs(wo[e][2048:4096])]) for e in range(2)])
    sh["w_pool_s"] = np.stack([np.stack([_slabs(inp["w_pool"][o, g])[0] for g in range(4)]) for o in range(2)])
    for nm, key in (("wq_s", "wq_x"), ("wk_s", "wk_x"), ("wv_s", "wv_x"), ("wo_s", "wo_x")):
        sh[nm] = np.stack([_slabs(inp[key][l]) for l in range(4)])
    sh["w_up_s"] = np.stack([_slabs(inp["w_up"][l]) for l in range(4)])
    sh["w_down_s"] = np.stack([np.concatenate([_slabs(inp["w_down"][l][q * 2048:(q + 1) * 2048]) for q in range(4)]) for l in range(4)])
    vec = np.zeros((128, NVEC), np.float32)
    vec[:, OFF_LNG:OFF_LNG + 192] = _fm(inp["ln_g"], 16).reshape(128, 192)
    vec[:, OFF_LNB:OFF_LNB + 192] = _fm(inp["ln_b"], 16).reshape(128, 192)
    vec[:, OFF_PSC:OFF_PSC + 32] = _fm(inp["pool_scale"], 16).reshape(128, 32)
    vec[:, OFF_SCW:OFF_SCW + 96] = _fm(inp["sc_conv_w"], 16).transpose(0, 1, 3, 2).reshape(128, 96)
    vec[:, OFF_CVW:OFF_CVW + 256] = _fm(inp["ssd_conv_w"], 32).transpose(0, 1, 3, 2).reshape(128, 256)
    vec[:, OFF_CVB:OFF_CVB + 64] = _fm(inp["ssd_conv_b"], 32).reshape(128, 64)
    vec[:, OFF_NRG:OFF_NRG + 32] = _fm(inp["ssd_norm_g"], 16).reshape(128, 32)
    sh["vecs"] = vec
    rows = np.zeros((128, NROW), np.float32)
    rows[:, R_DTB:R_DTB + 64] = inp["ssd_dt_bias"].reshape(1, 64)
    rows[:, R_ALOG:R_ALOG + 64] = inp["ssd_a_log"].reshape(1, 64)
    rows[:, R_DSK:R_DSK + 64] = inp["ssd_d"].reshape(1, 64)
    sh["rows"] = rows
    sh["consts"] = _consts()
    return sh


def prepare_core(inp, seq, sb0, TPp):
    m = {}
    xp = inp["x_prompt"][seq, :TPp]
    xs = inp["x_sample"][sb0:sb0 + NSEQ].reshape(NSEQ * ST, D)
    xa = np.concatenate([xp, xs], 0)
    m["xT"] = np.ascontiguousarray(xa.T.reshape(16, 128, -1).transpose(1, 0, 2))
    m["memT"] = np.ascontiguousarray(inp["mem_prompt"][seq].T.reshape(16, 128, 256).transpose(1, 0, 2))
    ck = inp["cache_mem_k"][:, sb0:sb0 + NSEQ]
    m["kT_s"] = np.ascontiguousarray(ck.reshape(4, NSEQ, 256, 4, 4, 128).transpose(0, 1, 3, 5, 4, 2))
    cv = inp["cache_mem_v"][:, sb0:sb0 + NSEQ]
    m["v_s"] = np.ascontiguousarray(cv.reshape(4, NSEQ, 2, 128, 4, 512).transpose(0, 1, 4, 3, 2, 5))
    s = inp["state_ssd"][:, sb0:sb0 + NSEQ]
    m["ssdT_s"] = np.ascontiguousarray(s.reshape(2, NSEQ, 8, 256, 128).transpose(0, 2, 4, 1, 3))
    m["ssdN_s"] = np.ascontiguousarray(s.reshape(2, NSEQ, 8, 2, 128, 128).transpose(0, 2, 3, 4, 1, 5))
    cs = inp["state_ssd_conv"][:, sb0:sb0 + NSEQ]
    cT = cs.reshape(2, NSEQ, 3, 32, 128).transpose(0, 4, 3, 1, 2)
    m["convT_s"] = np.ascontiguousarray(np.stack([cT[:, :, _cids(g)] for g in range(8)], 1))
    ss = inp["state_short_conv"][:, sb0:sb0 + NSEQ]
    m["scT_s"] = np.ascontiguousarray(ss.reshape(2, NSEQ, 2, 16, 128).transpose(0, 4, 3, 1, 2))
    sp = inp["state_pool"][:, sb0:sb0 + NSEQ]
    m["poolT_s"] = np.ascontiguousarray(sp.reshape(2, NSEQ, 15, 16, 128).transpose(0, 4, 3, 1, 2))
    return m


def _cids(g):
    return [2 * g, 2 * g + 1, 16 + g, 24 + g]


def _cb_s(a):
    full = np.zeros((2, 128, 32, NSEQ, 3), np.float32)
    for g in range(8):
        full[:, :, _cids(g)] = a[:, g]
    return full.transpose(0, 3, 4, 2, 1).reshape(2, NSEQ, 3, 4096)


def unT(a):
    return a.transpose(2, 1, 0).reshape(a.shape[2], -1)


BLOCKS = [(4, False), (3, False), (3, False), (3, False), (3, True)]
_prog = {}


def kernel(**inp):
    inp = {k: np.asarray(v) for k, v in inp.items()}
    key = "full"
    if key not in _prog:
        b = Builder(BLOCKS, 4)
        b.cvtmp = None
        _prog[key] = (b, b.build())
    b, nc = _prog[key]
    sh = prepare_shared(inp)
    in_maps = []
    for core in range(8):
        m = dict(sh)
        m.update(prepare_core(inp, core % 4, core * NSEQ, 2048))
        in_maps.append(m)
    res = run_bass_kernel_spmd(nc, in_maps, core_ids=list(range(8))).results
    return assemble(res, 2048)


def assemble(res, TPp):
    nb = len(res)
    npr = min(4, nb)
    y_prompt = np.stack([unT(res[c]["yT"][:, :, :TPp]) for c in range(npr)])
    if res[0]["yT"].shape[2] > TPp:
        y_sample = np.concatenate([unT(res[c]["yT"][:, :, TPp:]).reshape(NSEQ, ST, D) for c in range(nb)])
    else:
        y_sample = np.zeros((nb * NSEQ, ST, D), np.float32)
    h_p = np.stack([res[c]["hT_p"].reshape(2, 128, 32, 64).transpose(0, 2, 3, 1) for c in range(npr)], 1)
    h_s = np.concatenate([res[c]["hN_s"].transpose(0, 4, 1, 2, 3, 5).reshape(2, NSEQ, 32, 64, 128) for c in range(nb)], 1)
    cb_p = np.stack([res[c]["cbT_p"].transpose(0, 3, 2, 1).reshape(2, 3, 4096) for c in range(npr)], 1)
    cb_s = np.concatenate([_cb_s(res[c]["cbT_s"]) for c in range(nb)], 1)
    sb_p = np.stack([res[c]["sbT_p"].transpose(0, 3, 2, 1).reshape(2, 2, 2048) for c in range(npr)], 1)
    sb_s = np.concatenate([res[c]["sbT_s"].transpose(0, 3, 4, 2, 1).reshape(2, NSEQ, 2, 2048) for c in range(nb)], 1)
    pb_p = np.stack([res[c]["pbT_p"].transpose(0, 3, 2, 1).reshape(2, 15, 2048) for c in range(npr)], 1)
    pb_s = np.concatenate([res[c]["pbT_s"].transpose(0, 3, 4, 2, 1).reshape(2, NSEQ, 15, 2048) for c in range(nb)], 1)
    mk_p = np.stack([res[c]["mk_p"].reshape(4, 256, 4, 512) for c in range(npr)], 1)
    mv_p = np.stack([res[c]["mv_p"].reshape(4, 256, 4, 512) for c in range(npr)], 1)
    outs = (y_prompt, y_sample, h_p, h_s, cb_p, cb_s, sb_p, sb_s, pb_p, pb_s, mk_p, mv_p)
    return tuple(np.ascontiguousarray(o, dtype=np.float32) for o in outs)
```
